# Optimizing a Trainium2 kernel written in Bass

```python
import jax, jax.numpy as jnp
from jax import lax
import numpy as np

D_MODEL = 1024
BATCH = 8
SEQ = 2048
DEPTH = 2

GRID_W = 64
CTX_LEN = 256

HEAD_DIM = 64
N_Q_HEADS = 8
N_KV_HEADS = 2
GROUP = N_Q_HEADS // N_KV_HEADS
ATTN_W = N_Q_HEADS * HEAD_DIM
KV_W = N_KV_HEADS * HEAD_DIM
CONF_W = D_MODEL // 4
SC_W = D_MODEL // 4
MIX_W = ATTN_W + CONF_W + SC_W
IN_W = ATTN_W + 2 * KV_W + 2 * CONF_W + 3 * SC_W
SPLITS = tuple(int(s) for s in np.cumsum([ATTN_W, KV_W, KV_W, CONF_W, CONF_W, SC_W, SC_W]))

WINDOW = 128
BLOCK = 128
CONF_K = 31
SC_K = 3
FFN_K = 3
D_FF = 2816
ROPE_THETA = 10000.0
ROPE_FREQS = HEAD_DIM // 4
EPS = 1e-6
NEG = -1e30

kernel_name = "hybrid_parallel_group_dit_block"


def rms_norm(x, g):
    xf = x.astype(jnp.float32)
    y = xf * lax.rsqrt(jnp.mean(xf * xf, axis=-1, keepdims=True) + EPS)
    return (y * g.astype(jnp.float32)).astype(x.dtype)


def layer_norm(x, g, b):
    xf = x.astype(jnp.float32)
    mu = jnp.mean(xf, axis=-1, keepdims=True)
    var = jnp.mean(jnp.square(xf - mu), axis=-1, keepdims=True)
    y = (xf - mu) * lax.rsqrt(var + EPS)
    return (y * g.astype(jnp.float32) + b.astype(jnp.float32)).astype(x.dtype)


def dwconv(x, w):
    k, ch = w.shape
    pad = k // 2
    return lax.conv_general_dilated(
        x, w[:, None, :].astype(x.dtype), window_strides=(1,), padding=[(pad, pad)],
        dimension_numbers=("NWC", "WIO", "NWC"), feature_group_count=ch)


def axial_rope_tables(n_tokens):
    rows = n_tokens // GRID_W
    r, col = jnp.meshgrid(jnp.arange(rows), jnp.arange(GRID_W), indexing="ij")
    pos = jnp.stack([r.reshape(-1), col.reshape(-1)], axis=-1).astype(jnp.float32)
    inv = ROPE_THETA ** (-jnp.arange(ROPE_FREQS, dtype=jnp.float32) / ROPE_FREQS)
    ang = pos[:, :, None] * inv
    return jnp.cos(ang), jnp.sin(ang)


def apply_rope(x, cos, sin):
    b, l, h, _ = x.shape
    xr = x.astype(jnp.float32).reshape(b, l, h, 2, 2, ROPE_FREQS)
    a, bb = xr[..., 0, :], xr[..., 1, :]
    cs, sn = cos[None, :, None], sin[None, :, None]
    out = jnp.stack([a * cs - bb * sn, bb * cs + a * sn], axis=-2)
    return out.reshape(b, l, h, HEAD_DIM).astype(x.dtype)


def heads(t, n):
    return t.reshape(t.shape[0], t.shape[1], n, HEAD_DIM)


def sink_logits(sink, shape_prefix, n_q):
    s = sink.astype(jnp.float32).reshape(N_KV_HEADS, GROUP)[:, :, None, None]
    return jnp.broadcast_to(s, shape_prefix + (N_KV_HEADS, GROUP, n_q, 1))


def latent_window_attention(q, k, v, kc, vc, sink):
    b, l = q.shape[:2]
    nblk = l // BLOCK
    scale = HEAD_DIM ** -0.5
    qb = q.reshape(b, nblk, BLOCK, N_KV_HEADS, GROUP, HEAD_DIM)
    pad = ((0, 0), (BLOCK, BLOCK), (0, 0), (0, 0))
    kp = jnp.pad(k, pad).reshape(b, nblk + 2, BLOCK, N_KV_HEADS, HEAD_DIM)
    vp = jnp.pad(v, pad).reshape(b, nblk + 2, BLOCK, N_KV_HEADS, HEAD_DIM)
    kband = jnp.concatenate([kp[:, :-2], kp[:, 1:-1], kp[:, 2:]], axis=2)
    vband = jnp.concatenate([vp[:, :-2], vp[:, 1:-1], vp[:, 2:]], axis=2)
    s_loc = jnp.einsum("bnqhgd,bnkhd->bnhgqk", qb, kband,
                       preferred_element_type=jnp.float32) * scale
    blk = jnp.arange(nblk)[:, None, None] * BLOCK
    qpos = blk + jnp.arange(BLOCK)[None, :, None]
    kpos = blk - BLOCK + jnp.arange(3 * BLOCK)[None, None, :]
    mask = (jnp.abs(kpos - qpos) <= WINDOW) & (kpos >= 0) & (kpos < l)
    s_loc = jnp.where(mask[None, :, None, None], s_loc, NEG)
    s_ctx = jnp.einsum("bnqhgd,bchd->bnhgqc", qb, kc,
                       preferred_element_type=jnp.float32) * scale
    s_snk = sink_logits(sink, (b, nblk), BLOCK)
    p = jax.nn.softmax(jnp.concatenate([s_loc, s_ctx, s_snk], axis=-1), axis=-1)
    n_loc = 3 * BLOCK
    n_ctx = kc.shape[1]
    p_loc = p[..., :n_loc].astype(v.dtype)
    p_ctx = p[..., n_loc:n_loc + n_ctx].astype(v.dtype)
    o = (jnp.einsum("bnhgqk,bnkhd->bnqhgd", p_loc, vband)
         + jnp.einsum("bnhgqc,bchd->bnqhgd", p_ctx, vc))
    return o.reshape(b, l, ATTN_W)


def context_attention(qc, kc, vc, sink):
    b, lc = qc.shape[:2]
    scale = HEAD_DIM ** -0.5
    qg = qc.reshape(b, lc, N_KV_HEADS, GROUP, HEAD_DIM)
    s = jnp.einsum("bqhgd,bchd->bhgqc", qg, kc, preferred_element_type=jnp.float32) * scale
    s_snk = sink_logits(sink, (b,), lc)
    p = jax.nn.softmax(jnp.concatenate([s, s_snk], axis=-1), axis=-1)
    o = jnp.einsum("bhgqc,bchd->bqhgd", p[..., :lc].astype(vc.dtype), vc)
    return o.reshape(b, lc, ATTN_W)


def conformer_conv(val, gate, w_dw, b_dw, ln_g, ln_b):
    u = val * jax.nn.sigmoid(gate)
    u = dwconv(u, w_dw) + b_dw
    return jax.nn.silu(layer_norm(u, ln_g, ln_b))


def short_conv(bg, cg, u, w_dw):
    return bg * dwconv(cg * u, w_dw)


def merge_groups(a, cf, s, g_group, w_out):
    ga, gc, gs = jnp.split(g_group, [ATTN_W, ATTN_W + CONF_W])
    z = jnp.concatenate([rms_norm(a, ga), rms_norm(cf, gc), rms_norm(s, gs)], axis=-1)
    return z @ w_out


def token_mixer(h, hc, w_in, sink, w_conf_dw, b_conf_dw, conf_ln_g, conf_ln_b,
                w_sc_dw, g_group, w_out, cos, sin, update_ctx):
    q, k, v, cv, cgt, sb, scg, su = jnp.split(h @ w_in, SPLITS, axis=-1)
    if update_ctx:
        qc, kc, vc, cvc, cgc, sbc, scc, suc = jnp.split(hc @ w_in, SPLITS, axis=-1)
    else:
        kc, vc = jnp.split(hc @ w_in[:, ATTN_W:ATTN_W + 2 * KV_W], 2, axis=-1)
    kc, vc = heads(kc, N_KV_HEADS), heads(vc, N_KV_HEADS)
    q = apply_rope(heads(q, N_Q_HEADS), cos, sin)
    k = apply_rope(heads(k, N_KV_HEADS), cos, sin)
    a = latent_window_attention(q, k, heads(v, N_KV_HEADS), kc, vc, sink)
    cf = conformer_conv(cv, cgt, w_conf_dw, b_conf_dw, conf_ln_g, conf_ln_b)
    s = short_conv(sb, scg, su, w_sc_dw)
    y = merge_groups(a, cf, s, g_group, w_out)
    if not update_ctx:
        return y, None
    ac = context_attention(heads(qc, N_Q_HEADS), kc, vc, sink)
    cfc = conformer_conv(cvc, cgc, w_conf_dw, b_conf_dw, conf_ln_g, conf_ln_b)
    sc_ = short_conv(sbc, scc, suc, w_sc_dw)
    yc = merge_groups(ac, cfc, sc_, g_group, w_out)
    return y, yc


def conv_ffn(h, w_up, w_dw, w_down):
    u = dwconv(h @ w_up, w_dw)
    gate, val = jnp.split(u, 2, axis=-1)
    return (jax.nn.silu(gate) * val) @ w_down


def setup_inputs(seed: int = 0) -> dict:
    key = jax.random.key(seed)
    ks = jax.random.split(key, 24)
    f32 = jnp.float32
    nrm = lambda k, shape, s: jax.random.normal(k, shape, f32) * s
    gain = lambda k, shape: 1.0 + 0.02 * jax.random.normal(k, shape, f32)
    L = DEPTH
    return {
        "x": nrm(ks[0], (BATCH, SEQ, D_MODEL), 1.0),
        "c": nrm(ks[1], (BATCH, D_MODEL), 1.0),
        "ctx": nrm(ks[2], (BATCH, CTX_LEN, D_MODEL), 1.0),
        "c_ctx": nrm(ks[3], (D_MODEL,), 1.0),
        "w_ada": nrm(ks[4], (L, D_MODEL, 6 * D_MODEL), 0.5 * D_MODEL ** -0.5),
        "b_ada": nrm(ks[5], (L, 6 * D_MODEL), 0.02),
        "g_pre_mix": gain(ks[6], (L, D_MODEL)),
        "g_post_mix": gain(ks[7], (L, D_MODEL)),
        "g_pre_ffn": gain(ks[8], (L, D_MODEL)),
        "g_post_ffn": gain(ks[9], (L, D_MODEL)),
        "w_in": nrm(ks[10], (L, D_MODEL, IN_W), D_MODEL ** -0.5),
        "sink": nrm(ks[11], (L, N_Q_HEADS), 0.5),
        "w_conf_dw": nrm(ks[12], (L, CONF_K, CONF_W), CONF_K ** -0.5),
        "b_conf_dw": nrm(ks[13], (L, CONF_W), 0.02),
        "conf_ln_g": gain(ks[14], (L, CONF_W)),
        "conf_ln_b": nrm(ks[15], (L, CONF_W), 0.02),
        "w_sc_dw": nrm(ks[16], (L, SC_K, SC_W), SC_K ** -0.5),
        "g_group": gain(ks[17], (L, MIX_W)),
        "w_out": nrm(ks[18], (L, MIX_W, D_MODEL), MIX_W ** -0.5),
        "w_up": nrm(ks[19], (L, D_MODEL, 2 * D_FF), D_MODEL ** -0.5),
        "w_ffn_dw": nrm(ks[20], (L, FFN_K, 2 * D_FF), FFN_K ** -0.5),
        "w_down": nrm(ks[21], (L, D_FF, D_MODEL), D_FF ** -0.5),
    }


def reference(x, c, ctx, c_ctx, w_ada, b_ada, g_pre_mix, g_post_mix, g_pre_ffn, g_post_ffn,
              w_in, sink, w_conf_dw, b_conf_dw, conf_ln_g, conf_ln_b, w_sc_dw, g_group,
              w_out, w_up, w_ffn_dw, w_down):
    n_tok = x.shape[1]
    cos, sin = axial_rope_tables(n_tok)
    for l in range(DEPTH):
        update_ctx = l < DEPTH - 1
        mod = jax.nn.silu(c) @ w_ada[l] + b_ada[l]
        sh1, sc1, gt1, sh2, sc2, gt2 = [m[:, None, :] for m in jnp.split(mod, 6, axis=-1)]
        modc = jax.nn.silu(c_ctx) @ w_ada[l] + b_ada[l]
        sh1c, sc1c, gt1c, sh2c, sc2c, gt2c = jnp.split(modc, 6, axis=-1)
        h = rms_norm(x, g_pre_mix[l]) * (1.0 + sc1) + sh1
        hc = rms_norm(ctx, g_pre_mix[l]) * (1.0 + sc1c) + sh1c
        y, yc = token_mixer(h, hc, w_in[l], sink[l], w_conf_dw[l], b_conf_dw[l], conf_ln_g[l],
                            conf_ln_b[l], w_sc_dw[l], g_group[l], w_out[l], cos, sin, update_ctx)
        x = x + gt1 * rms_norm(y, g_post_mix[l])
        h = rms_norm(x, g_pre_ffn[l]) * (1.0 + sc2) + sh2
        x = x + gt2 * rms_norm(conv_ffn(h, w_up[l], w_ffn_dw[l], w_down[l]), g_post_ffn[l])
        if update_ctx:
            ctx = ctx + gt1c * rms_norm(yc, g_post_mix[l])
            hc = rms_norm(ctx, g_pre_ffn[l]) * (1.0 + sc2c) + sh2c
            ctx = ctx + gt2c * rms_norm(conv_ffn(hc, w_up[l], w_ffn_dw[l], w_down[l]), g_post_ffn[l])
    return x
```

```python
import contextlib
import os
import numpy as np
import ml_dtypes
import concourse.bass as bass
import concourse.mybir as mybir
from concourse.bass_utils import run_bass_kernel_spmd

F32 = mybir.dt.float32
BF16 = mybir.dt.bfloat16
AF = mybir.ActivationFunctionType
ALU = mybir.AluOpType

D = 1024
SEQ = 2048
CTX = 256
NT = 18
NTOK = NT * 128
DEPTH = 2
DFF = 2816
NFF = 22
EPS = 1e-6
NCW = 16
NV = 342


class Op:
    __slots__ = ("eng", "fn", "deps", "sem", "ticket", "signal", "ninc", "name")


class Sched:
    ENGS = ("pe", "act", "dve", "pool", "sp")

    def __init__(self, nc):
        self.nc = nc
        self.ops = {e: [] for e in self.ENGS}
        self.lastw = {}
        self.readers = {}
        self.dma_count = {}
        self.all_ops = []
        self.epoch = 0

    def add(self, eng, fn, reads=(), writes=(), dma_key=None, ndma=1, name=""):
        op = Op()
        op.eng, op.fn, op.name = eng, fn, name
        op.signal = False
        deps = set()
        excl = [r for r in reads if isinstance(r, str) and r.startswith("ps:")]
        reads = [r for r in reads if r not in excl]
        writes = list(writes) + excl
        for r in reads:
            lw = self.lastw.get(r)
            if lw is not None:
                deps.add(lw)
        for w in writes:
            lw = self.lastw.get(w)
            if lw is not None:
                deps.add(lw)
            lastrd = {}
            for rd in self.readers.get(w, ()):
                if rd.sem[0] == "dma":
                    deps.add(rd)
                else:
                    lastrd[rd.eng] = rd
            deps.update(lastrd.values())
        for r in reads:
            self.readers.setdefault(r, []).append(op)
        for w in writes:
            self.lastw[w] = op
            self.readers[w] = []
        deps.discard(op)
        if eng == "pe":
            deps = set(d for d in deps if d.eng != "pe")
        op.deps = deps
        if dma_key is not None:
            op.sem = ("dma", dma_key)
            self.dma_count[dma_key] = self.dma_count.get(dma_key, 0) + 16 * ndma
            op.ticket = self.dma_count[dma_key]
            op.ninc = ndma
            op.signal = True
        else:
            op.sem = ("eng", eng, self.epoch)
            op.ticket = None
            op.ninc = 0
        for d in deps:
            d.signal = True
        self.ops[eng].append(op)
        self.all_ops.append(op)
        return op

    def emit(self, final_waits=()):
        nc = self.nc
        cnts = {}
        for e in self.ENGS:
            for op in self.ops[e]:
                if op.sem[0] == "eng" and op.signal:
                    cnts[op.sem] = cnts.get(op.sem, 0) + 1
                    op.ticket = cnts[op.sem]
        with contextlib.ExitStack() as st:
            sems = {}
            for i, k in enumerate(cnts):
                sems[k] = st.enter_context(nc.semaphore("s%d" % i))
            for i, k in enumerate(self.dma_count):
                sems[("dma", k)] = st.enter_context(nc.semaphore("d%d" % i))
            block = st.enter_context(nc.Block())

            def run(engname, eng):
                waited = {}
                for op in self.ops[engname]:
                    need = {}
                    for d in op.deps:
                        if need.get(d.sem, 0) < d.ticket:
                            need[d.sem] = d.ticket
                    for s, v in need.items():
                        if waited.get(s, 0) < v:
                            eng.wait_ge(sems[s], v)
                            waited[s] = v
                    ins = op.fn(eng)
                    if op.sem[0] == "dma":
                        if not isinstance(ins, (list, tuple)):
                            ins = [ins]
                        assert len(ins) == op.ninc, (op.name, len(ins), op.ninc)
                        for i in ins:
                            i.then_inc(sems[op.sem], 16)
                    elif op.signal:
                        ins.then_inc(sems[op.sem], 1)
                if engname == "sp":
                    for op in final_waits:
                        eng.wait_ge(sems[op.sem], op.ticket)

            block.tensor(lambda eng: run("pe", eng))
            block.scalar(lambda eng: run("act", eng))
            block.vector(lambda eng: run("dve", eng))
            block.gpsimd(lambda eng: run("pool", eng))
            block.sync(lambda eng: run("sp", eng))


def split_cols(lo, hi, mx=512):
    out = []
    while lo < hi:
        n = min(mx, hi - lo)
        out.append((lo, n))
        lo += n
    return out


def build_program(stop_after=None):
    nc = bass.Bass("TRN2", target_bir_lowering=False)
    dram = lambda name, shape, dt=F32, kind="ExternalInput": nc.dram_tensor(name, shape, dt, kind=kind).ap()
    d_x = dram("x", [SEQ, D])
    d_ctx = dram("ctx", [CTX, D])
    d_cc = dram("cc", [128, 16])
    d_rope = dram("rope", [128, 2 * SEQ])
    d_mask = dram("mask", [128, 384])
    d_perm = dram("perm", [128, 128])
    d_wada = [dram("wada%d" % l, [D, 6 * D]) for l in range(DEPTH)]
    d_vec = [dram("vec%d" % l, [128, NV]) for l in range(DEPTH)]
    d_win = [dram("win%d" % l, [D, NCW * 128]) for l in range(DEPTH)]
    d_sink = [dram("sink%d" % l, [128, 8]) for l in range(DEPTH)]
    d_wout = [dram("wout%d" % l, [D, D]) for l in range(DEPTH)]
    d_wup = [dram("wup%d" % l, [D, 2 * DFF]) for l in range(DEPTH)]
    d_wdn = [dram("wdn%d" % l, [DFF, D]) for l in range(DEPTH)]
    d_out = dram("out", [SEQ, D], kind="ExternalOutput")
    d_xs = dram("xs_scratch", [NTOK, D], kind="Internal")

    st = contextlib.ExitStack()
    with st:
        sb = lambda name, shape, dt: st.enter_context(nc.sbuf_tensor(name, shape, dt))
        S = Sched(nc)

        BIG = sb("BIG", [128, 63296], BF16)
        HT = sb("HT", [128, 8, NTOK], BF16)
        GB = sb("GB", [128, 2, D], F32)
        XT = [sb("XT%d" % i, [128, D], F32) for i in range(3)]
        XN = sb("XN", [128, 4, D], BF16)
        JUNK = sb("JUNK", [128, D], BF16)
        TMP = [sb("TMP%d" % i, [128, 512], F32) for i in range(4)]
        TMPB = [sb("TMPB%d" % i, [128, 512], BF16) for i in range(3)]
        IDENT = sb("IDENT", [128, 128], BF16)
        IDENTF = sb("IDENTF", [128, 128], F32)
        ONESF = sb("ONESF", [128, 128], F32)
        ONESB = sb("ONESB", [128, 128], BF16)
        NHALF = sb("NHALF", [128, 8], F32)
        MASK = sb("MASK", [128, 384], BF16)
        PERM = sb("PERM", [128, 128], BF16)
        VECS = [sb("VEC%d" % i, [128, NV], F32) for i in range(2)]

        class L:
            VEC = None
            vecr = None
        CCF = sb("CCF", [128, 16], F32)
        CCB = sb("CCB", [128, 8, 2], BF16)
        MODC = sb("MODC", [128, 96], F32)
        AB = sb("AB", [128, 64], F32)
        GCS = [sb("GC%d" % i, [128, 32], F32) for i in range(2)]
        DIAGF = sb("DIAGF", [128, 128], F32)
        STAT = sb("STAT", [128, 64], F32)
        ESC = sb("ESC", [128, 8], F32)
        EPSC = sb("EPSC", [128, 1], F32)
        NLN = sb("NLN", [128, 4], F32)

        psf = [st.enter_context(nc.psum_tensor("psf%d" % i, [128, 512], F32)) for i in range(8)]
        psb = [psf[6 + i][:, :].bitcast(BF16) for i in range(2)]
        PSF = ["ps:f%d" % i for i in range(8)]
        PSB = [PSF[6], PSF[7]]

        def carve(off, shape):
            n = int(np.prod(shape))
            ap = BIG[:, off:off + n]
            if len(shape) == 2:
                ap = ap.rearrange("p (a b) -> p a b", b=shape[1])
            return ap, off + n
        o = 0
        QT, o = carve(o, [4, NTOK])
        KT, o = carve(o, [1, NTOK])
        KT = BIG[:, 9216:9216 + NTOK]
        VT, o = carve(o, [NT, 256])
        UCW = 2364
        UC, o = carve(o, [2, UCW])
        PW = 2308
        PP, o = carve(o, [2, PW])
        SBS, o = carve(o, [2, NTOK])
        DIAG, o = carve(o, [62, 128])
        WBUF_OFF = o
        WIN = []
        for i in range(2):
            w, o = carve(o, [8, 384]); WIN.append(w)
        WOUT, _ = carve(WBUF_OFF, [8, D])
        o = WBUF_OFF + 8 * D
        WADA = [WIN[0], WIN[1]]
        ZT = []
        for i in range(2):
            z, o = carve(o, [8, 512]); ZT.append(z)
        PT = []
        for i in range(4):
            p_, o = carve(o, [1, 512]); PT.append(BIG[:, o - 512:o])
        ROPE = BIG[:, o:o + 2 * SEQ]
        o += 2 * SEQ
        ATMP = []
        for i in range(2):
            ATMP.append(BIG[:, o:o + 1024].bitcast(F32))
            o += 1024
        assert o <= 63296, o
        o = 0
        WUP = []
        for i in range(3):
            w, o = carve(o, [8, 256]); WUP.append(w)
        UROW = []
        URW = 1160
        for i in range(4):
            u, o = carve(o, [1, URW]); UROW.append(BIG[:, o - URW:o])
        CROW = []
        for i in range(4):
            u, o = carve(o, [1, URW]); CROW.append(BIG[:, o - URW:o])
        WDN, o = carve(o, [NFF, D])
        assert o <= 38016, o
        GT, o = carve(o, [NFF, 1152])
        assert o <= 63296, o
        BIGR = "BIG"

        def MM(out, lhsT, rhs, start, stop, rd, wr):
            return S.add("pe", lambda e, a=(out, lhsT, rhs, start, stop): e.matmul(
                a[0], a[1], a[2], start=a[3], stop=a[4], skip_group_check=True), rd, wr)

        def TR(out, in_, rd, wr):
            return S.add("pe", lambda e, a=(out, in_): e.transpose(a[0], a[1], IDENT[:]), list(rd) + ["ident"], wr)

        def ACT(out, in_, func, rd, wr, scale=None, bias=None, accum=None):
            kw = {}
            if scale is not None:
                kw["scale"] = scale
            if bias is not None:
                kw["bias"] = bias
            if accum is not None:
                kw["accum_out"] = accum
            return S.add("act", lambda e, a=(out, in_, func, kw): e.activation(out=a[0], in_=a[1], func=a[2], **a[3]), rd, wr)

        def TT(eng, out, in0, in1, op, rd, wr):
            return S.add(eng, lambda e, a=(out, in0, in1, op): e.tensor_tensor(out=a[0], in0=a[1], in1=a[2], op=a[3]), rd, wr)

        def TS(out, in0, s1, s2, op0, op1, rd, wr):
            if op1 is None:
                return S.add("dve", lambda e, a=(out, in0, s1, op0): e.tensor_scalar(
                    out=a[0], in0=a[1], scalar1=a[2], scalar2=None, op0=a[3]), rd, wr)
            return S.add("dve", lambda e, a=(out, in0, s1, s2, op0, op1): e.tensor_scalar(
                out=a[0], in0=a[1], scalar1=a[2], scalar2=a[3], op0=a[4], op1=a[5]), rd, wr)

        def STT(out, in0, scalar, in1, op0, op1, rd, wr):
            return S.add("dve", lambda e, a=(out, in0, scalar, in1, op0, op1): e.scalar_tensor_tensor(
                out=a[0], in0=a[1], scalar=a[2], in1=a[3], op0=a[4], op1=a[5]), rd, wr)

        def CP(eng, out, in_, rd, wr):
            return S.add(eng, lambda e, a=(out, in_): e.tensor_copy(a[0], a[1]), rd, wr)

        def RECIP(out, in_, rd, wr):
            return S.add("dve", lambda e, a=(out, in_): e.reciprocal(a[0], a[1]), rd, wr)

        def MEMSET(eng, ap, val, wr):
            return S.add(eng, lambda e, a=(ap, val): e.memset(a[0], a[1]), [], wr)

        def DMA(q, out, in_, rd, wr, key):
            return S.add(q, lambda e, a=(out, in_): e.dma_start(out=a[0], in_=a[1]), rd, wr, dma_key=key)

        def POW(out, in_, n, rd, wr):
            return TT("pool", out, in_, NHALF[:, 0:n], ALU.pow, list(rd) + ["const"], wr)

        rr = {"f": 0, "b": 0, "x": 0, "t": 0, "tb": 0, "pt": 0}

        def nxt(kind, n):
            v = rr[kind] % n
            rr[kind] = (v + 1) % n
            return v

        MEMSET("pool", IDENT[:], 0.0, ["ident"])
        S.add("pool", lambda e: e.affine_select(out=IDENT[:], in_=IDENT[:], pattern=[[-1, 128]], compare_op=ALU.not_equal,
                                                fill=1.0, base=0, channel_multiplier=1), ["ident"], ["ident"])
        MEMSET("pool", IDENTF[:], 0.0, ["identf"])
        S.add("pool", lambda e: e.affine_select(out=IDENTF[:], in_=IDENTF[:], pattern=[[-1, 128]], compare_op=ALU.not_equal,
                                                fill=1.0, base=0, channel_multiplier=1), ["identf"], ["identf"])
        MEMSET("pool", ONESF[:], 1.0, ["const"])
        MEMSET("pool", ONESB[:], 1.0, ["const"])
        MEMSET("pool", NHALF[:], -0.5, ["const"])
        MEMSET("pool", EPSC[:], EPS, ["const"])
        DMA("pool", MASK[:], d_mask, [], ["const"], "mask")
        DMA("pool", PERM[:], d_perm, [], ["perm"], "perm")
        DMA("sp", CCF[:], d_cc, [], ["ccf"], "ccf")
        ACT(CCB[:].rearrange("p k r -> p (k r)"), CCF[:], AF.Silu, ["ccf"], ["ccb"])

        out_dmas = []

        def xsrc(l, i):
            if l == 0:
                if i < 16:
                    return d_x[i * 128:(i + 1) * 128, :], None
                return d_ctx[(i - 16) * 128:(i - 15) * 128, :], None
            return d_xs[i * 128:(i + 1) * 128, :], ("dxs", i)

        def mod_gen(l, bufs, bufres, bank, bw):
            VEC = VECS[l % 2]
            vecr = ("vec", l % 2)
            DMA("sp", VEC[:], d_vec[l], [], [vecr], ("vec", l % 2))
            DMA("sp", ESC[:], d_sink[l], [], ["esc"], "esc")
            ACT(ESC[:], ESC[:], AF.Exp, ["esc"], ["esc"])
            wv = d_wada[l].rearrange("(k p) n -> p k n", p=128)
            psM = psf[bank]
            mv = MODC[:].rearrange("p (j r) -> p j r", r=2)
            abv = AB[:].rearrange("p (w r k) -> p w r k", w=4, r=2)
            gcv = GCS[l % 2][:].rearrange("p (w r k) -> p w r k", w=2, r=2)
            nblk = 6 * D // bw
            cpb = bw // 128

            def load_blk(blk):
                b_ = blk % 2
                DMA("pool", bufs[b_], wv[:, :, blk * bw:(blk + 1) * bw], list(bufres[b_]), list(bufres[b_]), ("wada", b_))
            load_blk(0)
            for blk in range(nblk):
                buf = blk % 2
                if blk + 1 < nblk:
                    load_blk(blk + 1)
                for jj in range(cpb):
                    j = blk * cpb + jj
                    for k in range(8):
                        MM(psM[:, 2 * j:2 * j + 2], bufs[buf][:, k, jj * 128:(jj + 1) * 128], CCB[:, k, :],
                           k == 0, k == 7, list(bufres[buf]) + ["ccb"], [PSF[bank]])
                yield
                if blk == 16 // cpb - 1:
                    TT("dve", MODC[:, 0:32], psM[:, 0:32], VEC[:, 246:278], ALU.add, [PSF[bank], vecr], ["modc0"])
                    for r in range(2):
                        STT(abv[:, 0, r, :], mv[:, 8:16, r], 1.0, VEC[:, 0:8], ALU.add, ALU.mult, ["modc0", vecr], ["ab0"])
                        CP("dve", abv[:, 1, r, :], mv[:, 0:8, r], ["modc0"], ["ab0"])
                    yield
                if blk == 24 // cpb - 1:
                    TT("dve", MODC[:, 32:48], psM[:, 32:48], VEC[:, 278:294], ALU.add, [PSF[bank], vecr], ["modc1"])
                    for r in range(2):
                        TT("dve", gcv[:, 0, r, :], mv[:, 16:24, r], VEC[:, 8:16], ALU.mult, ["modc1", vecr], [("gc", l % 2, 0)])
                    yield
            TT("dve", MODC[:, 48:96], psM[:, 48:96], VEC[:, 294:342], ALU.add, [PSF[bank], vecr], ["modc"])
            for r in range(2):
                STT(abv[:, 2, r, :], mv[:, 32:40, r], 1.0, VEC[:, 16:24], ALU.add, ALU.mult, ["modc", vecr], ["ab1"])
                CP("dve", abv[:, 3, r, :], mv[:, 24:32, r], ["modc"], ["ab1"])
                TT("dve", gcv[:, 1, r, :], mv[:, 40:48, r], VEC[:, 24:32], ALU.mult, ["modc", vecr], [("gc", l % 2, 1)])
            yield

        def grows(w, l):
            gcv = GCS[l % 2][:].rearrange("p (w r k) -> p w r k", w=2, r=2)
            for r in range(2):
                for k in range(8):
                    TS(DIAGF[:], IDENTF[:], gcv[:, w, r, k:k + 1], None, ALU.mult, None, ["identf", ("gc", l % 2, w)], ["diagf"])
                    bank = 1 + (k // 4)
                    MM(psf[bank][:, (k % 4) * 128:(k % 4 + 1) * 128], ONESF[:], DIAGF[:], True, True,
                       ["const", "diagf"], [PSF[bank]])
                    if k % 4 == 3:
                        ACT(GB[:, r, (k // 4) * 512:(k // 4 + 1) * 512], psf[bank][:, :], AF.Copy,
                            [PSF[bank]], [("gb", r)])

        def norm_tile(t, xap, xres):
            sc = 4 * t
            ACT(JUNK[:], xap, AF.Square, [xres], ["junk", ("stat", sc)], accum=STAT[:, sc:sc + 1])
            TS(STAT[:, sc + 1:sc + 2], STAT[:, sc:sc + 1], 1.0 / D, EPS, ALU.mult, ALU.add, [("stat", sc)], [("stat", sc + 1)])
            POW(STAT[:, sc + 2:sc + 3], STAT[:, sc + 1:sc + 2], 1, [("stat", sc + 1)], [("stat", sc + 2)])
            TS(XN[:, t, :], xap, STAT[:, sc + 2:sc + 3], None, ALU.mult, None, [xres, ("stat", sc + 2)], [("xn", t)])

        def transpose_group(tiles, which, dst_res):
            r = 0 if tiles[0] < 16 else 1
            abv = AB[:].rearrange("p (w r k) -> p w r k", w=4, r=2)
            n = len(tiles)
            c0 = tiles[0] * 128
            for k in range(8):
                b = nxt("b", 2)
                for t in range(n):
                    TR(psb[b][:, t * 128:(t + 1) * 128], XN[:, t, k * 128:(k + 1) * 128], [("xn", t)], [PSB[b]])
                ACT(HT[:, k, c0:c0 + n * 128], psb[b][:, 0:n * 128], AF.Identity, [PSB[b], "ab%d" % which], [dst_res],
                    scale=abv[:, 2 * which, r, k:k + 1], bias=abv[:, 2 * which + 1, r, k:k + 1])

        GROUPS = [[0, 1, 2, 3], [4, 5, 6, 7], [8, 9, 10, 11], [12, 13, 14, 15], [16, 17]]

        def premix_gen(l, gsel):
            for g in gsel:
                tiles = GROUPS[g]
                for t, i in enumerate(tiles):
                    s = nxt("x", 3)
                    ap, res = xsrc(l, i)
                    DMA("sp", XT[s][:], ap, [res] if res else [], [("xt", s)], ("xt", s))
                    norm_tile(t, XT[s][:], ("xt", s))
                    yield
                transpose_group(tiles, 0, ("hT", g))
                yield

        def proj_gen(l):
            wv = d_win[l].rearrange("(k p) n -> p k n", p=128)
            coltiles = [(0, 512), (512, 512), (1024, 512), (1536, 512), (2048, 256)]
            blocks = [("qk", 0, 0, 2), ("qk", 2, 2, 2), ("qk", 4, 4, 1),
                      ("conf", 0, 5, 2), ("conf", 1, 7, 2), ("short", 0, 9, 3), ("short", 1, 12, 3), ("v", 0, 15, 1)]

            def load(bi):
                kind, idx, ch0, nch = blocks[bi]
                buf = bi % 2
                DMA("pool", WIN[buf][:, :, 0:nch * 128], wv[:, :, ch0 * 128:(ch0 + nch) * 128], [BIGR], [("win", buf)], ("win", buf))

            def rope_finish(item):
                fmain, tbq, dst, dres, c0, n = item
                f2 = (1 + nxt("f", 5))
                MM(psf[f2][:, 0:n], PERM[:], TMPB[tbq][:, 0:n], True, True, ["perm", ("tmpb", tbq)], [PSF[f2]])
                t1, t2 = nxt("t", 4), nxt("t", 4)
                TT("dve", TMP[t1][:, 0:n], psf[fmain][:, 0:n], ROPE[:, c0:c0 + n], ALU.mult,
                   [PSF[fmain], "rope"], [("tmp", t1)])
                TT("dve", TMP[t2][:, 0:n], psf[f2][:, 0:n], ROPE[:, SEQ + c0:SEQ + c0 + n], ALU.mult,
                   [PSF[f2], "rope"], [("tmp", t2)])
                TT("pool", dst, TMP[t1][:, 0:n], TMP[t2][:, 0:n], ALU.add, [("tmp", t1), ("tmp", t2)], [dres])
            load(0)
            for bi, (kind, idx, ch0, nch) in enumerate(blocks):
                if bi + 1 < len(blocks):
                    load(bi + 1)
                buf = bi % 2
                W = WIN[buf]
                wres = ("win", buf)
                if kind == "v":
                    for i in range(NT):
                        f = (1 + nxt("f", 5))
                        for k in range(8):
                            MM(psf[f][:, 0:128], HT[:, k, i * 128:(i + 1) * 128], W[:, k, 0:128], k == 0, k == 7,
                               [("hT", min(i // 4, 4)), wres], [PSF[f]])
                        CP("dve", VT[:, i, 0:64], psf[f][:, 0:64], [PSF[f]], [("V", i)])
                        ACT(VT[:, i, 192:256], psf[f][:, 64:128], AF.Copy, [PSF[f]], [("V", i)])
                        if i % 3 == 2:
                            yield
                    continue
                if kind == "qk":
                    pend = None
                    for ci in range(nch):
                        chunk = idx + ci
                        isk = chunk == 4
                        for g, (c0, n) in enumerate(coltiles):
                            isctx = g == 4
                            if isctx and l == DEPTH - 1 and not isk:
                                continue
                            f = (1 + nxt("f", 5))
                            for k in range(8):
                                MM(psf[f][:, 0:n], W[:, k, ci * 128:(ci + 1) * 128], HT[:, k, c0:c0 + n], k == 0, k == 7,
                                   [("hT", g), wres], [PSF[f]])
                            dst = KT[:, c0:c0 + n] if isk else QT[:, chunk, c0:c0 + n]
                            dres = ("kT", g) if isk else ("qT", chunk, g)
                            if isctx:
                                ACT(dst, psf[f][:, 0:n], AF.Copy, [PSF[f]], [dres])
                            else:
                                tbq = nxt("tb", 3)
                                ACT(TMPB[tbq][:, 0:n], psf[f][:, 0:n], AF.Copy, [PSF[f]], [("tmpb", tbq)])
                                if pend is not None:
                                    rope_finish(pend)
                                pend = (f, tbq, dst, dres, c0, n)
                            yield
                    if pend is not None:
                        rope_finish(pend)
                    continue
                for g, (c0, n) in enumerate(coltiles):
                    isctx = g == 4
                    if isctx and l == DEPTH - 1:
                        continue
                    banks = []
                    for ci in range(nch):
                        f = (1 + nxt("f", 5))
                        banks.append(f)
                        for k in range(8):
                            MM(psf[f][:, 0:n], W[:, k, ci * 128:(ci + 1) * 128], HT[:, k, c0:c0 + n], k == 0, k == 7,
                               [("hT", g), wres], [PSF[f]])
                    if kind == "conf":
                        t1 = nxt("t", 4)
                        ACT(TMP[t1][:, 0:n], psf[banks[1]][:, 0:n], AF.Sigmoid, [PSF[banks[1]]], [("tmp", t1)])
                        off = 15 + c0 if not isctx else 2093
                        TT("dve", UC[:, idx, off:off + n], psf[banks[0]][:, 0:n], TMP[t1][:, 0:n], ALU.mult,
                           [PSF[banks[0]], ("tmp", t1)], [("uc", g)])
                    elif kind == "short":
                        t1 = nxt("t", 4)
                        ACT(SBS[:, idx, c0:c0 + n], psf[banks[0]][:, 0:n], AF.Copy, [PSF[banks[0]]], [("sbs", g)])
                        ACT(TMP[t1][:, 0:n], psf[banks[2]][:, 0:n], AF.Copy, [PSF[banks[2]]], [("tmp", t1)])
                        off = 1 + c0 if not isctx else 2051
                        TT("dve", PP[:, idx, off:off + n], psf[banks[1]][:, 0:n], TMP[t1][:, 0:n], ALU.mult,
                           [PSF[banks[1]], ("tmp", t1)], [("pp", g)])
                    yield

        def stat_rstd(psS, n, N, rd, tmpi):
            ACT(TMP[tmpi][:, 0:N], psS[:, 0:N], AF.Sqrt, list(rd) + ["const"], [("tmp", tmpi)], scale=1.0 / n, bias=EPSC[:, 0:1])
            RECIP(TMP[tmpi][:, 0:N], TMP[tmpi][:, 0:N], [("tmp", tmpi)], [("tmp", tmpi)])

        def tgeom(T):
            isctx = T == 4
            N = 256 if isctx else 512
            q0 = 2048 if isctx else T * 512
            return isctx, N, q0

        def gnorm_gen(T, ch0, nch, bank, tmps):
            isctx, N, q0 = tgeom(T)
            zb = T % 2
            Z = ZT[zb]
            zr = lambda ch: ("zT", zb, ch)
            for ci in range(nch):
                tb = nxt("tb", 3)
                ACT(TMPB[tb][:, 0:N], Z[:, ch0 + ci, 0:N], AF.Square, [zr(ch0 + ci)], [("tmpb", tb)])
                MM(psf[bank][:, 0:N], ONESB[:, 0:128], TMPB[tb][:, 0:N], ci == 0, ci == nch - 1, ["const", ("tmpb", tb)], [PSF[bank]])
                yield
            tm_, tmr = tmps
            ACT(tm_[:, 0:N], psf[bank][:, 0:N], AF.Ln, [PSF[bank], "const"], [tmr], scale=1.0 / (nch * 128), bias=EPSC[:, 0:1])
            ACT(tm_[:, 0:N], tm_[:, 0:N], AF.Exp, [tmr], [tmr], scale=-0.5)
            yield
            for ci in range(nch):
                STT(Z[:, ch0 + ci, 0:N], Z[:, ch0 + ci, 0:N], L.VEC[:, 32 + ch0 + ci:33 + ch0 + ci], tm_[:, 0:N],
                    ALU.mult, ALU.mult, [zr(ch0 + ci), tmr, L.vecr], [zr(ch0 + ci)])
            yield

        def conf_short_gen(l, T):
            isctx, N, q0 = tgeom(T)
            zb = T % 2
            Z = ZT[zb]
            zr = lambda ch: ("zT", zb, ch)
            uoff = 2093 if isctx else 15 + q0
            cvt = [0, 1]
            tm = 2
            ucr = ["diag", ("uc", T), ("uc", max(T - 1, 0)), ("uc", min(T + 1, 3) if not isctx else 4)]
            for cc in range(2):
                for k in range(31):
                    MM(psf[6 + cc][:, 0:N], DIAG[:, cc * 31 + k, :], UC[:, cc, uoff - 15 + k:uoff - 15 + k + N], k == 0, k == 30,
                       ucr, [PSF[6 + cc]])
                    if k % 8 == 7:
                        yield
                ACT(TMP[cvt[cc]][:, 0:N], psf[6 + cc][:, 0:N], AF.Identity, [PSF[6 + cc], L.vecr], [("tmp", cvt[cc])], bias=L.VEC[:, 40 + cc:41 + cc])
                yield
            hb = [nxt("tb", 3), nxt("tb", 3)]
            for cc in range(2):
                CP("dve", TMPB[hb[cc]][:, 0:N], TMP[cvt[cc]][:, 0:N], [("tmp", cvt[cc])], [("tmpb", hb[cc])])
            yield
            for cc in range(2):
                MM(psf[6][:, 0:N], ONESB[:, 0:128], TMPB[hb[cc]][:, 0:N], cc == 0, cc == 1, ["const", ("tmpb", hb[cc])], [PSF[6]])
            yield
            TS(TMP[tm][:, 0:N], psf[6][:, 0:N], 1.0 / 256, None, ALU.mult, None, [PSF[6]], [("tmp", tm)])
            yield
            for cc in range(2):
                TT("dve", TMP[cvt[cc]][:, 0:N], TMP[cvt[cc]][:, 0:N], TMP[tm][:, 0:N], ALU.subtract,
                   [("tmp", cvt[cc]), ("tmp", tm)], [("tmp", cvt[cc])])
                ACT(TMPB[hb[cc]][:, 0:N], TMP[cvt[cc]][:, 0:N], AF.Square, [("tmp", cvt[cc])], [("tmpb", hb[cc])])
                yield
            for cc in range(2):
                MM(psf[7][:, 0:N], ONESB[:, 0:128], TMPB[hb[cc]][:, 0:N], cc == 0, cc == 1, ["const", ("tmpb", hb[cc])], [PSF[7]])
            yield
            ACT(TMP[tm][:, 0:N], psf[7][:, 0:N], AF.Ln, [PSF[7], "const"], [("tmp", tm)], scale=1.0 / 256, bias=EPSC[:, 0:1])
            ACT(TMP[tm][:, 0:N], TMP[tm][:, 0:N], AF.Exp, [("tmp", tm)], [("tmp", tm)], scale=-0.5)
            yield
            for cc in range(2):
                TT("dve", TMP[cvt[cc]][:, 0:N], TMP[cvt[cc]][:, 0:N], TMP[tm][:, 0:N], ALU.mult,
                   [("tmp", cvt[cc]), ("tmp", tm)], [("tmp", cvt[cc])])
            yield
            for cc in range(2):
                ACT(TMP[tm][:, 0:N], TMP[cvt[cc]][:, 0:N], AF.Exp, [("tmp", cvt[cc]), "nln"], [("tmp", tm)],
                    scale=NLN[:, cc:cc + 1], bias=NLN[:, 2 + cc:3 + cc])
                ACT(TMP[tm][:, 0:N], TMP[tm][:, 0:N], AF.Ln, [("tmp", tm), "const"], [("tmp", tm)], bias=ONESF[:, 0:1])
                ACT(TMP[tm][:, 0:N], TMP[tm][:, 0:N], AF.Exp, [("tmp", tm)], [("tmp", tm)], scale=-1.0)
                TS(TMP[cvt[cc]][:, 0:N], TMP[cvt[cc]][:, 0:N], L.VEC[:, 42 + cc:43 + cc], L.VEC[:, 44 + cc:45 + cc], ALU.mult, ALU.add,
                   [("tmp", cvt[cc]), L.vecr], [("tmp", cvt[cc])])
                TT("dve", Z[:, 4 + cc, 0:N], TMP[cvt[cc]][:, 0:N], TMP[tm][:, 0:N], ALU.mult,
                   [("tmp", cvt[cc]), ("tmp", tm)], [zr(4 + cc)])
                yield
            yield from gnorm_gen(T, 4, 2, 6, (TMP[tm], ("tmp", tm)))
            poff = 2051 if isctx else 1 + q0
            for cc in range(2):
                t1 = cc
                prd = [("pp", T), ("pp", max(T - 1, 0)), ("pp", min(T + 1, 3) if not isctx else 4), L.vecr]
                TS(TMP[t1][:, 0:N], PP[:, cc, poff - 1:poff - 1 + N], L.VEC[:, 46 + cc * 3:47 + cc * 3], None, ALU.mult, None,
                   prd, [("tmp", t1)])
                for k in (1, 2):
                    STT(TMP[t1][:, 0:N], PP[:, cc, poff - 1 + k:poff - 1 + k + N], L.VEC[:, 46 + cc * 3 + k:47 + cc * 3 + k],
                        TMP[t1][:, 0:N], ALU.mult, ALU.add, prd + [("tmp", t1)], [("tmp", t1)])
                yield
                TT("dve", Z[:, 6 + cc, 0:N], SBS[:, cc, (2048 if isctx else q0):(2048 if isctx else q0) + N], TMP[t1][:, 0:N], ALU.mult,
                   [("sbs", T), ("tmp", t1)], [zr(6 + cc)])
                yield
            yield from gnorm_gen(T, 6, 2, 7, (TMP[tm], ("tmp", tm)))

        def attn_gen(l, T):
            isctx, N, q0 = tgeom(T)
            zb = T % 2
            Z = ZT[zb]
            zr = lambda ch: ("zT", zb, ch)
            keys = [(16, 0, N, None), (17, 0, N, None)]
            if not isctx:
                for j in range(max(0, 4 * T - 1), min(15, 4 * T + 4) + 1):
                    lo = max(4 * T, j - 1)
                    hi = min(4 * T + 3, j + 1)
                    qoff = (lo - 4 * T) * 128
                    n = (hi - lo + 1) * 128
                    moff = (lo - (j - 1)) * 128
                    needm = (lo == j - 1) or (hi == j + 1)
                    keys.append((j, qoff, n, moff if needm else None))
            nk = len(keys)
            for c in range(4):
                for ki in range(nk + 1):
                    if ki < nk:
                        j, qoff, n, moff = keys[ki]
                        kg = min(j // 4, 4)
                        for half in range(2):
                            pr = slice(64 * half, 64 * half + 64)
                            f = half
                            MM(psf[f][:, 0:n], KT[pr, j * 128:(j + 1) * 128], QT[pr, c, q0 + qoff:q0 + qoff + n], True, moff is None,
                               [("kT", kg), ("qT", c, T)], [PSF[f]])
                        for half in range(2):
                            f = half
                            pt = 2 * half + (ki % 2)
                            if moff is not None:
                                msegs = []
                                if moff == 0:
                                    msegs.append((0, 0))
                                if moff + n == 384:
                                    msegs.append((n - 128, 256))
                                for si, (pc, mc) in enumerate(msegs):
                                    MM(psf[f][:, pc:pc + 128], IDENT[:], MASK[:, mc:mc + 128], False, si == len(msegs) - 1,
                                       ["ident", "const"], [PSF[f]])
                            ACT(PT[pt][:, 0:n], psf[f][:, 0:n], AF.Exp, [PSF[f]], [("pt", pt)], scale=0.125)
                    if ki >= 1:
                        pj, pqoff, pn, _ = keys[ki - 1]
                        for half in range(2):
                            pp_ = 2 * half + ((ki - 1) % 2)
                            MM(psf[4 + half][:, pqoff:pqoff + pn], VT[:, pj, half * 128:(half + 1) * 128], PT[pp_][:, 0:pn], ki == 1, ki == nk,
                               [("V", pj), "vones", ("pt", pp_)], [PSF[4 + half]])
                    yield
                for half in range(2):
                    h = c + 4 * half
                    pr = slice(64 * half, 64 * half + 64)
                    dn = slice(64 * (1 - half), 64 * (1 - half) + 64)
                    psO = psf[4 + half]
                    at = ATMP[half]
                    ACT(at[pr, 0:N], psO[dn, 0:N], AF.Ln, [PSF[4 + half], "esc"], [("atmp", half)], bias=ESC[dn, h:h + 1])
                    ACT(at[pr, 0:N], at[pr, 0:N], AF.Exp, [("atmp", half)], [("atmp", half)], scale=-1.0)
                    TT("dve", Z[pr, c, 0:N], psO[pr, 0:N], at[pr, 0:N], ALU.mult, [PSF[4 + half], ("atmp", half)], [zr(c)])
                yield

        def tail_gen(l, T):
            isctx, N, q0 = tgeom(T)
            zb = T % 2
            Z = ZT[zb]
            zr = lambda ch: ("zT", zb, ch)
            yield from gnorm_gen(T, 0, 4, 6, (TMP[3], ("tmp", 3)))
            ntl = N // 128
            tiles = [q0 // 128 + t for t in range(ntl)]
            for t, i in enumerate(tiles):
                for hf in range(2):
                    for k in range(8):
                        MM(psf[2 + hf][:, :], Z[:, k, t * 128:(t + 1) * 128], WOUT[:, k, hf * 512:(hf + 1) * 512], k == 0, k == 7,
                           [zr(k), "wout"], [PSF[2 + hf]])
                    yield
                s = nxt("x", 3)
                postnorm_tile(l, i, s, 0, yb=2)
                yield
                norm_tile(t, XT[s][:], ("xt", s))
                yield
            r = 0 if tiles[0] < 16 else 1
            abv = AB[:].rearrange("p (w r k) -> p w r k", w=4, r=2)
            n = len(tiles)
            c0 = tiles[0] * 128
            for k in range(8):
                b = nxt("b", 2)
                for t in range(n):
                    TR(psb[b][:, t * 128:(t + 1) * 128], XN[:, t, k * 128:(k + 1) * 128], [("xn", t)], [PSB[b]])
                ACT(HT[:, k, c0:c0 + n * 128], psb[b][:, 0:n * 128], AF.Identity, [PSB[b], "ab1"], [("hT", T)],
                    scale=abv[:, 2, r, k:k + 1], bias=abv[:, 3, r, k:k + 1])
                yield

        def chain(*gens):
            for g in gens:
                if g is not None:
                    yield from g

        def run_merged(gens):
            gens = [g for g in gens if g is not None]
            while gens:
                for g in list(gens):
                    try:
                        next(g)
                    except StopIteration:
                        gens.remove(g)

        def run_balanced(gens):
            st_ = [[g, 0, float(n)] for (g, n) in gens if g is not None]
            while st_:
                st_.sort(key=lambda x: x[1] / x[2])
                g = st_[0]
                try:
                    next(g[0])
                    g[1] += 1
                except StopIteration:
                    st_.remove(g)

        def mixer_all(l, nT, kstop=99):
            run_merged([conf_short_gen(l, 0)])
            run_merged([attn_gen(l, 0), conf_short_gen(l, 1)])
            for T in range(nT):
                if T == nT - 1:
                    return tail_gen(l, T)
                A = attn_gen(l, T + 1) if T + 1 < nT else None
                hasC = T + 2 < nT
                B = chain(tail_gen(l, T), conf_short_gen(l, T + 2) if hasC else None)
                run_balanced([(A, 40 if T + 1 < 4 else 12), (B, 60 if hasC else 30)])

        def postnorm_tile(l, i, s, w, final=False, yb=4):
            r = 0 if i < 16 else 1
            sc = 16 + 4 * (i % 4)
            if w == 0:
                ap, res = xsrc(l, i)
            else:
                ap, res = d_xs[i * 128:(i + 1) * 128, :], ("dxs", i)
            DMA("sp", XT[s][:], ap, [res] if res else [], [("xt", s)], ("xt", s))
            for hf in range(2):
                ACT(JUNK[:, 0:512], psf[yb + hf][:, :], AF.Square, [PSF[yb + hf]], ["junk", ("stat", sc + hf)],
                    accum=STAT[:, sc + hf:sc + hf + 1])
            TT("dve", STAT[:, sc + 2:sc + 3], STAT[:, sc:sc + 1], STAT[:, sc + 1:sc + 2], ALU.add,
               [("stat", sc), ("stat", sc + 1)], [("stat", sc + 2)])
            TS(STAT[:, sc + 2:sc + 3], STAT[:, sc + 2:sc + 3], 1.0 / D, EPS, ALU.mult, ALU.add, [("stat", sc + 2)], [("stat", sc + 2)])
            POW(STAT[:, sc + 3:sc + 4], STAT[:, sc + 2:sc + 3], 1, [("stat", sc + 2)], [("stat", sc + 3)])
            for hf in range(2):
                t1 = nxt("t", 4)
                STT(TMP[t1][:, :], psf[yb + hf][:, :], STAT[:, sc + 3:sc + 4], GB[:, r, hf * 512:(hf + 1) * 512],
                    ALU.mult, ALU.mult, [PSF[yb + hf], ("stat", sc + 3), ("gb", r)], [("tmp", t1)])
                TT("pool", XT[s][:, hf * 512:(hf + 1) * 512], XT[s][:, hf * 512:(hf + 1) * 512], TMP[t1][:, :], ALU.add,
                   [("xt", s), ("tmp", t1)], [("xt", s)])
            if final:
                if i < 16:
                    op = DMA("sp", d_out[i * 128:(i + 1) * 128, :], XT[s][:], [("xt", s)], [("dout", i)], ("xt", s))
                    out_dmas.append(op)
            else:
                DMA("sp", d_xs[i * 128:(i + 1) * 128, :], XT[s][:], [("xt", s)], [("dxs", i)], ("xt", s))

        def mixer_setup(l):
            TS(NLN[:, 0:4], L.VEC[:, 42:46], -1.0, None, ALU.mult, None, [L.vecr], ["nln"])
            DMA("pool", ROPE, d_rope, [BIGR], ["rope"], "rope")
            MEMSET("pool", UC[:, :, :], 0.0, [BIGR, ("uc", 0), ("uc", 1), ("uc", 2), ("uc", 3), ("uc", 4)])
            MEMSET("pool", PP[:, :, :], 0.0, [BIGR, ("pp", 0), ("pp", 1), ("pp", 2), ("pp", 3), ("pp", 4)])
            MEMSET("pool", VT[:, :, 64:192], 1.0, ["vones"])
            for cc in range(2):
                for k in range(31):
                    TS(DIAG[:, cc * 31 + k, :], IDENT[:], L.VEC[:, 52 + cc * 31 + k:53 + cc * 31 + k], None, ALU.mult, None,
                       ["ident", L.vecr], ["diag"])

        def load_wout(l):
            wv = d_wout[l].rearrange("(k p) n -> p k n", p=128)
            DMA("pool", WOUT[:, :, :], wv, [("win", 0), ("win", 1)], ["wout", ("win", 0), ("win", 1)], "wout")

        def ffn_geom(l, hi_):
            last = l == DEPTH - 1
            ta, tb_ = [(0, 9), (9, 16 if last else 18)][hi_]
            tok_lo, tok_hi = ta * 128, min(tb_, 16) * 128
            segs = []
            ulo = max(tok_lo - 1, 0)
            uhi = min(tok_hi + 1, SEQ)
            for (c0, n) in split_cols(ulo, uhi):
                segs.append((c0, n, c0 - tok_lo + 1))
            nx = tok_hi - tok_lo
            ctxcol = None
            if tb_ > 16:
                ctxcol = nx + 3
                segs.append((2048, 256, ctxcol))
            rowlen = (ctxcol + 256 + 1) if ctxcol is not None else nx + 2
            return ta, tb_, segs, nx, ctxcol, rowlen

        FF_EARLY = ["diag", "vones", "rope", ("atmp", 0), ("atmp", 1)] + [("pt", i) for i in range(4)] + \
                   [("uc", g) for g in range(5)] + [("pp", g) for g in range(5)] + \
                   [("sbs", g) for g in range(5)] + [("kT", g) for g in range(5)] + [("V", i) for i in range(NT)] + \
                   [("qT", c, g) for c in range(4) for g in range(5)]
        FF_LATE = [BIGR, "wout", ("win", 0), ("win", 1)] + [("zT", i, ch) for i in range(2) for ch in range(8)]

        def ffn_fence_a():
            S.add("pool", lambda e: e.memset(STAT[:, 60:61], 0.0), FF_EARLY, ["ffnfence"] + FF_EARLY)

        def ffn_fence_b():
            S.add("pool", lambda e: e.memset(STAT[:, 62:63], 0.0), FF_LATE + ["ffnfence"], ["ffnfenceB"] + FF_LATE)

        def ffn_up_gen(l, hi_, banks=(0, 1, 2, 3), order=None):
            wuv = d_wup[l].rearrange("(k p) n -> p k n", p=128)
            ta, tb_, segs, nx, ctxcol, rowlen = ffn_geom(l, hi_)
            for i in range(4):
                S.add("pool", lambda e, a=UROW[i]: e.memset(a[:, :], 0.0), ["ffnfence"], [("urow", i)])

            order = list(range(NFF)) if order is None else list(order)

            def load_wup(p):
                b_ = p % 3
                jj_ = order[p]
                DMA("pool", WUP[b_][:, :, :], wuv[:, :, jj_ * 256:(jj_ + 1) * 256], ["ffnfence"], [("wup", b_)], ("wup", b_))
            load_wup(0)
            load_wup(1)
            for p, j in enumerate(order):
                buf = p % 3
                if p + 2 < NFF:
                    load_wup(p + 2)
                if hi_ == 0 and p in (1, 3):
                    wdv = d_wdn[l].rearrange("(k p) n -> p k n", p=128)
                    h0 = 0 if p == 1 else 11
                    DMA("pool", WDN[:, h0:h0 + 11, :], wdv[:, h0:h0 + 11, :], ["ffnfence"], ["wdn"], "wdn%d" % (p // 2))
                ub = (p % 2) * 2
                L_ = rowlen - 2
                for gv in range(2):
                    ur = ub + gv
                    for (c0, n, col) in segs:
                        f = banks[nxt("f", 4)]
                        gs = sorted(set([min(c0 // 512, 4), min((c0 + n - 1) // 512, 4)]))
                        for k in range(8):
                            MM(psf[f][:, 0:n], WUP[buf][:, k, gv * 128:(gv + 1) * 128], HT[:, k, c0:c0 + n], k == 0, k == 7,
                               [("hT", g) for g in gs] + [("wup", buf)], [PSF[f]])
                        ACT(UROW[ur][:, col:col + n], psf[f][:, 0:n], AF.Copy, [PSF[f]], [("urow", ur)])
                    wc = 114 + (gv * NFF + j) * 3
                    TS(CROW[ur][:, 1:1 + L_], UROW[ur][:, 0:L_], L.VEC[:, wc:wc + 1], None, ALU.mult, None,
                       [("urow", ur), L.vecr], [("crow", ur)])
                    for k in (1, 2):
                        STT(CROW[ur][:, 1:1 + L_], UROW[ur][:, k:k + L_], L.VEC[:, wc + k:wc + k + 1], CROW[ur][:, 1:1 + L_],
                            ALU.mult, ALU.add, [("urow", ur), ("crow", ur), L.vecr], [("crow", ur)])
                    yield
                ACT(CROW[ub][:, 1:1 + L_], CROW[ub][:, 1:1 + L_], AF.Silu, [("crow", ub)], [("crow", ub)])
                TT("pool", GT[:, j, 0:nx], CROW[ub][:, 1:1 + nx], CROW[ub + 1][:, 1:1 + nx], ALU.mult,
                   [("crow", ub), ("crow", ub + 1)] + (["ffnfenceB"] if j < 15 else []), [("gT", j)])
                if ctxcol is not None:
                    TT("pool", GT[:, j, nx:nx + 256], CROW[ub][:, ctxcol:ctxcol + 256], CROW[ub + 1][:, ctxcol:ctxcol + 256], ALU.mult,
                       [("crow", ub), ("crow", ub + 1)] + (["ffnfenceB"] if j < 15 else []), [("gT", j)])
                yield

        def ffn_down(l, hi_):
            last = l == DEPTH - 1
            ta, tb_, segs, nx, ctxcol, rowlen = ffn_geom(l, hi_)
            for t, i in enumerate(range(ta, tb_)):
                yb = 4 if t % 2 == 0 else 2
                for hf in range(2):
                    for k in range(NFF):
                        MM(psf[yb + hf][:, :], GT[:, k, t * 128:(t + 1) * 128], WDN[:, k, hf * 512:(hf + 1) * 512], k == 0, k == NFF - 1,
                           [("gT", k), "wdn"], [PSF[yb + hf]])
                s = nxt("x", 3)
                postnorm_tile(l, i, s, 1, final=last, yb=yb)
                g = i // 4
                if not last:
                    norm_tile(i % 4, XT[s][:], ("xt", s))
                    if i == GROUPS[g][-1]:
                        transpose_group(GROUPS[g], 0, ("hT", g))

        def ffn_end(l):
            ffnres = ["wdn"] + [("wup", i) for i in range(3)] + [("urow", i) for i in range(4)] + \
                     [("crow", i) for i in range(4)] + [("gT", j) for j in range(NFF)]
            S.add("pool", lambda e: e.memset(STAT[:, 61:62], 0.0), ffnres + [BIGR], ffnres + [BIGR])

        kstop = int(os.environ.get("KSTOP", "99"))
        XNW = [XN[:, 0:2, :].rearrange("p a b -> p (a b)").rearrange("p (k n) -> p k n", n=256),
               XN[:, 2:4, :].rearrange("p a b -> p (a b)").rearrange("p (k n) -> p k n", n=256)]
        XNWR = [[("xn", 0), ("xn", 1)], [("xn", 2), ("xn", 3)]]
        for l in range(DEPTH):
            S.epoch = l
            L.VEC = VECS[l % 2]
            L.vecr = ("vec", l % 2)
            if l == 0:
                mg = mod_gen(0, [ZT[0], ZT[1]], [[("zT", 0, ch) for ch in range(8)], [("zT", 1, ch) for ch in range(8)]], 0, 512)
                import itertools
                for _ in range(5):
                    next(mg)
                run_merged([itertools.islice(mg, 3), premix_gen(0, [0, 1, 2, 3, 4])])

            else:
                pass
            if kstop <= 0:
                break
            grows(0, l)
            if kstop <= 1:
                break
            mixer_setup(l)
            if l == 0:
                run_balanced([(proj_gen(l), 70), (mg, 7)])
            else:
                run_merged([proj_gen(l)])
            load_wout(l)
            if kstop <= 2:
                break
            last_tail = mixer_all(l, 5 if l < DEPTH - 1 else 4, kstop)
            if kstop <= 4:
                run_merged([last_tail])
                break
            ffn_fence_a()
            import itertools
            up0 = ffn_up_gen(l, 0, banks=(0, 1, 4, 5), order=list(range(15, NFF)) + list(range(15)))
            run_balanced([(last_tail, 30), (itertools.islice(up0, 21), 21)])
            ffn_fence_b()
            run_balanced([(mod_gen(l + 1, XNW, XNWR, 6, 256) if l + 1 < DEPTH else None, 26), (up0, 45)])
            grows(1, l)
            up1 = ffn_up_gen(l, 1)
            next(up1)
            ffn_down(l, 0)
            run_merged([up1])
            ffn_down(l, 1)
            ffn_end(l)
            if kstop <= 5:
                break
        S.emit(final_waits=out_dmas)
    return nc


def rope_tables():
    t = np.arange(SEQ)
    pos = np.stack([t // 64, t % 64], 0).astype(np.float32)
    inv = (10000.0 ** (-np.arange(16, dtype=np.float32) / 16)).astype(np.float32)
    cos = np.zeros((128, SEQ), np.float32)
    sin = np.zeros((128, SEQ), np.float32)
    for p in range(128):
        d = p % 64
        r, idx = d // 32, d % 32
        ang = pos[r] * inv[idx % 16]
        cos[p] = np.cos(ang)
        sin[p] = np.sin(ang) * (-1.0 if idx < 16 else 1.0)
    return np.concatenate([cos, sin], 1)


def col_layout(v):
    return np.ascontiguousarray(v.reshape(-1, 128).T)


def rot_partner(d):
    r, idx = d // 32, d % 32
    return r * 32 + (idx + 16 if idx < 16 else idx - 16)


def win_perm():
    out = []
    for c in range(4):
        out.append(np.concatenate([np.arange(c * 64, c * 64 + 64), np.arange((c + 4) * 64, (c + 4) * 64 + 64)]))
    out.append(512 + np.arange(128))
    cv0, cg0, sb0, scg0, su0 = 768, 1024, 1280, 1536, 1792
    for cc in range(2):
        out += [cv0 + cc * 128 + np.arange(128), cg0 + cc * 128 + np.arange(128)]
    for cc in range(2):
        out += [sb0 + cc * 128 + np.arange(128), scg0 + cc * 128 + np.arange(128), su0 + cc * 128 + np.arange(128)]
    out += [640 + np.arange(128)]
    return np.concatenate(out)


def perm_matrix():
    pm = np.zeros((128, 128), np.float32)
    for m in range(128):
        k = (m // 64) * 64 + rot_partner(m % 64)
        pm[k, m] = 1.0
    return pm


def attn_chan_perm():
    idx = []
    for c in range(4):
        idx += list(range(c * 64, c * 64 + 64)) + list(range((c + 4) * 64, (c + 4) * 64 + 64))
    idx += list(range(512, 1024))
    return np.array(idx)


_CACHE = {}


def prepare_inputs(inputs):
    f = lambda a: np.ascontiguousarray(np.asarray(a, dtype=np.float32))
    x, c, ctx, c_ctx = f(inputs["x"]), f(inputs["c"]), f(inputs["ctx"]), f(inputs["c_ctx"])
    rope = rope_tables()
    kk = np.arange(128)[:, None]
    qq = np.arange(128)[None, :]
    mask = np.zeros((128, 384), np.float32)
    mask[:, 0:128] = np.where(kk <= qq, 0.0, -30000.0)
    mask[:, 256:384] = np.where(qq <= kk, 0.0, -30000.0)
    perm = win_perm()
    zperm = attn_chan_perm()
    shared = {"rope": rope, "mask": mask, "perm": perm_matrix()}
    for l in range(DEPTH):
        shared["wada%d" % l] = f(inputs["w_ada"][l])
        vec = np.zeros((128, NV), np.float32)
        vec[:, 0:8] = col_layout(f(inputs["g_pre_mix"][l]))
        vec[:, 8:16] = col_layout(f(inputs["g_post_mix"][l]))
        vec[:, 16:24] = col_layout(f(inputs["g_pre_ffn"][l]))
        vec[:, 24:32] = col_layout(f(inputs["g_post_ffn"][l]))
        vec[:, 32:40] = col_layout(f(inputs["g_group"][l])[zperm])
        vec[:, 40:42] = col_layout(f(inputs["b_conf_dw"][l]))
        vec[:, 42:44] = col_layout(f(inputs["conf_ln_g"][l]))
        vec[:, 44:46] = col_layout(f(inputs["conf_ln_b"][l]))
        wsc = f(inputs["w_sc_dw"][l])
        wcf = f(inputs["w_conf_dw"][l])
        wff = f(inputs["w_ffn_dw"][l])
        for cc in range(2):
            vec[:, 46 + cc * 3:49 + cc * 3] = wsc[:, cc * 128:(cc + 1) * 128].T
            vec[:, 52 + cc * 31:83 + cc * 31] = wcf[:, cc * 128:(cc + 1) * 128].T
        for ch in range(44):
            vec[:, 114 + ch * 3:117 + ch * 3] = wff[:, ch * 128:(ch + 1) * 128].T
        ba = col_layout(f(inputs["b_ada"][l]))
        vec[:, 246:342] = np.repeat(ba, 2, axis=1)
        shared["vec%d" % l] = vec
        shared["win%d" % l] = np.ascontiguousarray(f(inputs["w_in"][l])[:, perm])
        shared["sink%d" % l] = np.ascontiguousarray(np.broadcast_to(f(inputs["sink"][l])[None, :], (128, 8)))
        shared["wout%d" % l] = np.ascontiguousarray(f(inputs["w_out"][l])[zperm, :])
        wu = f(inputs["w_up"][l])
        shared["wup%d" % l] = np.ascontiguousarray(
            np.stack([wu[:, :DFF].reshape(D, NFF, 128), wu[:, DFF:].reshape(D, NFF, 128)], axis=2).reshape(D, 2 * DFF))
        shared["wdn%d" % l] = f(inputs["w_down"][l])
    in_maps = []
    for b in range(8):
        m = dict(shared)
        m["x"] = x[b]
        m["ctx"] = ctx[b]
        cc = np.zeros((128, 16), np.float32)
        cc[:, 0::2] = col_layout(c[b])
        cc[:, 1::2] = col_layout(c_ctx)
        m["cc"] = cc
        in_maps.append(m)
    return in_maps


def kernel(**inputs):
    in_maps = prepare_inputs(inputs)
    if "nc" not in _CACHE:
        _CACHE["nc"] = build_program()
    nc = _CACHE["nc"]
    res = run_bass_kernel_spmd(nc, in_maps, core_ids=list(range(8)))
    out = np.stack([np.asarray(r["out"], dtype=np.float32) for r in res.results], 0)
    return out
```

```python
import contextlib
import os
import numpy as np
import ml_dtypes
import concourse.bass as bass
import concourse.mybir as mybir
from concourse.bass_utils import run_bass_kernel_spmd

F32 = mybir.dt.float32
BF16 = mybir.dt.bfloat16
AF = mybir.ActivationFunctionType
ALU = mybir.AluOpType

D = 1024
SEQ = 2048
CTX = 256
NT = 18
NTOK = NT * 128
DEPTH = 2
DFF = 2816
NFF = 22
EPS = 1e-6
NCW = 16
NV = 342


class Op:
    __slots__ = ("eng", "fn", "deps", "sem", "ticket", "signal", "ninc", "name")


class Sched:
    ENGS = ("pe", "act", "dve", "pool", "sp")

    def __init__(self, nc):
        self.nc = nc
        self.ops = {e: [] for e in self.ENGS}
        self.lastw = {}
        self.readers = {}
        self.dma_count = {}
        self.all_ops = []
        self.epoch = 0

    def add(self, eng, fn, reads=(), writes=(), dma_key=None, ndma=1, name=""):
        op = Op()
        op.eng, op.fn, op.name = eng, fn, name
        op.signal = False
        deps = set()
        excl = [r for r in reads if isinstance(r, str) and r.startswith("ps:")]
        reads = [r for r in reads if r not in excl]
        writes = list(writes) + excl
        for r in reads:
            lw = self.lastw.get(r)
            if lw is not None:
                deps.add(lw)
        for w in writes:
            lw = self.lastw.get(w)
            if lw is not None:
                deps.add(lw)
            lastrd = {}
            for rd in self.readers.get(w, ()):
                if rd.sem[0] == "dma":
                    deps.add(rd)
                else:
                    lastrd[rd.eng] = rd
            deps.update(lastrd.values())
        for r in reads:
            self.readers.setdefault(r, []).append(op)
        for w in writes:
            self.lastw[w] = op
            self.readers[w] = []
        deps.discard(op)
        if eng == "pe":
            deps = set(d for d in deps if d.eng != "pe")
        op.deps = deps
        if dma_key is not None:
            op.sem = ("dma", dma_key)
            self.dma_count[dma_key] = self.dma_count.get(dma_key, 0) + 16 * ndma
            op.ticket = self.dma_count[dma_key]
            op.ninc = ndma
            op.signal = True
        else:
            op.sem = ("eng", eng, self.epoch)
            op.ticket = None
            op.ninc = 0
        for d in deps:
            d.signal = True
        self.ops[eng].append(op)
        self.all_ops.append(op)
        return op

    def emit(self, final_waits=()):
        nc = self.nc
        cnts = {}
        for e in self.ENGS:
            for op in self.ops[e]:
                if op.sem[0] == "eng" and op.signal:
                    cnts[op.sem] = cnts.get(op.sem, 0) + 1
                    op.ticket = cnts[op.sem]
        with contextlib.ExitStack() as st:
            sems = {}
            for i, k in enumerate(cnts):
                sems[k] = st.enter_context(nc.semaphore("s%d" % i))
            for i, k in enumerate(self.dma_count):
                sems[("dma", k)] = st.enter_context(nc.semaphore("d%d" % i))
            block = st.enter_context(nc.Block())

            def run(engname, eng):
                waited = {}
                for op in self.ops[engname]:
                    need = {}
                    for d in op.deps:
                        if need.get(d.sem, 0) < d.ticket:
                            need[d.sem] = d.ticket
                    for s, v in need.items():
                        if waited.get(s, 0) < v:
                            eng.wait_ge(sems[s], v)
                            waited[s] = v
                    ins = op.fn(eng)
                    if op.sem[0] == "dma":
                        if not isinstance(ins, (list, tuple)):
                            ins = [ins]
                        assert len(ins) == op.ninc, (op.name, len(ins), op.ninc)
                        for i in ins:
                            i.then_inc(sems[op.sem], 16)
                    elif op.signal:
                        ins.then_inc(sems[op.sem], 1)
                if engname == "sp":
                    for op in final_waits:
                        eng.wait_ge(sems[op.sem], op.ticket)

            block.tensor(lambda eng: run("pe", eng))
            block.scalar(lambda eng: run("act", eng))
            block.vector(lambda eng: run("dve", eng))
            block.gpsimd(lambda eng: run("pool", eng))
            block.sync(lambda eng: run("sp", eng))


def split_cols(lo, hi, mx=512):
    out = []
    while lo < hi:
        n = min(mx, hi - lo)
        out.append((lo, n))
        lo += n
    return out


def build_program(stop_after=None):
    nc = bass.Bass("TRN2", target_bir_lowering=False)
    dram = lambda name, shape, dt=F32, kind="ExternalInput": nc.dram_tensor(name, shape, dt, kind=kind).ap()
    d_x = dram("x", [SEQ, D])
    d_ctx = dram("ctx", [CTX, D])
    d_cc = dram("cc", [128, 16])
    d_rope = dram("rope", [128, 2 * SEQ])
    d_mask = dram("mask", [128, 384])
    d_perm = dram("perm", [128, 128])
    d_wada = [dram("wada%d" % l, [D, 6 * D]) for l in range(DEPTH)]
    d_vec = [dram("vec%d" % l, [128, NV]) for l in range(DEPTH)]
    d_win = [dram("win%d" % l, [D, NCW * 128]) for l in range(DEPTH)]
    d_sink = [dram("sink%d" % l, [128, 8]) for l in range(DEPTH)]
    d_wout = [dram("wout%d" % l, [D, D]) for l in range(DEPTH)]
    d_wup = [dram("wup%d" % l, [D, 2 * DFF]) for l in range(DEPTH)]
    d_wdn = [dram("wdn%d" % l, [DFF, D]) for l in range(DEPTH)]
    d_out = dram("out", [SEQ, D], kind="ExternalOutput")
    d_xs = dram("xs_scratch", [NTOK, D], kind="Internal")

    st = contextlib.ExitStack()
    with st:
        sb = lambda name, shape, dt: st.enter_context(nc.sbuf_tensor(name, shape, dt))
        S = Sched(nc)

        BIG = sb("BIG", [128, 63296], BF16)
        HT = sb("HT", [128, 8, NTOK], BF16)
        GB = sb("GB", [128, 2, D], F32)
        XT = [sb("XT%d" % i, [128, D], F32) for i in range(3)]
        XN = sb("XN", [128, 4, D], BF16)
        JUNK = sb("JUNK", [128, D], BF16)
        TMP = [sb("TMP%d" % i, [128, 512], F32) for i in range(4)]
        TMPB = [sb("TMPB%d" % i, [128, 512], BF16) for i in range(3)]
        IDENT = sb("IDENT", [128, 128], BF16)
        IDENTF = sb("IDENTF", [128, 128], F32)
        ONESF = sb("ONESF", [128, 128], F32)
        ONESB = sb("ONESB", [128, 128], BF16)
        NHALF = sb("NHALF", [128, 8], F32)
        MASK = sb("MASK", [128, 384], BF16)
        PERM = sb("PERM", [128, 128], BF16)
        VECS = [sb("VEC%d" % i, [128, NV], F32) for i in range(2)]

        class L:
            VEC = None
            vecr = None
        CCF = sb("CCF", [128, 16], F32)
        CCB = sb("CCB", [128, 8, 2], BF16)
        MODC = sb("MODC", [128, 96], F32)
        AB = sb("AB", [128, 64], F32)
        GCS = [sb("GC%d" % i, [128, 32], F32) for i in range(2)]
        DIAGF = sb("DIAGF", [128, 128], F32)
        STAT = sb("STAT", [128, 64], F32)
        ESC = sb("ESC", [128, 8], F32)
        EPSC = sb("EPSC", [128, 1], F32)
        NLN = sb("NLN", [128, 4], F32)

        psf = [st.enter_context(nc.psum_tensor("psf%d" % i, [128, 512], F32)) for i in range(8)]
        psb = [psf[6 + i][:, :].bitcast(BF16) for i in range(2)]
        PSF = ["ps:f%d" % i for i in range(8)]
        PSB = [PSF[6], PSF[7]]

        def carve(off, shape):
            n = int(np.prod(shape))
            ap = BIG[:, off:off + n]
            if len(shape) == 2:
                ap = ap.rearrange("p (a b) -> p a b", b=shape[1])
            return ap, off + n
        o = 0
        QT, o = carve(o, [4, NTOK])
        KT, o = carve(o, [1, NTOK])
        KT = BIG[:, 9216:9216 + NTOK]
        VT, o = carve(o, [NT, 256])
        UCW = 2364
        UC, o = carve(o, [2, UCW])
        PW = 2308
        PP, o = carve(o, [2, PW])
        SBS, o = carve(o, [2, NTOK])
        DIAG, o = carve(o, [62, 128])
        WBUF_OFF = o
        WIN = []
        for i in range(2):
            w, o = carve(o, [8, 384]); WIN.append(w)
        WOUT, _ = carve(WBUF_OFF, [8, D])
        o = WBUF_OFF + 8 * D
        WADA = [WIN[0], WIN[1]]
        ZT = []
        for i in range(2):
            z, o = carve(o, [8, 512]); ZT.append(z)
        PT = []
        for i in range(4):
            p_, o = carve(o, [1, 512]); PT.append(BIG[:, o - 512:o])
        ROPE = BIG[:, o:o + 2 * SEQ]
        o += 2 * SEQ
        ATMP = []
        for i in range(2):
            ATMP.append(BIG[:, o:o + 1024].bitcast(F32))
            o += 1024
        assert o <= 63296, o
        o = 0
        WUP = []
        for i in range(3):
            w, o = carve(o, [8, 256]); WUP.append(w)
        UROW = []
        URW = 1160
        for i in range(4):
            u, o = carve(o, [1, URW]); UROW.append(BIG[:, o - URW:o])
        CROW = []
        for i in range(4):
            u, o = carve(o, [1, URW]); CROW.append(BIG[:, o - URW:o])
        WDN, o = carve(o, [NFF, D])
        assert o <= 38016, o
        GT, o = carve(o, [NFF, 1152])
        assert o <= 63296, o
        BIGR = "BIG"

        def MM(out, lhsT, rhs, start, stop, rd, wr):
            return S.add("pe", lambda e, a=(out, lhsT, rhs, start, stop): e.matmul(
                a[0], a[1], a[2], start=a[3], stop=a[4], skip_group_check=True), rd, wr)

        def TR(out, in_, rd, wr):
            return S.add("pe", lambda e, a=(out, in_): e.transpose(a[0], a[1], IDENT[:]), list(rd) + ["ident"], wr)

        def ACT(out, in_, func, rd, wr, scale=None, bias=None, accum=None):
            kw = {}
            if scale is not None:
                kw["scale"] = scale
            if bias is not None:
                kw["bias"] = bias
            if accum is not None:
                kw["accum_out"] = accum
            return S.add("act", lambda e, a=(out, in_, func, kw): e.activation(out=a[0], in_=a[1], func=a[2], **a[3]), rd, wr)

        def TT(eng, out, in0, in1, op, rd, wr):
            return S.add(eng, lambda e, a=(out, in0, in1, op): e.tensor_tensor(out=a[0], in0=a[1], in1=a[2], op=a[3]), rd, wr)

        def TS(out, in0, s1, s2, op0, op1, rd, wr):
            if op1 is None:
                return S.add("dve", lambda e, a=(out, in0, s1, op0): e.tensor_scalar(
                    out=a[0], in0=a[1], scalar1=a[2], scalar2=None, op0=a[3]), rd, wr)
            return S.add("dve", lambda e, a=(out, in0, s1, s2, op0, op1): e.tensor_scalar(
                out=a[0], in0=a[1], scalar1=a[2], scalar2=a[3], op0=a[4], op1=a[5]), rd, wr)

        def STT(out, in0, scalar, in1, op0, op1, rd, wr):
            return S.add("dve", lambda e, a=(out, in0, scalar, in1, op0, op1): e.scalar_tensor_tensor(
                out=a[0], in0=a[1], scalar=a[2], in1=a[3], op0=a[4], op1=a[5]), rd, wr)

        def CP(eng, out, in_, rd, wr):
            return S.add(eng, lambda e, a=(out, in_): e.tensor_copy(a[0], a[1]), rd, wr)

        def RECIP(out, in_, rd, wr):
            return S.add("dve", lambda e, a=(out, in_): e.reciprocal(a[0], a[1]), rd, wr)

        def MEMSET(eng, ap, val, wr):
            return S.add(eng, lambda e, a=(ap, val): e.memset(a[0], a[1]), [], wr)

        def DMA(q, out, in_, rd, wr, key):
            return S.add(q, lambda e, a=(out, in_): e.dma_start(out=a[0], in_=a[1]), rd, wr, dma_key=key)

        def POW(out, in_, n, rd, wr):
            return TT("pool", out, in_, NHALF[:, 0:n], ALU.pow, list(rd) + ["const"], wr)

        rr = {"f": 0, "b": 0, "x": 0, "t": 0, "tb": 0, "pt": 0}

        def nxt(kind, n):
            v = rr[kind] % n
            rr[kind] = (v + 1) % n
            return v

        MEMSET("pool", IDENT[:], 0.0, ["ident"])
        S.add("pool", lambda e: e.affine_select(out=IDENT[:], in_=IDENT[:], pattern=[[-1, 128]], compare_op=ALU.not_equal,
                                                fill=1.0, base=0, channel_multiplier=1), ["ident"], ["ident"])
        MEMSET("pool", IDENTF[:], 0.0, ["identf"])
        S.add("pool", lambda e: e.affine_select(out=IDENTF[:], in_=IDENTF[:], pattern=[[-1, 128]], compare_op=ALU.not_equal,
                                                fill=1.0, base=0, channel_multiplier=1), ["identf"], ["identf"])
        MEMSET("pool", ONESF[:], 1.0, ["const"])
        MEMSET("pool", ONESB[:], 1.0, ["const"])
        MEMSET("pool", NHALF[:], -0.5, ["const"])
        MEMSET("pool", EPSC[:], EPS, ["const"])
        DMA("pool", MASK[:], d_mask, [], ["const"], "mask")
        DMA("pool", PERM[:], d_perm, [], ["perm"], "perm")
        DMA("sp", CCF[:], d_cc, [], ["ccf"], "ccf")
        ACT(CCB[:].rearrange("p k r -> p (k r)"), CCF[:], AF.Silu, ["ccf"], ["ccb"])

        out_dmas = []

        def xsrc(l, i):
            if l == 0:
                if i < 16:
                    return d_x[i * 128:(i + 1) * 128, :], None
                return d_ctx[(i - 16) * 128:(i - 15) * 128, :], None
            return d_xs[i * 128:(i + 1) * 128, :], ("dxs", i)

        def mod_gen(l, bufs, bufres, bank, bw):
            VEC = VECS[l % 2]
            vecr = ("vec", l % 2)
            DMA("sp", VEC[:], d_vec[l], [], [vecr], ("vec", l % 2))
            DMA("sp", ESC[:], d_sink[l], [], ["esc"], "esc")
            ACT(ESC[:], ESC[:], AF.Exp, ["esc"], ["esc"])
            wv = d_wada[l].rearrange("(k p) n -> p k n", p=128)
            psM = psf[bank]
            mv = MODC[:].rearrange("p (j r) -> p j r", r=2)
            abv = AB[:].rearrange("p (w r k) -> p w r k", w=4, r=2)
            gcv = GCS[l % 2][:].rearrange("p (w r k) -> p w r k", w=2, r=2)
            nblk = 6 * D // bw
            cpb = bw // 128

            def load_blk(blk):
                b_ = blk % 2
                DMA("pool", bufs[b_], wv[:, :, blk * bw:(blk + 1) * bw], list(bufres[b_]), list(bufres[b_]), ("wada", b_))
            load_blk(0)
            for blk in range(nblk):
                buf = blk % 2
                if blk + 1 < nblk:
                    load_blk(blk + 1)
                for jj in range(cpb):
                    j = blk * cpb + jj
                    for k in range(8):
                        MM(psM[:, 2 * j:2 * j + 2], bufs[buf][:, k, jj * 128:(jj + 1) * 128], CCB[:, k, :],
                           k == 0, k == 7, list(bufres[buf]) + ["ccb"], [PSF[bank]])
                yield
                if blk == 16 // cpb - 1:
                    TT("dve", MODC[:, 0:32], psM[:, 0:32], VEC[:, 246:278], ALU.add, [PSF[bank], vecr], ["modc0"])
                    for r in range(2):
                        STT(abv[:, 0, r, :], mv[:, 8:16, r], 1.0, VEC[:, 0:8], ALU.add, ALU.mult, ["modc0", vecr], ["ab0"])
                        CP("dve", abv[:, 1, r, :], mv[:, 0:8, r], ["modc0"], ["ab0"])
                    yield
                if blk == 24 // cpb - 1:
                    TT("dve", MODC[:, 32:48], psM[:, 32:48], VEC[:, 278:294], ALU.add, [PSF[bank], vecr], ["modc1"])
                    for r in range(2):
                        TT("dve", gcv[:, 0, r, :], mv[:, 16:24, r], VEC[:, 8:16], ALU.mult, ["modc1", vecr], [("gc", l % 2, 0)])
                    yield
            TT("dve", MODC[:, 48:96], psM[:, 48:96], VEC[:, 294:342], ALU.add, [PSF[bank], vecr], ["modc"])
            for r in range(2):
                STT(abv[:, 2, r, :], mv[:, 32:40, r], 1.0, VEC[:, 16:24], ALU.add, ALU.mult, ["modc", vecr], ["ab1"])
                CP("dve", abv[:, 3, r, :], mv[:, 24:32, r], ["modc"], ["ab1"])
                TT("dve", gcv[:, 1, r, :], mv[:, 40:48, r], VEC[:, 24:32], ALU.mult, ["modc", vecr], [("gc", l % 2, 1)])
            yield

        def grows(w, l):
            gcv = GCS[l % 2][:].rearrange("p (w r k) -> p w r k", w=2, r=2)
            for r in range(2):
                for k in range(8):
                    dg = DIAGF[:] if k % 2 == 0 else TMP[0][:, 0:128]
                    dgr = "diagf0" if k % 2 == 0 else ("tmp", 0)
                    TS(dg, IDENTF[:], gcv[:, w, r, k:k + 1], None, ALU.mult, None, ["identf", ("gc", l % 2, w)], [dgr])
                    bank = 1 + (k // 4)
                    MM(psf[bank][:, (k % 4) * 128:(k % 4 + 1) * 128], ONESF[:], dg, True, True,
                       ["const", dgr], [PSF[bank]])
                    if k % 4 == 3:
                        ACT(GB[:, r, (k // 4) * 512:(k // 4 + 1) * 512], psf[bank][:, :], AF.Copy,
                            [PSF[bank]], [("gb", r)])

        def norm_tile(t, xap, xres):
            sc = 4 * t
            ACT(JUNK[:], xap, AF.Square, [xres], ["junk", ("stat", sc)], accum=STAT[:, sc:sc + 1])
            TS(STAT[:, sc + 1:sc + 2], STAT[:, sc:sc + 1], 1.0 / D, EPS, ALU.mult, ALU.add, [("stat", sc)], [("stat", sc + 1)])
            POW(STAT[:, sc + 2:sc + 3], STAT[:, sc + 1:sc + 2], 1, [("stat", sc + 1)], [("stat", sc + 2)])
            TS(XN[:, t, :], xap, STAT[:, sc + 2:sc + 3], None, ALU.mult, None, [xres, ("stat", sc + 2)], [("xn", t)])

        def transpose_group(tiles, which, dst_res):
            r = 0 if tiles[0] < 16 else 1
            abv = AB[:].rearrange("p (w r k) -> p w r k", w=4, r=2)
            n = len(tiles)
            c0 = tiles[0] * 128
            for k in range(8):
                b = nxt("b", 2)
                for t in range(n):
                    TR(psb[b][:, t * 128:(t + 1) * 128], XN[:, t, k * 128:(k + 1) * 128], [("xn", t)], [PSB[b]])
                ACT(HT[:, k, c0:c0 + n * 128], psb[b][:, 0:n * 128], AF.Identity, [PSB[b], "ab%d" % which], [dst_res],
                    scale=abv[:, 2 * which, r, k:k + 1], bias=abv[:, 2 * which + 1, r, k:k + 1])

        GROUPS = [[0, 1, 2, 3], [4, 5, 6, 7], [8, 9, 10, 11], [12, 13, 14, 15], [16, 17]]

        def premix_gen(l, gsel):
            for g in gsel:
                tiles = GROUPS[g]
                for t, i in enumerate(tiles):
                    s = nxt("x", 3)
                    ap, res = xsrc(l, i)
                    DMA("sp", XT[s][:], ap, [res] if res else [], [("xt", s)], ("xt", s))
                    norm_tile(t, XT[s][:], ("xt", s))
                    yield
                transpose_group(tiles, 0, ("hT", g))
                yield

        def proj_gen(l):
            wv = d_win[l].rearrange("(k p) n -> p k n", p=128)
            coltiles = [(0, 512), (512, 512), (1024, 512), (1536, 512), (2048, 256)]
            blocks = [("qk", 0, 0, 2), ("qk", 2, 2, 2), ("qk", 4, 4, 1),
                      ("conf", 0, 5, 2), ("conf", 1, 7, 2), ("short", 0, 9, 3), ("short", 1, 12, 3), ("v", 0, 15, 1)]

            def load(bi):
                kind, idx, ch0, nch = blocks[bi]
                buf = bi % 2
                DMA("pool", WIN[buf][:, :, 0:nch * 128], wv[:, :, ch0 * 128:(ch0 + nch) * 128], [BIGR], [("win", buf)], ("win", buf))

            def rope_finish(item):
                fmain, tbq, dst, dres, c0, n = item
                f2 = (1 + nxt("f", 5))
                MM(psf[f2][:, 0:n], PERM[:], TMPB[tbq][:, 0:n], True, True, ["perm", ("tmpb", tbq)], [PSF[f2]])
                t1, t2 = nxt("t", 4), nxt("t", 4)
                TT("dve", TMP[t1][:, 0:n], psf[fmain][:, 0:n], ROPE[:, c0:c0 + n], ALU.mult,
                   [PSF[fmain], "rope"], [("tmp", t1)])
                TT("dve", TMP[t2][:, 0:n], psf[f2][:, 0:n], ROPE[:, SEQ + c0:SEQ + c0 + n], ALU.mult,
                   [PSF[f2], "rope"], [("tmp", t2)])
                TT("pool", dst, TMP[t1][:, 0:n], TMP[t2][:, 0:n], ALU.add, [("tmp", t1), ("tmp", t2)], [dres])
            load(0)
            for bi, (kind, idx, ch0, nch) in enumerate(blocks):
                if bi + 1 < len(blocks):
                    load(bi + 1)
                buf = bi % 2
                W = WIN[buf]
                wres = ("win", buf)
                if kind == "v":
                    for i in range(NT):
                        f = (1 + nxt("f", 5))
                        for k in range(8):
                            MM(psf[f][:, 0:128], HT[:, k, i * 128:(i + 1) * 128], W[:, k, 0:128], k == 0, k == 7,
                               [("hT", min(i // 4, 4)), wres], [PSF[f]])
                        CP("dve", VT[:, i, 0:64], psf[f][:, 0:64], [PSF[f]], [("V", i)])
                        ACT(VT[:, i, 192:256], psf[f][:, 64:128], AF.Copy, [PSF[f]], [("V", i)])
                        if i % 3 == 2:
                            yield
                    continue
                if kind == "qk":
                    pend = None
                    for ci in range(nch):
                        chunk = idx + ci
                        isk = chunk == 4
                        for g, (c0, n) in enumerate(coltiles):
                            isctx = g == 4
                            if isctx and l == DEPTH - 1 and not isk:
                                continue
                            f = (1 + nxt("f", 5))
                            for k in range(8):
                                MM(psf[f][:, 0:n], W[:, k, ci * 128:(ci + 1) * 128], HT[:, k, c0:c0 + n], k == 0, k == 7,
                                   [("hT", g), wres], [PSF[f]])
                            dst = KT[:, c0:c0 + n] if isk else QT[:, chunk, c0:c0 + n]
                            dres = ("kT", g) if isk else ("qT", chunk, g)
                            if isctx:
                                ACT(dst, psf[f][:, 0:n], AF.Copy, [PSF[f]], [dres])
                            else:
                                tbq = nxt("tb", 3)
                                ACT(TMPB[tbq][:, 0:n], psf[f][:, 0:n], AF.Copy, [PSF[f]], [("tmpb", tbq)])
                                if pend is not None:
                                    rope_finish(pend)
                                pend = (f, tbq, dst, dres, c0, n)
                            yield
                    if pend is not None:
                        rope_finish(pend)
                    continue
                for g, (c0, n) in enumerate(coltiles):
                    isctx = g == 4
                    if isctx and l == DEPTH - 1:
                        continue
                    banks = []
                    for ci in range(nch):
                        f = (1 + nxt("f", 5))
                        banks.append(f)
                        for k in range(8):
                            MM(psf[f][:, 0:n], W[:, k, ci * 128:(ci + 1) * 128], HT[:, k, c0:c0 + n], k == 0, k == 7,
                               [("hT", g), wres], [PSF[f]])
                    if kind == "conf":
                        t1 = nxt("t", 4)
                        ACT(TMP[t1][:, 0:n], psf[banks[1]][:, 0:n], AF.Sigmoid, [PSF[banks[1]]], [("tmp", t1)])
                        off = 15 + c0 if not isctx else 2093
                        TT("dve", UC[:, idx, off:off + n], psf[banks[0]][:, 0:n], TMP[t1][:, 0:n], ALU.mult,
                           [PSF[banks[0]], ("tmp", t1)], [("uc", g)])
                    elif kind == "short":
                        t1 = nxt("t", 4)
                        ACT(SBS[:, idx, c0:c0 + n], psf[banks[0]][:, 0:n], AF.Copy, [PSF[banks[0]]], [("sbs", g)])
                        ACT(TMP[t1][:, 0:n], psf[banks[2]][:, 0:n], AF.Copy, [PSF[banks[2]]], [("tmp", t1)])
                        off = 1 + c0 if not isctx else 2051
                        TT("dve", PP[:, idx, off:off + n], psf[banks[1]][:, 0:n], TMP[t1][:, 0:n], ALU.mult,
                           [PSF[banks[1]], ("tmp", t1)], [("pp", g)])
                    yield

        def stat_rstd(psS, n, N, rd, tmpi):
            ACT(TMP[tmpi][:, 0:N], psS[:, 0:N], AF.Sqrt, list(rd) + ["const"], [("tmp", tmpi)], scale=1.0 / n, bias=EPSC[:, 0:1])
            RECIP(TMP[tmpi][:, 0:N], TMP[tmpi][:, 0:N], [("tmp", tmpi)], [("tmp", tmpi)])

        def tgeom(T):
            isctx = T == 4
            N = 256 if isctx else 512
            q0 = 2048 if isctx else T * 512
            return isctx, N, q0

        def gnorm_gen(T, ch0, nch, bank, tmps):
            isctx, N, q0 = tgeom(T)
            zb = T % 2
            Z = ZT[zb]
            zr = lambda ch: ("zT", zb, ch)
            for ci in range(nch):
                tb = nxt("tb", 3)
                ACT(TMPB[tb][:, 0:N], Z[:, ch0 + ci, 0:N], AF.Square, [zr(ch0 + ci)], [("tmpb", tb)])
                MM(psf[bank][:, 0:N], ONESB[:, 0:128], TMPB[tb][:, 0:N], ci == 0, ci == nch - 1, ["const", ("tmpb", tb)], [PSF[bank]])
                yield
            tm_, tmr = tmps
            ACT(tm_[:, 0:N], psf[bank][:, 0:N], AF.Ln, [PSF[bank], "const"], [tmr], scale=1.0 / (nch * 128), bias=EPSC[:, 0:1])
            ACT(tm_[:, 0:N], tm_[:, 0:N], AF.Exp, [tmr], [tmr], scale=-0.5)
            yield
            for ci in range(nch):
                STT(Z[:, ch0 + ci, 0:N], Z[:, ch0 + ci, 0:N], L.VEC[:, 32 + ch0 + ci:33 + ch0 + ci], tm_[:, 0:N],
                    ALU.mult, ALU.mult, [zr(ch0 + ci), tmr, L.vecr], [zr(ch0 + ci)])
            yield

        def conf_short_gen(l, T):
            isctx, N, q0 = tgeom(T)
            zb = T % 2
            Z = ZT[zb]
            zr = lambda ch: ("zT", zb, ch)
            uoff = 2093 if isctx else 15 + q0
            cvt = [0, 1]
            tm = 2
            ucr = ["diag", ("uc", T), ("uc", max(T - 1, 0)), ("uc", min(T + 1, 3) if not isctx else 4)]
            for cc in range(2):
                for k in range(31):
                    MM(psf[6 + cc][:, 0:N], DIAG[:, cc * 31 + k, :], UC[:, cc, uoff - 15 + k:uoff - 15 + k + N], k == 0, k == 30,
                       ucr, [PSF[6 + cc]])
                    if k % 8 == 7:
                        yield
                ACT(TMP[cvt[cc]][:, 0:N], psf[6 + cc][:, 0:N], AF.Identity, [PSF[6 + cc], L.vecr], [("tmp", cvt[cc])], bias=L.VEC[:, 40 + cc:41 + cc])
                yield
            hb = [nxt("tb", 3), nxt("tb", 3)]
            for cc in range(2):
                CP("dve", TMPB[hb[cc]][:, 0:N], TMP[cvt[cc]][:, 0:N], [("tmp", cvt[cc])], [("tmpb", hb[cc])])
            yield
            for cc in range(2):
                MM(psf[6][:, 0:N], ONESB[:, 0:128], TMPB[hb[cc]][:, 0:N], cc == 0, cc == 1, ["const", ("tmpb", hb[cc])], [PSF[6]])
            yield
            TS(TMP[tm][:, 0:N], psf[6][:, 0:N], 1.0 / 256, None, ALU.mult, None, [PSF[6]], [("tmp", tm)])
            yield
            for cc in range(2):
                TT("dve", TMP[cvt[cc]][:, 0:N], TMP[cvt[cc]][:, 0:N], TMP[tm][:, 0:N], ALU.subtract,
                   [("tmp", cvt[cc]), ("tmp", tm)], [("tmp", cvt[cc])])
                ACT(TMPB[hb[cc]][:, 0:N], TMP[cvt[cc]][:, 0:N], AF.Square, [("tmp", cvt[cc])], [("tmpb", hb[cc])])
                yield
            for cc in range(2):
                MM(psf[7][:, 0:N], ONESB[:, 0:128], TMPB[hb[cc]][:, 0:N], cc == 0, cc == 1, ["const", ("tmpb", hb[cc])], [PSF[7]])
            yield
            ACT(TMP[tm][:, 0:N], psf[7][:, 0:N], AF.Ln, [PSF[7], "const"], [("tmp", tm)], scale=1.0 / 256, bias=EPSC[:, 0:1])
            ACT(TMP[tm][:, 0:N], TMP[tm][:, 0:N], AF.Exp, [("tmp", tm)], [("tmp", tm)], scale=-0.5)
            yield
            for cc in range(2):
                TT("dve", TMP[cvt[cc]][:, 0:N], TMP[cvt[cc]][:, 0:N], TMP[tm][:, 0:N], ALU.mult,
                   [("tmp", cvt[cc]), ("tmp", tm)], [("tmp", cvt[cc])])
            yield
            for cc in range(2):
                ACT(TMP[tm][:, 0:N], TMP[cvt[cc]][:, 0:N], AF.Exp, [("tmp", cvt[cc]), "nln"], [("tmp", tm)],
                    scale=NLN[:, cc:cc + 1], bias=NLN[:, 2 + cc:3 + cc])
                ACT(TMP[tm][:, 0:N], TMP[tm][:, 0:N], AF.Ln, [("tmp", tm), "const"], [("tmp", tm)], bias=ONESF[:, 0:1])
                ACT(TMP[tm][:, 0:N], TMP[tm][:, 0:N], AF.Exp, [("tmp", tm)], [("tmp", tm)], scale=-1.0)
                TS(TMP[cvt[cc]][:, 0:N], TMP[cvt[cc]][:, 0:N], L.VEC[:, 42 + cc:43 + cc], L.VEC[:, 44 + cc:45 + cc], ALU.mult, ALU.add,
                   [("tmp", cvt[cc]), L.vecr], [("tmp", cvt[cc])])
                TT("dve", Z[:, 4 + cc, 0:N], TMP[cvt[cc]][:, 0:N], TMP[tm][:, 0:N], ALU.mult,
                   [("tmp", cvt[cc]), ("tmp", tm)], [zr(4 + cc)])
                yield
            yield from gnorm_gen(T, 4, 2, 6, (TMP[tm], ("tmp", tm)))
            poff = 2051 if isctx else 1 + q0
            for cc in range(2):
                t1 = cc
                prd = [("pp", T), ("pp", max(T - 1, 0)), ("pp", min(T + 1, 3) if not isctx else 4), L.vecr]
                TS(TMP[t1][:, 0:N], PP[:, cc, poff - 1:poff - 1 + N], L.VEC[:, 46 + cc * 3:47 + cc * 3], None, ALU.mult, None,
                   prd, [("tmp", t1)])
                for k in (1, 2):
                    STT(TMP[t1][:, 0:N], PP[:, cc, poff - 1 + k:poff - 1 + k + N], L.VEC[:, 46 + cc * 3 + k:47 + cc * 3 + k],
                        TMP[t1][:, 0:N], ALU.mult, ALU.add, prd + [("tmp", t1)], [("tmp", t1)])
                yield
                TT("dve", Z[:, 6 + cc, 0:N], SBS[:, cc, (2048 if isctx else q0):(2048 if isctx else q0) + N], TMP[t1][:, 0:N], ALU.mult,
                   [("sbs", T), ("tmp", t1)], [zr(6 + cc)])
                yield
            yield from gnorm_gen(T, 6, 2, 7, (TMP[tm], ("tmp", tm)))

        def attn_gen(l, T):
            isctx, N, q0 = tgeom(T)
            zb = T % 2
            Z = ZT[zb]
            zr = lambda ch: ("zT", zb, ch)
            keys = [(16, 0, N, None), (17, 0, N, None)]
            if not isctx:
                for j in range(max(0, 4 * T - 1), min(15, 4 * T + 4) + 1):
                    lo = max(4 * T, j - 1)
                    hi = min(4 * T + 3, j + 1)
                    qoff = (lo - 4 * T) * 128
                    n = (hi - lo + 1) * 128
                    moff = (lo - (j - 1)) * 128
                    needm = (lo == j - 1) or (hi == j + 1)
                    keys.append((j, qoff, n, moff if needm else None))
            nk = len(keys)
            for c in range(4):
                for ki in range(nk + 1):
                    if ki < nk:
                        j, qoff, n, moff = keys[ki]
                        kg = min(j // 4, 4)
                        for half in range(2):
                            pr = slice(64 * half, 64 * half + 64)
                            f = half
                            MM(psf[f][:, 0:n], KT[pr, j * 128:(j + 1) * 128], QT[pr, c, q0 + qoff:q0 + qoff + n], True, moff is None,
                               [("kT", kg), ("qT", c, T)], [PSF[f]])
                        for half in range(2):
                            f = half
                            pt = 2 * half + (ki % 2)
                            if moff is not None:
                                msegs = []
                                if moff == 0:
                                    msegs.append((0, 0))
                                if moff + n == 384:
                                    msegs.append((n - 128, 256))
                                for si, (pc, mc) in enumerate(msegs):
                                    MM(psf[f][:, pc:pc + 128], IDENT[:], MASK[:, mc:mc + 128], False, si == len(msegs) - 1,
                                       ["ident", "const"], [PSF[f]])
                            ACT(PT[pt][:, 0:n], psf[f][:, 0:n], AF.Exp, [PSF[f]], [("pt", pt)], scale=0.125)
                    if ki >= 1:
                        pj, pqoff, pn, _ = keys[ki - 1]
                        for half in range(2):
                            pp_ = 2 * half + ((ki - 1) % 2)
                            MM(psf[4 + half][:, pqoff:pqoff + pn], VT[:, pj, half * 128:(half + 1) * 128], PT[pp_][:, 0:pn], ki == 1, ki == nk,
                               [("V", pj), "vones", ("pt", pp_)], [PSF[4 + half]])
                    yield
                for half in range(2):
                    h = c + 4 * half
                    pr = slice(64 * half, 64 * half + 64)
                    dn = slice(64 * (1 - half), 64 * (1 - half) + 64)
                    psO = psf[4 + half]
                    at = ATMP[half]
                    ACT(at[pr, 0:N], psO[dn, 0:N], AF.Ln, [PSF[4 + half], "esc"], [("atmp", half)], bias=ESC[dn, h:h + 1])
                    ACT(at[pr, 0:N], at[pr, 0:N], AF.Exp, [("atmp", half)], [("atmp", half)], scale=-1.0)
                    TT("dve", Z[pr, c, 0:N], psO[pr, 0:N], at[pr, 0:N], ALU.mult, [PSF[4 + half], ("atmp", half)], [zr(c)])
                yield

        def tail_gen(l, T):
            isctx, N, q0 = tgeom(T)
            zb = T % 2
            Z = ZT[zb]
            zr = lambda ch: ("zT", zb, ch)
            yield from gnorm_gen(T, 0, 4, 6, (TMP[3], ("tmp", 3)))
            ntl = N // 128
            tiles = [q0 // 128 + t for t in range(ntl)]
            for t, i in enumerate(tiles):
                for hf in range(2):
                    for k in range(8):
                        MM(psf[2 + hf][:, :], Z[:, k, t * 128:(t + 1) * 128], WOUT[:, k, hf * 512:(hf + 1) * 512], k == 0, k == 7,
                           [zr(k), "wout"], [PSF[2 + hf]])
                    yield
                s = nxt("x", 3)
                postnorm_tile(l, i, s, 0, yb=2)
                yield
                norm_tile(t, XT[s][:], ("xt", s))
                yield
            r = 0 if tiles[0] < 16 else 1
            abv = AB[:].rearrange("p (w r k) -> p w r k", w=4, r=2)
            n = len(tiles)
            c0 = tiles[0] * 128
            for k in range(8):
                b = nxt("b", 2)
                for t in range(n):
                    TR(psb[b][:, t * 128:(t + 1) * 128], XN[:, t, k * 128:(k + 1) * 128], [("xn", t)], [PSB[b]])
                ACT(HT[:, k, c0:c0 + n * 128], psb[b][:, 0:n * 128], AF.Identity, [PSB[b], "ab1"], [("hT", T)],
                    scale=abv[:, 2, r, k:k + 1], bias=abv[:, 3, r, k:k + 1])
                yield

        def chain(*gens):
            for g in gens:
                if g is not None:
                    yield from g

        def run_merged(gens):
            gens = [g for g in gens if g is not None]
            while gens:
                for g in list(gens):
                    try:
                        next(g)
                    except StopIteration:
                        gens.remove(g)

        def run_balanced(gens):
            st_ = [[g, 0, float(n)] for (g, n) in gens if g is not None]
            while st_:
                st_.sort(key=lambda x: x[1] / x[2])
                g = st_[0]
                try:
                    next(g[0])
                    g[1] += 1
                except StopIteration:
                    st_.remove(g)

        def mixer_all(l, nT, kstop=99):
            run_merged([conf_short_gen(l, 0)])
            run_merged([attn_gen(l, 0), conf_short_gen(l, 1)])
            for T in range(nT):
                if T == nT - 1:
                    return tail_gen(l, T)
                A = attn_gen(l, T + 1) if T + 1 < nT else None
                hasC = T + 2 < nT
                B = chain(tail_gen(l, T), conf_short_gen(l, T + 2) if hasC else None)
                run_balanced([(A, 40 if T + 1 < 4 else 12), (B, 60 if hasC else 30)])

        def postnorm_tile(l, i, s, w, final=False, yb=4):
            r = 0 if i < 16 else 1
            sc = 16 + 4 * (i % 4)
            if w == 0:
                ap, res = xsrc(l, i)
            else:
                ap, res = d_xs[i * 128:(i + 1) * 128, :], ("dxs", i)
            DMA("sp", XT[s][:], ap, [res] if res else [], [("xt", s)], ("xt", s))
            for hf in range(2):
                ACT(JUNK[:, 0:512], psf[yb + hf][:, :], AF.Square, [PSF[yb + hf]], ["junk", ("stat", sc + hf)],
                    accum=STAT[:, sc + hf:sc + hf + 1])
            TT("dve", STAT[:, sc + 2:sc + 3], STAT[:, sc:sc + 1], STAT[:, sc + 1:sc + 2], ALU.add,
               [("stat", sc), ("stat", sc + 1)], [("stat", sc + 2)])
            TS(STAT[:, sc + 2:sc + 3], STAT[:, sc + 2:sc + 3], 1.0 / D, EPS, ALU.mult, ALU.add, [("stat", sc + 2)], [("stat", sc + 2)])
            POW(STAT[:, sc + 3:sc + 4], STAT[:, sc + 2:sc + 3], 1, [("stat", sc + 2)], [("stat", sc + 3)])
            for hf in range(2):
                t1 = nxt("t", 4)
                STT(TMP[t1][:, :], psf[yb + hf][:, :], STAT[:, sc + 3:sc + 4], GB[:, r, hf * 512:(hf + 1) * 512],
                    ALU.mult, ALU.mult, [PSF[yb + hf], ("stat", sc + 3), ("gb", r)], [("tmp", t1)])
                TT("pool", XT[s][:, hf * 512:(hf + 1) * 512], XT[s][:, hf * 512:(hf + 1) * 512], TMP[t1][:, :], ALU.add,
                   [("xt", s), ("tmp", t1)], [("xt", s)])
            if final:
                if i < 16:
                    op = DMA("sp", d_out[i * 128:(i + 1) * 128, :], XT[s][:], [("xt", s)], [("dout", i)], ("xt", s))
                    out_dmas.append(op)
            else:
                DMA("sp", d_xs[i * 128:(i + 1) * 128, :], XT[s][:], [("xt", s)], [("dxs", i)], ("xt", s))

        def mixer_setup(l):
            TS(NLN[:, 0:4], L.VEC[:, 42:46], -1.0, None, ALU.mult, None, [L.vecr], ["nln"])
            DMA("pool", ROPE, d_rope, [BIGR], ["rope"], "rope")
            MEMSET("pool", UC[:, :, :], 0.0, [BIGR, ("uc", 0), ("uc", 1), ("uc", 2), ("uc", 3), ("uc", 4)])
            MEMSET("pool", PP[:, :, :], 0.0, [BIGR, ("pp", 0), ("pp", 1), ("pp", 2), ("pp", 3), ("pp", 4)])
            MEMSET("pool", VT[:, :, 64:192], 1.0, ["vones"])
            for cc in range(2):
                for k in range(31):
                    TS(DIAG[:, cc * 31 + k, :], IDENT[:], L.VEC[:, 52 + cc * 31 + k:53 + cc * 31 + k], None, ALU.mult, None,
                       ["ident", L.vecr], ["diag"])

        def load_wout(l):
            wv = d_wout[l].rearrange("(k p) n -> p k n", p=128)
            DMA("pool", WOUT[:, :, :], wv, [("win", 0), ("win", 1)], ["wout", ("win", 0), ("win", 1)], "wout")

        def ffn_geom(l, hi_):
            last = l == DEPTH - 1
            ta, tb_ = [(0, 9), (9, 16 if last else 18)][hi_]
            tok_lo, tok_hi = ta * 128, min(tb_, 16) * 128
            segs = []
            ulo = max(tok_lo - 1, 0)
            uhi = min(tok_hi + 1, SEQ)
            for (c0, n) in split_cols(ulo, uhi):
                segs.append((c0, n, c0 - tok_lo + 1))
            nx = tok_hi - tok_lo
            ctxcol = None
            if tb_ > 16:
                ctxcol = nx + 3
                segs.append((2048, 256, ctxcol))
            rowlen = (ctxcol + 256 + 1) if ctxcol is not None else nx + 2
            return ta, tb_, segs, nx, ctxcol, rowlen

        FF_EARLY = ["diag", "vones", "rope", ("atmp", 0), ("atmp", 1)] + [("pt", i) for i in range(4)] + \
                   [("uc", g) for g in range(5)] + [("pp", g) for g in range(5)] + \
                   [("sbs", g) for g in range(5)] + [("kT", g) for g in range(5)] + [("V", i) for i in range(NT)] + \
                   [("qT", c, g) for c in range(4) for g in range(5)]
        FF_LATE = [BIGR, "wout", ("win", 0), ("win", 1)] + [("zT", i, ch) for i in range(2) for ch in range(8)]

        def ffn_fence_a():
            S.add("pool", lambda e: e.memset(STAT[:, 60:61], 0.0), FF_EARLY, ["ffnfence"] + FF_EARLY)

        def ffn_fence_b():
            S.add("pool", lambda e: e.memset(STAT[:, 62:63], 0.0), FF_LATE + ["ffnfence"], ["ffnfenceB"] + FF_LATE)

        def ffn_up_gen(l, hi_, banks=(0, 1, 2, 3), order=None):
            wuv = d_wup[l].rearrange("(k p) n -> p k n", p=128)
            ta, tb_, segs, nx, ctxcol, rowlen = ffn_geom(l, hi_)
            for i in range(4):
                S.add("pool", lambda e, a=UROW[i]: e.memset(a[:, :], 0.0), ["ffnfence"], [("urow", i)])

            order = list(range(NFF)) if order is None else list(order)

            def load_wup(p):
                b_ = p % 3
                jj_ = order[p]
                DMA("pool", WUP[b_][:, :, :], wuv[:, :, jj_ * 256:(jj_ + 1) * 256], ["ffnfence"], [("wup", b_)], ("wup", b_))
            load_wup(0)
            load_wup(1)
            for p, j in enumerate(order):
                buf = p % 3
                if p + 2 < NFF:
                    load_wup(p + 2)
                if hi_ == 0 and p in (1, 3):
                    wdv = d_wdn[l].rearrange("(k p) n -> p k n", p=128)
                    h0 = 0 if p == 1 else 11
                    DMA("pool", WDN[:, h0:h0 + 11, :], wdv[:, h0:h0 + 11, :], ["ffnfence"], ["wdn"], "wdn%d" % (p // 2))
                ub = (p % 2) * 2
                L_ = rowlen - 2
                for gv in range(2):
                    ur = ub + gv
                    for (c0, n, col) in segs:
                        f = banks[nxt("f", 4)]
                        gs = sorted(set([min(c0 // 512, 4), min((c0 + n - 1) // 512, 4)]))
                        for k in range(8):
                            MM(psf[f][:, 0:n], WUP[buf][:, k, gv * 128:(gv + 1) * 128], HT[:, k, c0:c0 + n], k == 0, k == 7,
                               [("hT", g) for g in gs] + [("wup", buf)], [PSF[f]])
                        ACT(UROW[ur][:, col:col + n], psf[f][:, 0:n], AF.Copy, [PSF[f]], [("urow", ur)])
                    wc = 114 + (gv * NFF + j) * 3
                    TS(CROW[ur][:, 1:1 + L_], UROW[ur][:, 0:L_], L.VEC[:, wc:wc + 1], None, ALU.mult, None,
                       [("urow", ur), L.vecr], [("crow", ur)])
                    for k in (1, 2):
                        STT(CROW[ur][:, 1:1 + L_], UROW[ur][:, k:k + L_], L.VEC[:, wc + k:wc + k + 1], CROW[ur][:, 1:1 + L_],
                            ALU.mult, ALU.add, [("urow", ur), ("crow", ur), L.vecr], [("crow", ur)])
                    yield
                ACT(CROW[ub][:, 1:1 + L_], CROW[ub][:, 1:1 + L_], AF.Silu, [("crow", ub)], [("crow", ub)])
                TT("pool", GT[:, j, 0:nx], CROW[ub][:, 1:1 + nx], CROW[ub + 1][:, 1:1 + nx], ALU.mult,
                   [("crow", ub), ("crow", ub + 1)] + (["ffnfenceB"] if j < 15 else []), [("gT", j)])
                if ctxcol is not None:
                    TT("pool", GT[:, j, nx:nx + 256], CROW[ub][:, ctxcol:ctxcol + 256], CROW[ub + 1][:, ctxcol:ctxcol + 256], ALU.mult,
                       [("crow", ub), ("crow", ub + 1)] + (["ffnfenceB"] if j < 15 else []), [("gT", j)])
                yield

        def ffn_down(l, hi_):
            last = l == DEPTH - 1
            ta, tb_, segs, nx, ctxcol, rowlen = ffn_geom(l, hi_)
            for t, i in enumerate(range(ta, tb_)):
                yb = 4 if t % 2 == 0 else 2
                for hf in range(2):
                    for k in range(NFF):
                        MM(psf[yb + hf][:, :], GT[:, k, t * 128:(t + 1) * 128], WDN[:, k, hf * 512:(hf + 1) * 512], k == 0, k == NFF - 1,
                           [("gT", k), "wdn"], [PSF[yb + hf]])
                s = nxt("x", 3)
                postnorm_tile(l, i, s, 1, final=last, yb=yb)
                g = i // 4
                if not last and g != 2:
                    norm_tile(i % 4, XT[s][:], ("xt", s))
                    if i == GROUPS[g][-1]:
                        transpose_group(GROUPS[g], 0, ("hT", g))

        def ffn_end(l):
            ffnres = ["wdn"] + [("wup", i) for i in range(3)] + [("urow", i) for i in range(4)] + \
                     [("crow", i) for i in range(4)] + [("gT", j) for j in range(NFF)]
            S.add("pool", lambda e: e.memset(STAT[:, 61:62], 0.0), ffnres + [BIGR], ffnres + [BIGR])

        kstop = int(os.environ.get("KSTOP", "99"))
        XNW = [XN[:, 0:2, :].rearrange("p a b -> p (a b)").rearrange("p (k n) -> p k n", n=256),
               XN[:, 2:4, :].rearrange("p a b -> p (a b)").rearrange("p (k n) -> p k n", n=256)]
        XNWR = [[("xn", 0), ("xn", 1)], [("xn", 2), ("xn", 3)]]
        for l in range(DEPTH):
            S.epoch = l
            L.VEC = VECS[l % 2]
            L.vecr = ("vec", l % 2)
            if l == 0:
                mg = mod_gen(0, [ZT[0], ZT[1]], [[("zT", 0, ch) for ch in range(8)], [("zT", 1, ch) for ch in range(8)]], 0, 512)
                import itertools
                for _ in range(5):
                    next(mg)
                run_merged([itertools.islice(mg, 3), premix_gen(0, [0, 1, 2, 3, 4])])

            else:
                run_merged([premix_gen(l, [2])])
            if kstop <= 0:
                break
            grows(0, l)
            if kstop <= 1:
                break
            mixer_setup(l)
            if l == 0:
                run_balanced([(proj_gen(l), 70), (mg, 7)])
            else:
                run_merged([proj_gen(l)])
            load_wout(l)
            if kstop <= 2:
                break
            last_tail = mixer_all(l, 5 if l < DEPTH - 1 else 4, kstop)
            if kstop <= 4:
                run_merged([last_tail])
                break
            ffn_fence_a()
            import itertools
            up0 = ffn_up_gen(l, 0, banks=(0, 1, 4, 5), order=list(range(15, NFF)) + list(range(15)))
            run_balanced([(last_tail, 30), (itertools.islice(up0, 21), 21)])
            ffn_fence_b()
            run_balanced([(mod_gen(l + 1, XNW, XNWR, 6, 256) if l + 1 < DEPTH else None, 26), (up0, 45)])
            grows(1, l)
            up1 = ffn_up_gen(l, 1)
            next(up1)
            ffn_down(l, 0)
            run_merged([up1])
            ffn_down(l, 1)
            ffn_end(l)
            if kstop <= 5:
                break
        S.emit(final_waits=out_dmas)
    return nc


def rope_tables():
    t = np.arange(SEQ)
    pos = np.stack([t // 64, t % 64], 0).astype(np.float32)
    inv = (10000.0 ** (-np.arange(16, dtype=np.float32) / 16)).astype(np.float32)
    cos = np.zeros((128, SEQ), np.float32)
    sin = np.zeros((128, SEQ), np.float32)
    for p in range(128):
        d = p % 64
        r, idx = d // 32, d % 32
        ang = pos[r] * inv[idx % 16]
        cos[p] = np.cos(ang)
        sin[p] = np.sin(ang) * (-1.0 if idx < 16 else 1.0)
    return np.concatenate([cos, sin], 1)


def col_layout(v):
    return np.ascontiguousarray(v.reshape(-1, 128).T)


def rot_partner(d):
    r, idx = d // 32, d % 32
    return r * 32 + (idx + 16 if idx < 16 else idx - 16)


def win_perm():
    out = []
    for c in range(4):
        out.append(np.concatenate([np.arange(c * 64, c * 64 + 64), np.arange((c + 4) * 64, (c + 4) * 64 + 64)]))
    out.append(512 + np.arange(128))
    cv0, cg0, sb0, scg0, su0 = 768, 1024, 1280, 1536, 1792
    for cc in range(2):
        out += [cv0 + cc * 128 + np.arange(128), cg0 + cc * 128 + np.arange(128)]
    for cc in range(2):
        out += [sb0 + cc * 128 + np.arange(128), scg0 + cc * 128 + np.arange(128), su0 + cc * 128 + np.arange(128)]
    out += [640 + np.arange(128)]
    return np.concatenate(out)


def perm_matrix():
    pm = np.zeros((128, 128), np.float32)
    for m in range(128):
        k = (m // 64) * 64 + rot_partner(m % 64)
        pm[k, m] = 1.0
    return pm


def attn_chan_perm():
    idx = []
    for c in range(4):
        idx += list(range(c * 64, c * 64 + 64)) + list(range((c + 4) * 64, (c + 4) * 64 + 64))
    idx += list(range(512, 1024))
    return np.array(idx)


_CACHE = {}


def prepare_inputs(inputs):
    f = lambda a: np.ascontiguousarray(np.asarray(a, dtype=np.float32))
    x, c, ctx, c_ctx = f(inputs["x"]), f(inputs["c"]), f(inputs["ctx"]), f(inputs["c_ctx"])
    rope = rope_tables()
    kk = np.arange(128)[:, None]
    qq = np.arange(128)[None, :]
    mask = np.zeros((128, 384), np.float32)
    mask[:, 0:128] = np.where(kk <= qq, 0.0, -30000.0)
    mask[:, 256:384] = np.where(qq <= kk, 0.0, -30000.0)
    perm = win_perm()
    zperm = attn_chan_perm()
    shared = {"rope": rope, "mask": mask, "perm": perm_matrix()}
    for l in range(DEPTH):
        shared["wada%d" % l] = f(inputs["w_ada"][l])
        vec = np.zeros((128, NV), np.float32)
        vec[:, 0:8] = col_layout(f(inputs["g_pre_mix"][l]))
        vec[:, 8:16] = col_layout(f(inputs["g_post_mix"][l]))
        vec[:, 16:24] = col_layout(f(inputs["g_pre_ffn"][l]))
        vec[:, 24:32] = col_layout(f(inputs["g_post_ffn"][l]))
        vec[:, 32:40] = col_layout(f(inputs["g_group"][l])[zperm])
        vec[:, 40:42] = col_layout(f(inputs["b_conf_dw"][l]))
        vec[:, 42:44] = col_layout(f(inputs["conf_ln_g"][l]))
        vec[:, 44:46] = col_layout(f(inputs["conf_ln_b"][l]))
        wsc = f(inputs["w_sc_dw"][l])
        wcf = f(inputs["w_conf_dw"][l])
        wff = f(inputs["w_ffn_dw"][l])
        for cc in range(2):
            vec[:, 46 + cc * 3:49 + cc * 3] = wsc[:, cc * 128:(cc + 1) * 128].T
            vec[:, 52 + cc * 31:83 + cc * 31] = wcf[:, cc * 128:(cc + 1) * 128].T
        for ch in range(44):
            vec[:, 114 + ch * 3:117 + ch * 3] = wff[:, ch * 128:(ch + 1) * 128].T
        ba = col_layout(f(inputs["b_ada"][l]))
        vec[:, 246:342] = np.repeat(ba, 2, axis=1)
        shared["vec%d" % l] = vec
        shared["win%d" % l] = np.ascontiguousarray(f(inputs["w_in"][l])[:, perm])
        shared["sink%d" % l] = np.ascontiguousarray(np.broadcast_to(f(inputs["sink"][l])[None, :], (128, 8)))
        shared["wout%d" % l] = np.ascontiguousarray(f(inputs["w_out"][l])[zperm, :])
        wu = f(inputs["w_up"][l])
        shared["wup%d" % l] = np.ascontiguousarray(
            np.stack([wu[:, :DFF].reshape(D, NFF, 128), wu[:, DFF:].reshape(D, NFF, 128)], axis=2).reshape(D, 2 * DFF))
        shared["wdn%d" % l] = f(inputs["w_down"][l])
    in_maps = []
    for b in range(8):
        m = dict(shared)
        m["x"] = x[b]
        m["ctx"] = ctx[b]
        cc = np.zeros((128, 16), np.float32)
        cc[:, 0::2] = col_layout(c[b])
        cc[:, 1::2] = col_layout(c_ctx)
        m["cc"] = cc
        in_maps.append(m)
    return in_maps


def kernel(**inputs):
    in_maps = prepare_inputs(inputs)
    if "nc" not in _CACHE:
        _CACHE["nc"] = build_program()
    nc = _CACHE["nc"]
    res = run_bass_kernel_spmd(nc, in_maps, core_ids=list(range(8)))
    out = np.stack([np.asarray(r["out"], dtype=np.float32) for r in res.results], 0)
    return out
```

```python
import contextlib
import os
import numpy as np
import ml_dtypes
import concourse.bass as bass
import concourse.mybir as mybir
from concourse.bass_utils import run_bass_kernel_spmd

F32 = mybir.dt.float32
BF16 = mybir.dt.bfloat16
AF = mybir.ActivationFunctionType
ALU = mybir.AluOpType

D = 1024
SEQ = 2048
CTX = 256
NT = 18
NTOK = NT * 128
DEPTH = 2
DFF = 2816
NFF = 22
EPS = 1e-6
NCW = 16
NV = 342


class Op:
    __slots__ = ("eng", "fn", "deps", "sem", "ticket", "signal", "ninc", "name")


class Sched:
    ENGS = ("pe", "act", "dve", "pool", "sp")

    def __init__(self, nc):
        self.nc = nc
        self.ops = {e: [] for e in self.ENGS}
        self.lastw = {}
        self.readers = {}
        self.dma_count = {}
        self.all_ops = []
        self.epoch = 0

    def add(self, eng, fn, reads=(), writes=(), dma_key=None, ndma=1, name=""):
        op = Op()
        op.eng, op.fn, op.name = eng, fn, name
        op.signal = False
        deps = set()
        excl = [r for r in reads if isinstance(r, str) and r.startswith("ps:")]
        reads = [r for r in reads if r not in excl]
        writes = list(writes) + excl
        for r in reads:
            lw = self.lastw.get(r)
            if lw is not None:
                deps.add(lw)
        for w in writes:
            lw = self.lastw.get(w)
            if lw is not None:
                deps.add(lw)
            lastrd = {}
            for rd in self.readers.get(w, ()):
                if rd.sem[0] == "dma":
                    deps.add(rd)
                else:
                    lastrd[rd.eng] = rd
            deps.update(lastrd.values())
        for r in reads:
            self.readers.setdefault(r, []).append(op)
        for w in writes:
            self.lastw[w] = op
            self.readers[w] = []
        deps.discard(op)
        if eng == "pe":
            deps = set(d for d in deps if d.eng != "pe")
        op.deps = deps
        if dma_key is not None:
            op.sem = ("dma", dma_key)
            self.dma_count[dma_key] = self.dma_count.get(dma_key, 0) + 16 * ndma
            op.ticket = self.dma_count[dma_key]
            op.ninc = ndma
            op.signal = True
        else:
            op.sem = ("eng", eng, self.epoch)
            op.ticket = None
            op.ninc = 0
        for d in deps:
            d.signal = True
        self.ops[eng].append(op)
        self.all_ops.append(op)
        return op

    def emit(self, final_waits=()):
        nc = self.nc
        cnts = {}
        for e in self.ENGS:
            for op in self.ops[e]:
                if op.sem[0] == "eng" and op.signal:
                    cnts[op.sem] = cnts.get(op.sem, 0) + 1
                    op.ticket = cnts[op.sem]
        with contextlib.ExitStack() as st:
            sems = {}
            for i, k in enumerate(cnts):
                sems[k] = st.enter_context(nc.semaphore("s%d" % i))
            for i, k in enumerate(self.dma_count):
                sems[("dma", k)] = st.enter_context(nc.semaphore("d%d" % i))
            block = st.enter_context(nc.Block())

            def run(engname, eng):
                waited = {}
                for op in self.ops[engname]:
                    need = {}
                    for d in op.deps:
                        if need.get(d.sem, 0) < d.ticket:
                            need[d.sem] = d.ticket
                    for s, v in need.items():
                        if waited.get(s, 0) < v:
                            eng.wait_ge(sems[s], v)
                            waited[s] = v
                    ins = op.fn(eng)
                    if op.sem[0] == "dma":
                        if not isinstance(ins, (list, tuple)):
                            ins = [ins]
                        assert len(ins) == op.ninc, (op.name, len(ins), op.ninc)
                        for i in ins:
                            i.then_inc(sems[op.sem], 16)
                    elif op.signal:
                        ins.then_inc(sems[op.sem], 1)
                if engname == "sp":
                    for op in final_waits:
                        eng.wait_ge(sems[op.sem], op.ticket)

            block.tensor(lambda eng: run("pe", eng))
            block.scalar(lambda eng: run("act", eng))
            block.vector(lambda eng: run("dve", eng))
            block.gpsimd(lambda eng: run("pool", eng))
            block.sync(lambda eng: run("sp", eng))


def split_cols(lo, hi, mx=512):
    out = []
    while lo < hi:
        n = min(mx, hi - lo)
        out.append((lo, n))
        lo += n
    return out


def build_program(stop_after=None):
    nc = bass.Bass("TRN2", target_bir_lowering=False)
    dram = lambda name, shape, dt=F32, kind="ExternalInput": nc.dram_tensor(name, shape, dt, kind=kind).ap()
    d_x = dram("x", [SEQ, D])
    d_ctx = dram("ctx", [CTX, D])
    d_cc = dram("cc", [128, 16])
    d_rope = dram("rope", [128, 2 * SEQ])
    d_mask = dram("mask", [128, 384])
    d_perm = dram("perm", [128, 128])
    d_wada = [dram("wada%d" % l, [D, 6 * D]) for l in range(DEPTH)]
    d_vec = [dram("vec%d" % l, [128, NV]) for l in range(DEPTH)]
    d_win = [dram("win%d" % l, [D, NCW * 128]) for l in range(DEPTH)]
    d_sink = [dram("sink%d" % l, [128, 8]) for l in range(DEPTH)]
    d_wout = [dram("wout%d" % l, [D, D]) for l in range(DEPTH)]
    d_wup = [dram("wup%d" % l, [D, 2 * DFF]) for l in range(DEPTH)]
    d_wdn = [dram("wdn%d" % l, [DFF, D]) for l in range(DEPTH)]
    d_out = dram("out", [SEQ, D], kind="ExternalOutput")
    d_xs = dram("xs_scratch", [NTOK, D], kind="Internal")

    st = contextlib.ExitStack()
    with st:
        sb = lambda name, shape, dt: st.enter_context(nc.sbuf_tensor(name, shape, dt))
        S = Sched(nc)

        BIG = sb("BIG", [128, 63296], BF16)
        HT = sb("HT", [128, 8, NTOK], BF16)
        GB = sb("GB", [128, 2, D], F32)
        XT = [sb("XT%d" % i, [128, D], F32) for i in range(3)]
        XN = sb("XN", [128, 4, D], BF16)
        JUNK = sb("JUNK", [128, D], BF16)
        TMP = [sb("TMP%d" % i, [128, 512], F32) for i in range(4)]
        TMPB = [sb("TMPB%d" % i, [128, 512], BF16) for i in range(3)]
        IDENT = sb("IDENT", [128, 128], BF16)
        IDENTF = sb("IDENTF", [128, 128], F32)
        ONESF = sb("ONESF", [128, 128], F32)
        ONESB = sb("ONESB", [128, 128], BF16)
        NHALF = sb("NHALF", [128, 8], F32)
        MASK = sb("MASK", [128, 384], BF16)
        PERM = sb("PERM", [128, 128], BF16)
        VECS = [sb("VEC%d" % i, [128, NV], F32) for i in range(2)]

        class L:
            VEC = None
            vecr = None
        CCF = sb("CCF", [128, 16], F32)
        CCB = sb("CCB", [128, 8, 2], BF16)
        MODC = sb("MODC", [128, 96], F32)
        AB = sb("AB", [128, 64], F32)
        GCS = [sb("GC%d" % i, [128, 32], F32) for i in range(2)]
        DIAGF = sb("DIAGF", [128, 128], F32)
        STAT = sb("STAT", [128, 64], F32)
        ESC = sb("ESC", [128, 8], F32)
        EPSC = sb("EPSC", [128, 1], F32)
        NLN = sb("NLN", [128, 4], F32)

        psf = [st.enter_context(nc.psum_tensor("psf%d" % i, [128, 512], F32)) for i in range(8)]
        psb = [psf[6 + i][:, :].bitcast(BF16) for i in range(2)]
        PSF = ["ps:f%d" % i for i in range(8)]
        PSB = [PSF[6], PSF[7]]

        def carve(off, shape):
            n = int(np.prod(shape))
            ap = BIG[:, off:off + n]
            if len(shape) == 2:
                ap = ap.rearrange("p (a b) -> p a b", b=shape[1])
            return ap, off + n
        o = 0
        QT, o = carve(o, [4, NTOK])
        KT, o = carve(o, [1, NTOK])
        KT = BIG[:, 9216:9216 + NTOK]
        VT, o = carve(o, [NT, 256])
        UCW = 2364
        UC, o = carve(o, [2, UCW])
        PW = 2308
        PP, o = carve(o, [2, PW])
        SBS, o = carve(o, [2, NTOK])
        DIAG, o = carve(o, [62, 128])
        WBUF_OFF = o
        WIN = []
        for i in range(2):
            w, o = carve(o, [8, 384]); WIN.append(w)
        WOUT, _ = carve(WBUF_OFF, [8, D])
        o = WBUF_OFF + 8 * D
        WADA = [WIN[0], WIN[1]]
        ZT = []
        for i in range(2):
            z, o = carve(o, [8, 512]); ZT.append(z)
        PT = []
        for i in range(4):
            p_, o = carve(o, [1, 512]); PT.append(BIG[:, o - 512:o])
        ROPE = BIG[:, o:o + 2 * SEQ]
        o += 2 * SEQ
        ATMP = []
        for i in range(2):
            ATMP.append(BIG[:, o:o + 1024].bitcast(F32))
            o += 1024
        assert o <= 63296, o
        o = 0
        WUP = []
        for i in range(3):
            w, o = carve(o, [8, 256]); WUP.append(w)
        UROW = []
        URW = 1160
        for i in range(4):
            u, o = carve(o, [1, URW]); UROW.append(BIG[:, o - URW:o])
        CROW = []
        for i in range(4):
            u, o = carve(o, [1, URW]); CROW.append(BIG[:, o - URW:o])
        WDN, o = carve(o, [NFF, D])
        assert o <= 38016, o
        GT, o = carve(o, [NFF, 1152])
        assert o <= 63296, o
        BIGR = "BIG"

        def MM(out, lhsT, rhs, start, stop, rd, wr):
            return S.add("pe", lambda e, a=(out, lhsT, rhs, start, stop): e.matmul(
                a[0], a[1], a[2], start=a[3], stop=a[4], skip_group_check=True), rd, wr)

        def TR(out, in_, rd, wr):
            return S.add("pe", lambda e, a=(out, in_): e.transpose(a[0], a[1], IDENT[:]), list(rd) + ["ident"], wr)

        def ACT(out, in_, func, rd, wr, scale=None, bias=None, accum=None):
            kw = {}
            if scale is not None:
                kw["scale"] = scale
            if bias is not None:
                kw["bias"] = bias
            if accum is not None:
                kw["accum_out"] = accum
            return S.add("act", lambda e, a=(out, in_, func, kw): e.activation(out=a[0], in_=a[1], func=a[2], **a[3]), rd, wr)

        def TT(eng, out, in0, in1, op, rd, wr):
            return S.add(eng, lambda e, a=(out, in0, in1, op): e.tensor_tensor(out=a[0], in0=a[1], in1=a[2], op=a[3]), rd, wr)

        def TS(out, in0, s1, s2, op0, op1, rd, wr):
            if op1 is None:
                return S.add("dve", lambda e, a=(out, in0, s1, op0): e.tensor_scalar(
                    out=a[0], in0=a[1], scalar1=a[2], scalar2=None, op0=a[3]), rd, wr)
            return S.add("dve", lambda e, a=(out, in0, s1, s2, op0, op1): e.tensor_scalar(
                out=a[0], in0=a[1], scalar1=a[2], scalar2=a[3], op0=a[4], op1=a[5]), rd, wr)

        def STT(out, in0, scalar, in1, op0, op1, rd, wr):
            return S.add("dve", lambda e, a=(out, in0, scalar, in1, op0, op1): e.scalar_tensor_tensor(
                out=a[0], in0=a[1], scalar=a[2], in1=a[3], op0=a[4], op1=a[5]), rd, wr)

        def CP(eng, out, in_, rd, wr):
            return S.add(eng, lambda e, a=(out, in_): e.tensor_copy(a[0], a[1]), rd, wr)

        def RECIP(out, in_, rd, wr):
            return S.add("dve", lambda e, a=(out, in_): e.reciprocal(a[0], a[1]), rd, wr)

        def MEMSET(eng, ap, val, wr):
            return S.add(eng, lambda e, a=(ap, val): e.memset(a[0], a[1]), [], wr)

        def DMA(q, out, in_, rd, wr, key):
            return S.add(q, lambda e, a=(out, in_): e.dma_start(out=a[0], in_=a[1]), rd, wr, dma_key=key)

        def POW(out, in_, n, rd, wr):
            return TT("pool", out, in_, NHALF[:, 0:n], ALU.pow, list(rd) + ["const"], wr)

        rr = {"f": 0, "b": 0, "x": 0, "t": 0, "tb": 0, "pt": 0}

        def nxt(kind, n):
            v = rr[kind] % n
            rr[kind] = (v + 1) % n
            return v

        MEMSET("pool", IDENT[:], 0.0, ["ident"])
        S.add("pool", lambda e: e.affine_select(out=IDENT[:], in_=IDENT[:], pattern=[[-1, 128]], compare_op=ALU.not_equal,
                                                fill=1.0, base=0, channel_multiplier=1), ["ident"], ["ident"])
        MEMSET("pool", IDENTF[:], 0.0, ["identf"])
        S.add("pool", lambda e: e.affine_select(out=IDENTF[:], in_=IDENTF[:], pattern=[[-1, 128]], compare_op=ALU.not_equal,
                                                fill=1.0, base=0, channel_multiplier=1), ["identf"], ["identf"])
        MEMSET("pool", ONESF[:], 1.0, ["const"])
        MEMSET("pool", ONESB[:], 1.0, ["const"])
        MEMSET("pool", NHALF[:], -0.5, ["const"])
        MEMSET("pool", EPSC[:], EPS, ["const"])
        DMA("pool", MASK[:], d_mask, [], ["const"], "mask")
        DMA("pool", PERM[:], d_perm, [], ["perm"], "perm")
        DMA("sp", CCF[:], d_cc, [], ["ccf"], "ccf")
        ACT(CCB[:].rearrange("p k r -> p (k r)"), CCF[:], AF.Silu, ["ccf"], ["ccb"])

        out_dmas = []

        def xsrc(l, i):
            if l == 0:
                if i < 16:
                    return d_x[i * 128:(i + 1) * 128, :], None
                return d_ctx[(i - 16) * 128:(i - 15) * 128, :], None
            return d_xs[i * 128:(i + 1) * 128, :], ("dxs", i)

        def mod_gen(l, bufs, bufres, bank, bw):
            VEC = VECS[l % 2]
            vecr = ("vec", l % 2)
            DMA("sp", VEC[:], d_vec[l], [], [vecr], ("vec", l % 2))
            DMA("sp", ESC[:], d_sink[l], [], ["esc"], "esc")
            ACT(ESC[:], ESC[:], AF.Exp, ["esc"], ["esc"])
            wv = d_wada[l].rearrange("(k p) n -> p k n", p=128)
            psM = psf[bank]
            mv = MODC[:].rearrange("p (j r) -> p j r", r=2)
            abv = AB[:].rearrange("p (w r k) -> p w r k", w=4, r=2)
            gcv = GCS[l % 2][:].rearrange("p (w r k) -> p w r k", w=2, r=2)
            nblk = 6 * D // bw
            cpb = bw // 128

            def load_blk(blk):
                b_ = blk % 2
                DMA("pool", bufs[b_], wv[:, :, blk * bw:(blk + 1) * bw], list(bufres[b_]), list(bufres[b_]), ("wada", b_))
            load_blk(0)
            for blk in range(nblk):
                buf = blk % 2
                if blk + 1 < nblk:
                    load_blk(blk + 1)
                for jj in range(cpb):
                    j = blk * cpb + jj
                    for k in range(8):
                        MM(psM[:, 2 * j:2 * j + 2], bufs[buf][:, k, jj * 128:(jj + 1) * 128], CCB[:, k, :],
                           k == 0, k == 7, list(bufres[buf]) + ["ccb"], [PSF[bank]])
                yield
                if blk == 16 // cpb - 1:
                    TT("dve", MODC[:, 0:32], psM[:, 0:32], VEC[:, 246:278], ALU.add, [PSF[bank], vecr], ["modc0"])
                    for r in range(2):
                        STT(abv[:, 0, r, :], mv[:, 8:16, r], 1.0, VEC[:, 0:8], ALU.add, ALU.mult, ["modc0", vecr], ["ab0"])
                        CP("dve", abv[:, 1, r, :], mv[:, 0:8, r], ["modc0"], ["ab0"])
                    yield
                if blk == 24 // cpb - 1:
                    TT("dve", MODC[:, 32:48], psM[:, 32:48], VEC[:, 278:294], ALU.add, [PSF[bank], vecr], ["modc1"])
                    for r in range(2):
                        TT("dve", gcv[:, 0, r, :], mv[:, 16:24, r], VEC[:, 8:16], ALU.mult, ["modc1", vecr], [("gc", l % 2, 0)])
                    yield
            TT("dve", MODC[:, 48:96], psM[:, 48:96], VEC[:, 294:342], ALU.add, [PSF[bank], vecr], ["modc"])
            for r in range(2):
                STT(abv[:, 2, r, :], mv[:, 32:40, r], 1.0, VEC[:, 16:24], ALU.add, ALU.mult, ["modc", vecr], ["ab1"])
                CP("dve", abv[:, 3, r, :], mv[:, 24:32, r], ["modc"], ["ab1"])
                TT("dve", gcv[:, 1, r, :], mv[:, 40:48, r], VEC[:, 24:32], ALU.mult, ["modc", vecr], [("gc", l % 2, 1)])
            yield

        def grows(w, l):
            gcv = GCS[l % 2][:].rearrange("p (w r k) -> p w r k", w=2, r=2)
            for r in range(2):
                for k in range(8):
                    dg = DIAGF[:] if k % 4 == 0 else TMP[k % 4 - 1][:, 0:128]
                    dgr = "diagf0" if k % 4 == 0 else ("tmp", k % 4 - 1)
                    TS(dg, IDENTF[:], gcv[:, w, r, k:k + 1], None, ALU.mult, None, ["identf", ("gc", l % 2, w)], [dgr])
                    bank = 1 + (k // 4)
                    MM(psf[bank][:, (k % 4) * 128:(k % 4 + 1) * 128], ONESF[:], dg, True, True,
                       ["const", dgr], [PSF[bank]])
                    if k % 4 == 3:
                        ACT(GB[:, r, (k // 4) * 512:(k // 4 + 1) * 512], psf[bank][:, :], AF.Copy,
                            [PSF[bank]], [("gb", r)])

        def norm_tile(t, xap, xres):
            sc = 4 * t
            ACT(JUNK[:], xap, AF.Square, [xres], ["junk", ("stat", sc)], accum=STAT[:, sc:sc + 1])
            TS(STAT[:, sc + 1:sc + 2], STAT[:, sc:sc + 1], 1.0 / D, EPS, ALU.mult, ALU.add, [("stat", sc)], [("stat", sc + 1)])
            POW(STAT[:, sc + 2:sc + 3], STAT[:, sc + 1:sc + 2], 1, [("stat", sc + 1)], [("stat", sc + 2)])
            TS(XN[:, t, :], xap, STAT[:, sc + 2:sc + 3], None, ALU.mult, None, [xres, ("stat", sc + 2)], [("xn", t)])

        def transpose_group(tiles, which, dst_res):
            r = 0 if tiles[0] < 16 else 1
            abv = AB[:].rearrange("p (w r k) -> p w r k", w=4, r=2)
            n = len(tiles)
            c0 = tiles[0] * 128
            for k in range(8):
                b = nxt("b", 2)
                for t in range(n):
                    TR(psb[b][:, t * 128:(t + 1) * 128], XN[:, t, k * 128:(k + 1) * 128], [("xn", t)], [PSB[b]])
                ACT(HT[:, k, c0:c0 + n * 128], psb[b][:, 0:n * 128], AF.Identity, [PSB[b], "ab%d" % which], [dst_res],
                    scale=abv[:, 2 * which, r, k:k + 1], bias=abv[:, 2 * which + 1, r, k:k + 1])

        GROUPS = [[0, 1, 2, 3], [4, 5, 6, 7], [8, 9, 10, 11], [12, 13, 14, 15], [16, 17]]

        def premix_gen(l, gsel):
            for g in gsel:
                tiles = GROUPS[g]
                for t, i in enumerate(tiles):
                    s = nxt("x", 3)
                    ap, res = xsrc(l, i)
                    DMA("sp", XT[s][:], ap, [res] if res else [], [("xt", s)], ("xt", s))
                    norm_tile(t, XT[s][:], ("xt", s))
                    yield
                transpose_group(tiles, 0, ("hT", g))
                yield

        def proj_gen(l):
            wv = d_win[l].rearrange("(k p) n -> p k n", p=128)
            coltiles = [(0, 512), (512, 512), (1024, 512), (1536, 512), (2048, 256)]
            blocks = [("qk", 0, 0, 2), ("qk", 2, 2, 2), ("qk", 4, 4, 1),
                      ("conf", 0, 5, 2), ("conf", 1, 7, 2), ("short", 0, 9, 3), ("short", 1, 12, 3), ("v", 0, 15, 1)]

            def load(bi):
                kind, idx, ch0, nch = blocks[bi]
                buf = bi % 2
                DMA("pool", WIN[buf][:, :, 0:nch * 128], wv[:, :, ch0 * 128:(ch0 + nch) * 128], [BIGR], [("win", buf)], ("win", buf))

            def rope_finish(item):
                fmain, tbq, dst, dres, c0, n = item
                f2 = (1 + nxt("f", 5))
                MM(psf[f2][:, 0:n], PERM[:], TMPB[tbq][:, 0:n], True, True, ["perm", ("tmpb", tbq)], [PSF[f2]])
                t1, t2 = nxt("t", 4), nxt("t", 4)
                TT("dve", TMP[t1][:, 0:n], psf[fmain][:, 0:n], ROPE[:, c0:c0 + n], ALU.mult,
                   [PSF[fmain], "rope"], [("tmp", t1)])
                TT("dve", TMP[t2][:, 0:n], psf[f2][:, 0:n], ROPE[:, SEQ + c0:SEQ + c0 + n], ALU.mult,
                   [PSF[f2], "rope"], [("tmp", t2)])
                TT("pool", dst, TMP[t1][:, 0:n], TMP[t2][:, 0:n], ALU.add, [("tmp", t1), ("tmp", t2)], [dres])
            load(0)
            for bi, (kind, idx, ch0, nch) in enumerate(blocks):
                if bi + 1 < len(blocks):
                    load(bi + 1)
                buf = bi % 2
                W = WIN[buf]
                wres = ("win", buf)
                if kind == "v":
                    for i in range(NT):
                        f = (1 + nxt("f", 5))
                        for k in range(8):
                            MM(psf[f][:, 0:128], HT[:, k, i * 128:(i + 1) * 128], W[:, k, 0:128], k == 0, k == 7,
                               [("hT", min(i // 4, 4)), wres], [PSF[f]])
                        CP("dve", VT[:, i, 0:64], psf[f][:, 0:64], [PSF[f]], [("V", i)])
                        ACT(VT[:, i, 192:256], psf[f][:, 64:128], AF.Copy, [PSF[f]], [("V", i)])
                        if i % 3 == 2:
                            yield
                    continue
                if kind == "qk":
                    pend = None
                    for ci in range(nch):
                        chunk = idx + ci
                        isk = chunk == 4
                        for g, (c0, n) in enumerate(coltiles):
                            isctx = g == 4
                            if isctx and l == DEPTH - 1 and not isk:
                                continue
                            f = (1 + nxt("f", 5))
                            for k in range(8):
                                MM(psf[f][:, 0:n], W[:, k, ci * 128:(ci + 1) * 128], HT[:, k, c0:c0 + n], k == 0, k == 7,
                                   [("hT", g), wres], [PSF[f]])
                            dst = KT[:, c0:c0 + n] if isk else QT[:, chunk, c0:c0 + n]
                            dres = ("kT", g) if isk else ("qT", chunk, g)
                            if isctx:
                                ACT(dst, psf[f][:, 0:n], AF.Copy, [PSF[f]], [dres])
                            else:
                                tbq = nxt("tb", 3)
                                ACT(TMPB[tbq][:, 0:n], psf[f][:, 0:n], AF.Copy, [PSF[f]], [("tmpb", tbq)])
                                if pend is not None:
                                    rope_finish(pend)
                                pend = (f, tbq, dst, dres, c0, n)
                            yield
                    if pend is not None:
                        rope_finish(pend)
                    continue
                for g, (c0, n) in enumerate(coltiles):
                    isctx = g == 4
                    if isctx and l == DEPTH - 1:
                        continue
                    banks = []
                    for ci in range(nch):
                        f = (1 + nxt("f", 5))
                        banks.append(f)
                        for k in range(8):
                            MM(psf[f][:, 0:n], W[:, k, ci * 128:(ci + 1) * 128], HT[:, k, c0:c0 + n], k == 0, k == 7,
                               [("hT", g), wres], [PSF[f]])
                    if kind == "conf":
                        t1 = nxt("t", 4)
                        ACT(TMP[t1][:, 0:n], psf[banks[1]][:, 0:n], AF.Sigmoid, [PSF[banks[1]]], [("tmp", t1)])
                        off = 15 + c0 if not isctx else 2093
                        TT("dve", UC[:, idx, off:off + n], psf[banks[0]][:, 0:n], TMP[t1][:, 0:n], ALU.mult,
                           [PSF[banks[0]], ("tmp", t1)], [("uc", g)])
                    elif kind == "short":
                        t1 = nxt("t", 4)
                        ACT(SBS[:, idx, c0:c0 + n], psf[banks[0]][:, 0:n], AF.Copy, [PSF[banks[0]]], [("sbs", g)])
                        ACT(TMP[t1][:, 0:n], psf[banks[2]][:, 0:n], AF.Copy, [PSF[banks[2]]], [("tmp", t1)])
                        off = 1 + c0 if not isctx else 2051
                        TT("dve", PP[:, idx, off:off + n], psf[banks[1]][:, 0:n], TMP[t1][:, 0:n], ALU.mult,
                           [PSF[banks[1]], ("tmp", t1)], [("pp", g)])
                    yield

        def stat_rstd(psS, n, N, rd, tmpi):
            ACT(TMP[tmpi][:, 0:N], psS[:, 0:N], AF.Sqrt, list(rd) + ["const"], [("tmp", tmpi)], scale=1.0 / n, bias=EPSC[:, 0:1])
            RECIP(TMP[tmpi][:, 0:N], TMP[tmpi][:, 0:N], [("tmp", tmpi)], [("tmp", tmpi)])

        def tgeom(T):
            isctx = T == 4
            N = 256 if isctx else 512
            q0 = 2048 if isctx else T * 512
            return isctx, N, q0

        def gnorm_gen(T, ch0, nch, bank, tmps):
            isctx, N, q0 = tgeom(T)
            zb = T % 2
            Z = ZT[zb]
            zr = lambda ch: ("zT", zb, ch)
            for ci in range(nch):
                tb = nxt("tb", 3)
                ACT(TMPB[tb][:, 0:N], Z[:, ch0 + ci, 0:N], AF.Square, [zr(ch0 + ci)], [("tmpb", tb)])
                MM(psf[bank][:, 0:N], ONESB[:, 0:128], TMPB[tb][:, 0:N], ci == 0, ci == nch - 1, ["const", ("tmpb", tb)], [PSF[bank]])
                yield
            tm_, tmr = tmps
            ACT(tm_[:, 0:N], psf[bank][:, 0:N], AF.Ln, [PSF[bank], "const"], [tmr], scale=1.0 / (nch * 128), bias=EPSC[:, 0:1])
            ACT(tm_[:, 0:N], tm_[:, 0:N], AF.Exp, [tmr], [tmr], scale=-0.5)
            yield
            for ci in range(nch):
                STT(Z[:, ch0 + ci, 0:N], Z[:, ch0 + ci, 0:N], L.VEC[:, 32 + ch0 + ci:33 + ch0 + ci], tm_[:, 0:N],
                    ALU.mult, ALU.mult, [zr(ch0 + ci), tmr, L.vecr], [zr(ch0 + ci)])
            yield

        def conf_short_gen(l, T):
            isctx, N, q0 = tgeom(T)
            zb = T % 2
            Z = ZT[zb]
            zr = lambda ch: ("zT", zb, ch)
            uoff = 2093 if isctx else 15 + q0
            cvt = [0, 1]
            tm = 2
            ucr = ["diag", ("uc", T), ("uc", max(T - 1, 0)), ("uc", min(T + 1, 3) if not isctx else 4)]
            for cc in range(2):
                for k in range(31):
                    MM(psf[6 + cc][:, 0:N], DIAG[:, cc * 31 + k, :], UC[:, cc, uoff - 15 + k:uoff - 15 + k + N], k == 0, k == 30,
                       ucr, [PSF[6 + cc]])
                    if k % 8 == 7:
                        yield
                ACT(TMP[cvt[cc]][:, 0:N], psf[6 + cc][:, 0:N], AF.Identity, [PSF[6 + cc], L.vecr], [("tmp", cvt[cc])], bias=L.VEC[:, 40 + cc:41 + cc])
                yield
            hb = [nxt("tb", 3), nxt("tb", 3)]
            for cc in range(2):
                CP("dve", TMPB[hb[cc]][:, 0:N], TMP[cvt[cc]][:, 0:N], [("tmp", cvt[cc])], [("tmpb", hb[cc])])
            yield
            for cc in range(2):
                MM(psf[6][:, 0:N], ONESB[:, 0:128], TMPB[hb[cc]][:, 0:N], cc == 0, cc == 1, ["const", ("tmpb", hb[cc])], [PSF[6]])
            yield
            TS(TMP[tm][:, 0:N], psf[6][:, 0:N], 1.0 / 256, None, ALU.mult, None, [PSF[6]], [("tmp", tm)])
            yield
            for cc in range(2):
                TT("dve", TMP[cvt[cc]][:, 0:N], TMP[cvt[cc]][:, 0:N], TMP[tm][:, 0:N], ALU.subtract,
                   [("tmp", cvt[cc]), ("tmp", tm)], [("tmp", cvt[cc])])
                ACT(TMPB[hb[cc]][:, 0:N], TMP[cvt[cc]][:, 0:N], AF.Square, [("tmp", cvt[cc])], [("tmpb", hb[cc])])
                yield
            for cc in range(2):
                MM(psf[7][:, 0:N], ONESB[:, 0:128], TMPB[hb[cc]][:, 0:N], cc == 0, cc == 1, ["const", ("tmpb", hb[cc])], [PSF[7]])
            yield
            ACT(TMP[tm][:, 0:N], psf[7][:, 0:N], AF.Ln, [PSF[7], "const"], [("tmp", tm)], scale=1.0 / 256, bias=EPSC[:, 0:1])
            ACT(TMP[tm][:, 0:N], TMP[tm][:, 0:N], AF.Exp, [("tmp", tm)], [("tmp", tm)], scale=-0.5)
            yield
            for cc in range(2):
                TT("dve", TMP[cvt[cc]][:, 0:N], TMP[cvt[cc]][:, 0:N], TMP[tm][:, 0:N], ALU.mult,
                   [("tmp", cvt[cc]), ("tmp", tm)], [("tmp", cvt[cc])])
            yield
            for cc in range(2):
                ACT(TMP[tm][:, 0:N], TMP[cvt[cc]][:, 0:N], AF.Exp, [("tmp", cvt[cc]), "nln"], [("tmp", tm)],
                    scale=NLN[:, cc:cc + 1], bias=NLN[:, 2 + cc:3 + cc])
                ACT(TMP[tm][:, 0:N], TMP[tm][:, 0:N], AF.Ln, [("tmp", tm), "const"], [("tmp", tm)], bias=ONESF[:, 0:1])
                ACT(TMP[tm][:, 0:N], TMP[tm][:, 0:N], AF.Exp, [("tmp", tm)], [("tmp", tm)], scale=-1.0)
                TS(TMP[cvt[cc]][:, 0:N], TMP[cvt[cc]][:, 0:N], L.VEC[:, 42 + cc:43 + cc], L.VEC[:, 44 + cc:45 + cc], ALU.mult, ALU.add,
                   [("tmp", cvt[cc]), L.vecr], [("tmp", cvt[cc])])
                TT("dve", Z[:, 4 + cc, 0:N], TMP[cvt[cc]][:, 0:N], TMP[tm][:, 0:N], ALU.mult,
                   [("tmp", cvt[cc]), ("tmp", tm)], [zr(4 + cc)])
                yield
            yield from gnorm_gen(T, 4, 2, 6, (TMP[tm], ("tmp", tm)))
            poff = 2051 if isctx else 1 + q0
            for cc in range(2):
                t1 = cc
                prd = [("pp", T), ("pp", max(T - 1, 0)), ("pp", min(T + 1, 3) if not isctx else 4), L.vecr]
                TS(TMP[t1][:, 0:N], PP[:, cc, poff - 1:poff - 1 + N], L.VEC[:, 46 + cc * 3:47 + cc * 3], None, ALU.mult, None,
                   prd, [("tmp", t1)])
                for k in (1, 2):
                    STT(TMP[t1][:, 0:N], PP[:, cc, poff - 1 + k:poff - 1 + k + N], L.VEC[:, 46 + cc * 3 + k:47 + cc * 3 + k],
                        TMP[t1][:, 0:N], ALU.mult, ALU.add, prd + [("tmp", t1)], [("tmp", t1)])
                yield
                TT("dve", Z[:, 6 + cc, 0:N], SBS[:, cc, (2048 if isctx else q0):(2048 if isctx else q0) + N], TMP[t1][:, 0:N], ALU.mult,
                   [("sbs", T), ("tmp", t1)], [zr(6 + cc)])
                yield
            yield from gnorm_gen(T, 6, 2, 7, (TMP[tm], ("tmp", tm)))

        def attn_gen(l, T):
            isctx, N, q0 = tgeom(T)
            zb = T % 2
            Z = ZT[zb]
            zr = lambda ch: ("zT", zb, ch)
            keys = [(16, 0, N, None), (17, 0, N, None)]
            if not isctx:
                for j in range(max(0, 4 * T - 1), min(15, 4 * T + 4) + 1):
                    lo = max(4 * T, j - 1)
                    hi = min(4 * T + 3, j + 1)
                    qoff = (lo - 4 * T) * 128
                    n = (hi - lo + 1) * 128
                    moff = (lo - (j - 1)) * 128
                    needm = (lo == j - 1) or (hi == j + 1)
                    keys.append((j, qoff, n, moff if needm else None))
            nk = len(keys)
            for c in range(4):
                for ki in range(nk + 1):
                    if ki < nk:
                        j, qoff, n, moff = keys[ki]
                        kg = min(j // 4, 4)
                        for half in range(2):
                            pr = slice(64 * half, 64 * half + 64)
                            f = half
                            MM(psf[f][:, 0:n], KT[pr, j * 128:(j + 1) * 128], QT[pr, c, q0 + qoff:q0 + qoff + n], True, moff is None,
                               [("kT", kg), ("qT", c, T)], [PSF[f]])
                        for half in range(2):
                            f = half
                            pt = 2 * half + (ki % 2)
                            if moff is not None:
                                msegs = []
                                if moff == 0:
                                    msegs.append((0, 0))
                                if moff + n == 384:
                                    msegs.append((n - 128, 256))
                                for si, (pc, mc) in enumerate(msegs):
                                    MM(psf[f][:, pc:pc + 128], IDENT[:], MASK[:, mc:mc + 128], False, si == len(msegs) - 1,
                                       ["ident", "const"], [PSF[f]])
                            ACT(PT[pt][:, 0:n], psf[f][:, 0:n], AF.Exp, [PSF[f]], [("pt", pt)], scale=0.125)
                    if ki >= 1:
                        pj, pqoff, pn, _ = keys[ki - 1]
                        for half in range(2):
                            pp_ = 2 * half + ((ki - 1) % 2)
                            MM(psf[4 + half][:, pqoff:pqoff + pn], VT[:, pj, half * 128:(half + 1) * 128], PT[pp_][:, 0:pn], ki == 1, ki == nk,
                               [("V", pj), "vones", ("pt", pp_)], [PSF[4 + half]])
                    yield
                for half in range(2):
                    h = c + 4 * half
                    pr = slice(64 * half, 64 * half + 64)
                    dn = slice(64 * (1 - half), 64 * (1 - half) + 64)
                    psO = psf[4 + half]
                    at = ATMP[half]
                    ACT(at[pr, 0:N], psO[dn, 0:N], AF.Ln, [PSF[4 + half], "esc"], [("atmp", half)], bias=ESC[dn, h:h + 1])
                    ACT(at[pr, 0:N], at[pr, 0:N], AF.Exp, [("atmp", half)], [("atmp", half)], scale=-1.0)
                    TT("dve", Z[pr, c, 0:N], psO[pr, 0:N], at[pr, 0:N], ALU.mult, [PSF[4 + half], ("atmp", half)], [zr(c)])
                yield

        def tail_gen(l, T):
            isctx, N, q0 = tgeom(T)
            zb = T % 2
            Z = ZT[zb]
            zr = lambda ch: ("zT", zb, ch)
            yield from gnorm_gen(T, 0, 4, 6, (TMP[3], ("tmp", 3)))
            ntl = N // 128
            tiles = [q0 // 128 + t for t in range(ntl)]
            for t, i in enumerate(tiles):
                for hf in range(2):
                    for k in range(8):
                        MM(psf[2 + hf][:, :], Z[:, k, t * 128:(t + 1) * 128], WOUT[:, k, hf * 512:(hf + 1) * 512], k == 0, k == 7,
                           [zr(k), "wout"], [PSF[2 + hf]])
                    yield
                s = nxt("x", 3)
                postnorm_tile(l, i, s, 0, yb=2)
                yield
                norm_tile(t, XT[s][:], ("xt", s))
                yield
            r = 0 if tiles[0] < 16 else 1
            abv = AB[:].rearrange("p (w r k) -> p w r k", w=4, r=2)
            n = len(tiles)
            c0 = tiles[0] * 128
            for k in range(8):
                b = nxt("b", 2)
                for t in range(n):
                    TR(psb[b][:, t * 128:(t + 1) * 128], XN[:, t, k * 128:(k + 1) * 128], [("xn", t)], [PSB[b]])
                ACT(HT[:, k, c0:c0 + n * 128], psb[b][:, 0:n * 128], AF.Identity, [PSB[b], "ab1"], [("hT", T)],
                    scale=abv[:, 2, r, k:k + 1], bias=abv[:, 3, r, k:k + 1])
                yield

        def chain(*gens):
            for g in gens:
                if g is not None:
                    yield from g

        def run_merged(gens):
            gens = [g for g in gens if g is not None]
            while gens:
                for g in list(gens):
                    try:
                        next(g)
                    except StopIteration:
                        gens.remove(g)

        def run_balanced(gens):
            st_ = [[g, 0, float(n)] for (g, n) in gens if g is not None]
            while st_:
                st_.sort(key=lambda x: x[1] / x[2])
                g = st_[0]
                try:
                    next(g[0])
                    g[1] += 1
                except StopIteration:
                    st_.remove(g)

        def mixer_all(l, nT, kstop=99):
            run_merged([conf_short_gen(l, 0)])
            run_merged([attn_gen(l, 0), conf_short_gen(l, 1)])
            for T in range(nT):
                if T == nT - 1:
                    return tail_gen(l, T)
                A = attn_gen(l, T + 1) if T + 1 < nT else None
                hasC = T + 2 < nT
                B = chain(tail_gen(l, T), conf_short_gen(l, T + 2) if hasC else None)
                run_balanced([(A, 40 if T + 1 < 4 else 12), (B, 60 if hasC else 30)])

        def postnorm_tile(l, i, s, w, final=False, yb=4):
            r = 0 if i < 16 else 1
            sc = 16 + 4 * (i % 4)
            if w == 0:
                ap, res = xsrc(l, i)
            else:
                ap, res = d_xs[i * 128:(i + 1) * 128, :], ("dxs", i)
            DMA("sp", XT[s][:], ap, [res] if res else [], [("xt", s)], ("xt", s))
            for hf in range(2):
                ACT(JUNK[:, 0:512], psf[yb + hf][:, :], AF.Square, [PSF[yb + hf]], ["junk", ("stat", sc + hf)],
                    accum=STAT[:, sc + hf:sc + hf + 1])
            TT("dve", STAT[:, sc + 2:sc + 3], STAT[:, sc:sc + 1], STAT[:, sc + 1:sc + 2], ALU.add,
               [("stat", sc), ("stat", sc + 1)], [("stat", sc + 2)])
            TS(STAT[:, sc + 2:sc + 3], STAT[:, sc + 2:sc + 3], 1.0 / D, EPS, ALU.mult, ALU.add, [("stat", sc + 2)], [("stat", sc + 2)])
            POW(STAT[:, sc + 3:sc + 4], STAT[:, sc + 2:sc + 3], 1, [("stat", sc + 2)], [("stat", sc + 3)])
            for hf in range(2):
                t1 = nxt("t", 4)
                STT(TMP[t1][:, :], psf[yb + hf][:, :], STAT[:, sc + 3:sc + 4], GB[:, r, hf * 512:(hf + 1) * 512],
                    ALU.mult, ALU.mult, [PSF[yb + hf], ("stat", sc + 3), ("gb", r)], [("tmp", t1)])
                TT("pool", XT[s][:, hf * 512:(hf + 1) * 512], XT[s][:, hf * 512:(hf + 1) * 512], TMP[t1][:, :], ALU.add,
                   [("xt", s), ("tmp", t1)], [("xt", s)])
            if final:
                if i < 16:
                    op = DMA("sp", d_out[i * 128:(i + 1) * 128, :], XT[s][:], [("xt", s)], [("dout", i)], ("xt", s))
                    out_dmas.append(op)
            else:
                DMA("sp", d_xs[i * 128:(i + 1) * 128, :], XT[s][:], [("xt", s)], [("dxs", i)], ("xt", s))

        def mixer_setup(l):
            TS(NLN[:, 0:4], L.VEC[:, 42:46], -1.0, None, ALU.mult, None, [L.vecr], ["nln"])
            DMA("pool", ROPE, d_rope, [BIGR], ["rope"], "rope")
            MEMSET("pool", UC[:, :, :], 0.0, [BIGR, ("uc", 0), ("uc", 1), ("uc", 2), ("uc", 3), ("uc", 4)])
            MEMSET("pool", PP[:, :, :], 0.0, [BIGR, ("pp", 0), ("pp", 1), ("pp", 2), ("pp", 3), ("pp", 4)])
            MEMSET("pool", VT[:, :, 64:192], 1.0, ["vones"])
            for cc in range(2):
                for k in range(31):
                    TS(DIAG[:, cc * 31 + k, :], IDENT[:], L.VEC[:, 52 + cc * 31 + k:53 + cc * 31 + k], None, ALU.mult, None,
                       ["ident", L.vecr], ["diag"])

        def load_wout(l):
            wv = d_wout[l].rearrange("(k p) n -> p k n", p=128)
            DMA("pool", WOUT[:, :, :], wv, [("win", 0), ("win", 1)], ["wout", ("win", 0), ("win", 1)], "wout")

        def ffn_geom(l, hi_):
            last = l == DEPTH - 1
            ta, tb_ = [(0, 9), (9, 16 if last else 18)][hi_]
            tok_lo, tok_hi = ta * 128, min(tb_, 16) * 128
            segs = []
            ulo = max(tok_lo - 1, 0)
            uhi = min(tok_hi + 1, SEQ)
            for (c0, n) in split_cols(ulo, uhi):
                segs.append((c0, n, c0 - tok_lo + 1))
            nx = tok_hi - tok_lo
            ctxcol = None
            if tb_ > 16:
                ctxcol = nx + 3
                segs.append((2048, 256, ctxcol))
            rowlen = (ctxcol + 256 + 1) if ctxcol is not None else nx + 2
            return ta, tb_, segs, nx, ctxcol, rowlen

        FF_EARLY = ["diag", "vones", "rope", ("atmp", 0), ("atmp", 1)] + [("pt", i) for i in range(4)] + \
                   [("uc", g) for g in range(5)] + [("pp", g) for g in range(5)] + \
                   [("sbs", g) for g in range(5)] + [("kT", g) for g in range(5)] + [("V", i) for i in range(NT)] + \
                   [("qT", c, g) for c in range(4) for g in range(5)]
        FF_LATE = [BIGR, "wout", ("win", 0), ("win", 1)] + [("zT", i, ch) for i in range(2) for ch in range(8)]

        def ffn_fence_a():
            S.add("pool", lambda e: e.memset(STAT[:, 60:61], 0.0), FF_EARLY, ["ffnfence"] + FF_EARLY)

        def ffn_fence_b():
            S.add("pool", lambda e: e.memset(STAT[:, 62:63], 0.0), FF_LATE + ["ffnfence"], ["ffnfenceB"] + FF_LATE)

        def ffn_up_gen(l, hi_, banks=(0, 1, 2, 3), order=None):
            wuv = d_wup[l].rearrange("(k p) n -> p k n", p=128)
            ta, tb_, segs, nx, ctxcol, rowlen = ffn_geom(l, hi_)
            for i in range(4):
                S.add("pool", lambda e, a=UROW[i]: e.memset(a[:, :], 0.0), ["ffnfence"], [("urow", i)])

            order = list(range(NFF)) if order is None else list(order)

            def load_wup(p):
                b_ = p % 3
                jj_ = order[p]
                DMA("pool", WUP[b_][:, :, :], wuv[:, :, jj_ * 256:(jj_ + 1) * 256], ["ffnfence"], [("wup", b_)], ("wup", b_))
            load_wup(0)
            load_wup(1)
            for p, j in enumerate(order):
                buf = p % 3
                if p + 2 < NFF:
                    load_wup(p + 2)
                if hi_ == 0 and p in (1, 3):
                    wdv = d_wdn[l].rearrange("(k p) n -> p k n", p=128)
                    h0 = 0 if p == 1 else 11
                    DMA("pool", WDN[:, h0:h0 + 11, :], wdv[:, h0:h0 + 11, :], ["ffnfence"], ["wdn"], "wdn%d" % (p // 2))
                ub = (p % 2) * 2
                L_ = rowlen - 2
                for gv in range(2):
                    ur = ub + gv
                    for (c0, n, col) in segs:
                        f = banks[nxt("f", 4)]
                        gs = sorted(set([min(c0 // 512, 4), min((c0 + n - 1) // 512, 4)]))
                        for k in range(8):
                            MM(psf[f][:, 0:n], WUP[buf][:, k, gv * 128:(gv + 1) * 128], HT[:, k, c0:c0 + n], k == 0, k == 7,
                               [("hT", g) for g in gs] + [("wup", buf)], [PSF[f]])
                        ACT(UROW[ur][:, col:col + n], psf[f][:, 0:n], AF.Copy, [PSF[f]], [("urow", ur)])
                    wc = 114 + (gv * NFF + j) * 3
                    TS(CROW[ur][:, 1:1 + L_], UROW[ur][:, 0:L_], L.VEC[:, wc:wc + 1], None, ALU.mult, None,
                       [("urow", ur), L.vecr], [("crow", ur)])
                    for k in (1, 2):
                        STT(CROW[ur][:, 1:1 + L_], UROW[ur][:, k:k + L_], L.VEC[:, wc + k:wc + k + 1], CROW[ur][:, 1:1 + L_],
                            ALU.mult, ALU.add, [("urow", ur), ("crow", ur), L.vecr], [("crow", ur)])
                    yield
                ACT(CROW[ub][:, 1:1 + L_], CROW[ub][:, 1:1 + L_], AF.Silu, [("crow", ub)], [("crow", ub)])
                TT("pool", GT[:, j, 0:nx], CROW[ub][:, 1:1 + nx], CROW[ub + 1][:, 1:1 + nx], ALU.mult,
                   [("crow", ub), ("crow", ub + 1)] + (["ffnfenceB"] if j < 15 else []), [("gT", j)])
                if ctxcol is not None:
                    TT("pool", GT[:, j, nx:nx + 256], CROW[ub][:, ctxcol:ctxcol + 256], CROW[ub + 1][:, ctxcol:ctxcol + 256], ALU.mult,
                       [("crow", ub), ("crow", ub + 1)] + (["ffnfenceB"] if j < 15 else []), [("gT", j)])
                yield

        def ffn_down(l, hi_):
            last = l == DEPTH - 1
            ta, tb_, segs, nx, ctxcol, rowlen = ffn_geom(l, hi_)
            for t, i in enumerate(range(ta, tb_)):
                yb = 4 if t % 2 == 0 else 2
                for hf in range(2):
                    for k in range(NFF):
                        MM(psf[yb + hf][:, :], GT[:, k, t * 128:(t + 1) * 128], WDN[:, k, hf * 512:(hf + 1) * 512], k == 0, k == NFF - 1,
                           [("gT", k), "wdn"], [PSF[yb + hf]])
                s = nxt("x", 3)
                postnorm_tile(l, i, s, 1, final=last, yb=yb)
                g = i // 4
                if not last and g != 2:
                    norm_tile(i % 4, XT[s][:], ("xt", s))
                    if i == GROUPS[g][-1]:
                        transpose_group(GROUPS[g], 0, ("hT", g))

        def ffn_end(l):
            ffnres = ["wdn"] + [("wup", i) for i in range(3)] + [("urow", i) for i in range(4)] + \
                     [("crow", i) for i in range(4)] + [("gT", j) for j in range(NFF)]
            S.add("pool", lambda e: e.memset(STAT[:, 61:62], 0.0), ffnres + [BIGR], ffnres + [BIGR])

        kstop = int(os.environ.get("KSTOP", "99"))
        XNW = [XN[:, 0:2, :].rearrange("p a b -> p (a b)").rearrange("p (k n) -> p k n", n=256),
               XN[:, 2:4, :].rearrange("p a b -> p (a b)").rearrange("p (k n) -> p k n", n=256)]
        XNWR = [[("xn", 0), ("xn", 1)], [("xn", 2), ("xn", 3)]]
        for l in range(DEPTH):
            S.epoch = l
            L.VEC = VECS[l % 2]
            L.vecr = ("vec", l % 2)
            if l == 0:
                mg = mod_gen(0, [ZT[0], ZT[1]], [[("zT", 0, ch) for ch in range(8)], [("zT", 1, ch) for ch in range(8)]], 0, 512)
                import itertools
                for _ in range(5):
                    next(mg)
                run_merged([itertools.islice(mg, 3), premix_gen(0, [0, 1, 2, 3, 4])])

            else:
                run_merged([premix_gen(l, [2])])
            if kstop <= 0:
                break
            grows(0, l)
            if kstop <= 1:
                break
            mixer_setup(l)
            if l == 0:
                run_balanced([(proj_gen(l), 70), (mg, 7)])
            else:
                run_merged([proj_gen(l)])
            load_wout(l)
            if kstop <= 2:
                break
            last_tail = mixer_all(l, 5 if l < DEPTH - 1 else 4, kstop)
            if kstop <= 4:
                run_merged([last_tail])
                break
            ffn_fence_a()
            import itertools
            up0 = ffn_up_gen(l, 0, banks=(0, 1, 4, 5), order=list(range(15, NFF)) + list(range(15)))
            run_balanced([(last_tail, 30), (itertools.islice(up0, 21), 21)])
            ffn_fence_b()
            run_balanced([(mod_gen(l + 1, XNW, XNWR, 6, 256) if l + 1 < DEPTH else None, 26), (up0, 45)])
            grows(1, l)
            up1 = ffn_up_gen(l, 1)
            next(up1)
            ffn_down(l, 0)
            run_merged([up1])
            ffn_down(l, 1)
            ffn_end(l)
            if kstop <= 5:
                break
        S.emit(final_waits=out_dmas)
    return nc


def rope_tables():
    t = np.arange(SEQ)
    pos = np.stack([t // 64, t % 64], 0).astype(np.float32)
    inv = (10000.0 ** (-np.arange(16, dtype=np.float32) / 16)).astype(np.float32)
    cos = np.zeros((128, SEQ), np.float32)
    sin = np.zeros((128, SEQ), np.float32)
    for p in range(128):
        d = p % 64
        r, idx = d // 32, d % 32
        ang = pos[r] * inv[idx % 16]
        cos[p] = np.cos(ang)
        sin[p] = np.sin(ang) * (-1.0 if idx < 16 else 1.0)
    return np.concatenate([cos, sin], 1)


def col_layout(v):
    return np.ascontiguousarray(v.reshape(-1, 128).T)


def rot_partner(d):
    r, idx = d // 32, d % 32
    return r * 32 + (idx + 16 if idx < 16 else idx - 16)


def win_perm():
    out = []
    for c in range(4):
        out.append(np.concatenate([np.arange(c * 64, c * 64 + 64), np.arange((c + 4) * 64, (c + 4) * 64 + 64)]))
    out.append(512 + np.arange(128))
    cv0, cg0, sb0, scg0, su0 = 768, 1024, 1280, 1536, 1792
    for cc in range(2):
        out += [cv0 + cc * 128 + np.arange(128), cg0 + cc * 128 + np.arange(128)]
    for cc in range(2):
        out += [sb0 + cc * 128 + np.arange(128), scg0 + cc * 128 + np.arange(128), su0 + cc * 128 + np.arange(128)]
    out += [640 + np.arange(128)]
    return np.concatenate(out)


def perm_matrix():
    pm = np.zeros((128, 128), np.float32)
    for m in range(128):
        k = (m // 64) * 64 + rot_partner(m % 64)
        pm[k, m] = 1.0
    return pm


def attn_chan_perm():
    idx = []
    for c in range(4):
        idx += list(range(c * 64, c * 64 + 64)) + list(range((c + 4) * 64, (c + 4) * 64 + 64))
    idx += list(range(512, 1024))
    return np.array(idx)


_CACHE = {}


def prepare_inputs(inputs):
    f = lambda a: np.ascontiguousarray(np.asarray(a, dtype=np.float32))
    x, c, ctx, c_ctx = f(inputs["x"]), f(inputs["c"]), f(inputs["ctx"]), f(inputs["c_ctx"])
    rope = rope_tables()
    kk = np.arange(128)[:, None]
    qq = np.arange(128)[None, :]
    mask = np.zeros((128, 384), np.float32)
    mask[:, 0:128] = np.where(kk <= qq, 0.0, -30000.0)
    mask[:, 256:384] = np.where(qq <= kk, 0.0, -30000.0)
    perm = win_perm()
    zperm = attn_chan_perm()
    shared = {"rope": rope, "mask": mask, "perm": perm_matrix()}
    for l in range(DEPTH):
        shared["wada%d" % l] = f(inputs["w_ada"][l])
        vec = np.zeros((128, NV), np.float32)
        vec[:, 0:8] = col_layout(f(inputs["g_pre_mix"][l]))
        vec[:, 8:16] = col_layout(f(inputs["g_post_mix"][l]))
        vec[:, 16:24] = col_layout(f(inputs["g_pre_ffn"][l]))
        vec[:, 24:32] = col_layout(f(inputs["g_post_ffn"][l]))
        vec[:, 32:40] = col_layout(f(inputs["g_group"][l])[zperm])
        vec[:, 40:42] = col_layout(f(inputs["b_conf_dw"][l]))
        vec[:, 42:44] = col_layout(f(inputs["conf_ln_g"][l]))
        vec[:, 44:46] = col_layout(f(inputs["conf_ln_b"][l]))
        wsc = f(inputs["w_sc_dw"][l])
        wcf = f(inputs["w_conf_dw"][l])
        wff = f(inputs["w_ffn_dw"][l])
        for cc in range(2):
            vec[:, 46 + cc * 3:49 + cc * 3] = wsc[:, cc * 128:(cc + 1) * 128].T
            vec[:, 52 + cc * 31:83 + cc * 31] = wcf[:, cc * 128:(cc + 1) * 128].T
        for ch in range(44):
            vec[:, 114 + ch * 3:117 + ch * 3] = wff[:, ch * 128:(ch + 1) * 128].T
        ba = col_layout(f(inputs["b_ada"][l]))
        vec[:, 246:342] = np.repeat(ba, 2, axis=1)
        shared["vec%d" % l] = vec
        shared["win%d" % l] = np.ascontiguousarray(f(inputs["w_in"][l])[:, perm])
        shared["sink%d" % l] = np.ascontiguousarray(np.broadcast_to(f(inputs["sink"][l])[None, :], (128, 8)))
        shared["wout%d" % l] = np.ascontiguousarray(f(inputs["w_out"][l])[zperm, :])
        wu = f(inputs["w_up"][l])
        shared["wup%d" % l] = np.ascontiguousarray(
            np.stack([wu[:, :DFF].reshape(D, NFF, 128), wu[:, DFF:].reshape(D, NFF, 128)], axis=2).reshape(D, 2 * DFF))
        shared["wdn%d" % l] = f(inputs["w_down"][l])
    in_maps = []
    for b in range(8):
        m = dict(shared)
        m["x"] = x[b]
        m["ctx"] = ctx[b]
        cc = np.zeros((128, 16), np.float32)
        cc[:, 0::2] = col_layout(c[b])
        cc[:, 1::2] = col_layout(c_ctx)
        m["cc"] = cc
        in_maps.append(m)
    return in_maps


def kernel(**inputs):
    in_maps = prepare_inputs(inputs)
    if "nc" not in _CACHE:
        _CACHE["nc"] = build_program()
    nc = _CACHE["nc"]
    res = run_bass_kernel_spmd(nc, in_maps, core_ids=list(range(8)))
    out = np.stack([np.asarray(r["out"], dtype=np.float32) for r in res.results], 0)
    return out
```

```python
import contextlib
import os
import numpy as np
import ml_dtypes
import concourse.bass as bass
import concourse.mybir as mybir
from concourse.bass_utils import run_bass_kernel_spmd

F32 = mybir.dt.float32
BF16 = mybir.dt.bfloat16
AF = mybir.ActivationFunctionType
ALU = mybir.AluOpType

D = 1024
SEQ = 2048
CTX = 256
NT = 18
NTOK = NT * 128
DEPTH = 2
DFF = 2816
NFF = 22
EPS = 1e-6
NCW = 16
NV = 342


class Op:
    __slots__ = ("eng", "fn", "deps", "sem", "ticket", "signal", "ninc", "name")


class Sched:
    ENGS = ("pe", "act", "dve", "pool", "sp")

    def __init__(self, nc):
        self.nc = nc
        self.ops = {e: [] for e in self.ENGS}
        self.lastw = {}
        self.readers = {}
        self.dma_count = {}
        self.all_ops = []
        self.epoch = 0

    def add(self, eng, fn, reads=(), writes=(), dma_key=None, ndma=1, name=""):
        op = Op()
        op.eng, op.fn, op.name = eng, fn, name
        op.signal = False
        deps = set()
        excl = [r for r in reads if isinstance(r, str) and r.startswith("ps:")]
        reads = [r for r in reads if r not in excl]
        writes = list(writes) + excl
        for r in reads:
            lw = self.lastw.get(r)
            if lw is not None:
                deps.add(lw)
        for w in writes:
            lw = self.lastw.get(w)
            if lw is not None:
                deps.add(lw)
            lastrd = {}
            for rd in self.readers.get(w, ()):
                if rd.sem[0] == "dma":
                    deps.add(rd)
                else:
                    lastrd[rd.eng] = rd
            deps.update(lastrd.values())
        for r in reads:
            self.readers.setdefault(r, []).append(op)
        for w in writes:
            self.lastw[w] = op
            self.readers[w] = []
        deps.discard(op)
        if eng == "pe":
            deps = set(d for d in deps if d.eng != "pe")
        op.deps = deps
        if dma_key is not None:
            op.sem = ("dma", dma_key)
            self.dma_count[dma_key] = self.dma_count.get(dma_key, 0) + 16 * ndma
            op.ticket = self.dma_count[dma_key]
            op.ninc = ndma
            op.signal = True
        else:
            op.sem = ("eng", eng, self.epoch)
            op.ticket = None
            op.ninc = 0
        for d in deps:
            d.signal = True
        self.ops[eng].append(op)
        self.all_ops.append(op)
        return op

    def emit(self, final_waits=()):
        nc = self.nc
        cnts = {}
        for e in self.ENGS:
            for op in self.ops[e]:
                if op.sem[0] == "eng" and op.signal:
                    cnts[op.sem] = cnts.get(op.sem, 0) + 1
                    op.ticket = cnts[op.sem]
        with contextlib.ExitStack() as st:
            sems = {}
            for i, k in enumerate(cnts):
                sems[k] = st.enter_context(nc.semaphore("s%d" % i))
            for i, k in enumerate(self.dma_count):
                sems[("dma", k)] = st.enter_context(nc.semaphore("d%d" % i))
            block = st.enter_context(nc.Block())

            def run(engname, eng):
                waited = {}
                for op in self.ops[engname]:
                    need = {}
                    for d in op.deps:
                        if need.get(d.sem, 0) < d.ticket:
                            need[d.sem] = d.ticket
                    for s, v in need.items():
                        if waited.get(s, 0) < v:
                            eng.wait_ge(sems[s], v)
                            waited[s] = v
                    ins = op.fn(eng)
                    if op.sem[0] == "dma":
                        if not isinstance(ins, (list, tuple)):
                            ins = [ins]
                        assert len(ins) == op.ninc, (op.name, len(ins), op.ninc)
                        for i in ins:
                            i.then_inc(sems[op.sem], 16)
                    elif op.signal:
                        ins.then_inc(sems[op.sem], 1)
                if engname == "sp":
                    for op in final_waits:
                        eng.wait_ge(sems[op.sem], op.ticket)

            block.tensor(lambda eng: run("pe", eng))
            block.scalar(lambda eng: run("act", eng))
            block.vector(lambda eng: run("dve", eng))
            block.gpsimd(lambda eng: run("pool", eng))
            block.sync(lambda eng: run("sp", eng))


def split_cols(lo, hi, mx=512):
    out = []
    while lo < hi:
        n = min(mx, hi - lo)
        out.append((lo, n))
        lo += n
    return out


def build_program(stop_after=None):
    nc = bass.Bass("TRN2", target_bir_lowering=False)
    dram = lambda name, shape, dt=F32, kind="ExternalInput": nc.dram_tensor(name, shape, dt, kind=kind).ap()
    d_x = dram("x", [SEQ, D])
    d_ctx = dram("ctx", [CTX, D])
    d_cc = dram("cc", [128, 16])
    d_rope = dram("rope", [128, 2 * SEQ])
    d_mask = dram("mask", [128, 384])
    d_perm = dram("perm", [128, 128])
    d_wada = [dram("wada%d" % l, [D, 6 * D]) for l in range(DEPTH)]
    d_vec = [dram("vec%d" % l, [128, NV]) for l in range(DEPTH)]
    d_win = [dram("win%d" % l, [D, NCW * 128]) for l in range(DEPTH)]
    d_sink = [dram("sink%d" % l, [128, 8]) for l in range(DEPTH)]
    d_wout = [dram("wout%d" % l, [D, D]) for l in range(DEPTH)]
    d_wup = [dram("wup%d" % l, [D, 2 * DFF]) for l in range(DEPTH)]
    d_wdn = [dram("wdn%d" % l, [DFF, D]) for l in range(DEPTH)]
    d_out = dram("out", [SEQ, D], kind="ExternalOutput")
    d_xs = dram("xs_scratch", [NTOK, D], kind="Internal")

    st = contextlib.ExitStack()
    with st:
        sb = lambda name, shape, dt: st.enter_context(nc.sbuf_tensor(name, shape, dt))
        S = Sched(nc)

        BIG = sb("BIG", [128, 63296], BF16)
        HT = sb("HT", [128, 8, NTOK], BF16)
        GB = sb("GB", [128, 2, D], F32)
        XT = [sb("XT%d" % i, [128, D], F32) for i in range(3)]
        XN = sb("XN", [128, 4, D], BF16)
        JUNK = sb("JUNK", [128, D], BF16)
        TMP = [sb("TMP%d" % i, [128, 512], F32) for i in range(4)]
        TMPB = [sb("TMPB%d" % i, [128, 512], BF16) for i in range(3)]
        IDENT = sb("IDENT", [128, 128], BF16)
        IDENTF = sb("IDENTF", [128, 128], F32)
        ONESF = sb("ONESF", [128, 128], F32)
        ONESB = sb("ONESB", [128, 128], BF16)
        NHALF = sb("NHALF", [128, 8], F32)
        MASK = sb("MASK", [128, 384], BF16)
        PERM = sb("PERM", [128, 128], BF16)
        VECS = [sb("VEC%d" % i, [128, NV], F32) for i in range(2)]

        class L:
            VEC = None
            vecr = None
        CCF = sb("CCF", [128, 16], F32)
        CCB = sb("CCB", [128, 8, 2], BF16)
        MODC = sb("MODC", [128, 96], F32)
        AB = sb("AB", [128, 64], F32)
        GCS = [sb("GC%d" % i, [128, 32], F32) for i in range(2)]
        DIAGF = sb("DIAGF", [128, 128], F32)
        STAT = sb("STAT", [128, 64], F32)
        ESC = sb("ESC", [128, 8], F32)
        EPSC = sb("EPSC", [128, 1], F32)
        NLN = sb("NLN", [128, 4], F32)

        psf = [st.enter_context(nc.psum_tensor("psf%d" % i, [128, 512], F32)) for i in range(8)]
        psb = [psf[6 + i][:, :].bitcast(BF16) for i in range(2)]
        PSF = ["ps:f%d" % i for i in range(8)]
        PSB = [PSF[6], PSF[7]]

        def carve(off, shape):
            n = int(np.prod(shape))
            ap = BIG[:, off:off + n]
            if len(shape) == 2:
                ap = ap.rearrange("p (a b) -> p a b", b=shape[1])
            return ap, off + n
        o = 0
        QT, o = carve(o, [4, NTOK])
        KT, o = carve(o, [1, NTOK])
        KT = BIG[:, 9216:9216 + NTOK]
        VT, o = carve(o, [NT, 256])
        UCW = 2364
        UC, o = carve(o, [2, UCW])
        PW = 2308
        PP, o = carve(o, [2, PW])
        SBS, o = carve(o, [2, NTOK])
        DIAG, o = carve(o, [62, 128])
        WBUF_OFF = o
        WIN = []
        for i in range(2):
            w, o = carve(o, [8, 384]); WIN.append(w)
        WOUT, _ = carve(WBUF_OFF, [8, D])
        o = WBUF_OFF + 8 * D
        WADA = [WIN[0], WIN[1]]
        ZT = []
        for i in range(2):
            z, o = carve(o, [8, 512]); ZT.append(z)
        PT = []
        for i in range(4):
            p_, o = carve(o, [1, 512]); PT.append(BIG[:, o - 512:o])
        ROPE = BIG[:, o:o + 2 * SEQ]
        o += 2 * SEQ
        ATMP = []
        for i in range(2):
            ATMP.append(BIG[:, o:o + 1024].bitcast(F32))
            o += 1024
        assert o <= 63296, o
        o = 0
        WUP = []
        for i in range(3):
            w, o = carve(o, [8, 256]); WUP.append(w)
        UROW = []
        URW = 1160
        for i in range(4):
            u, o = carve(o, [1, URW]); UROW.append(BIG[:, o - URW:o])
        CROW = []
        for i in range(4):
            u, o = carve(o, [1, URW]); CROW.append(BIG[:, o - URW:o])
        WDN, o = carve(o, [NFF, D])
        assert o <= 38016, o
        GT, o = carve(o, [NFF, 1152])
        assert o <= 63296, o
        BIGR = "BIG"

        def MM(out, lhsT, rhs, start, stop, rd, wr):
            return S.add("pe", lambda e, a=(out, lhsT, rhs, start, stop): e.matmul(
                a[0], a[1], a[2], start=a[3], stop=a[4], skip_group_check=True), rd, wr)

        def TR(out, in_, rd, wr):
            return S.add("pe", lambda e, a=(out, in_): e.transpose(a[0], a[1], IDENT[:]), list(rd) + ["ident"], wr)

        def ACT(out, in_, func, rd, wr, scale=None, bias=None, accum=None):
            kw = {}
            if scale is not None:
                kw["scale"] = scale
            if bias is not None:
                kw["bias"] = bias
            if accum is not None:
                kw["accum_out"] = accum
            return S.add("act", lambda e, a=(out, in_, func, kw): e.activation(out=a[0], in_=a[1], func=a[2], **a[3]), rd, wr)

        def TT(eng, out, in0, in1, op, rd, wr):
            return S.add(eng, lambda e, a=(out, in0, in1, op): e.tensor_tensor(out=a[0], in0=a[1], in1=a[2], op=a[3]), rd, wr)

        def TS(out, in0, s1, s2, op0, op1, rd, wr):
            if op1 is None:
                return S.add("dve", lambda e, a=(out, in0, s1, op0): e.tensor_scalar(
                    out=a[0], in0=a[1], scalar1=a[2], scalar2=None, op0=a[3]), rd, wr)
            return S.add("dve", lambda e, a=(out, in0, s1, s2, op0, op1): e.tensor_scalar(
                out=a[0], in0=a[1], scalar1=a[2], scalar2=a[3], op0=a[4], op1=a[5]), rd, wr)

        def STT(out, in0, scalar, in1, op0, op1, rd, wr):
            return S.add("dve", lambda e, a=(out, in0, scalar, in1, op0, op1): e.scalar_tensor_tensor(
                out=a[0], in0=a[1], scalar=a[2], in1=a[3], op0=a[4], op1=a[5]), rd, wr)

        def CP(eng, out, in_, rd, wr):
            return S.add(eng, lambda e, a=(out, in_): e.tensor_copy(a[0], a[1]), rd, wr)

        def RECIP(out, in_, rd, wr):
            return S.add("dve", lambda e, a=(out, in_): e.reciprocal(a[0], a[1]), rd, wr)

        def MEMSET(eng, ap, val, wr):
            return S.add(eng, lambda e, a=(ap, val): e.memset(a[0], a[1]), [], wr)

        def DMA(q, out, in_, rd, wr, key):
            return S.add(q, lambda e, a=(out, in_): e.dma_start(out=a[0], in_=a[1]), rd, wr, dma_key=key)

        def POW(out, in_, n, rd, wr):
            return TT("pool", out, in_, NHALF[:, 0:n], ALU.pow, list(rd) + ["const"], wr)

        rr = {"f": 0, "b": 0, "x": 0, "t": 0, "tb": 0, "pt": 0}

        def nxt(kind, n):
            v = rr[kind] % n
            rr[kind] = (v + 1) % n
            return v

        MEMSET("pool", IDENT[:], 0.0, ["ident"])
        S.add("pool", lambda e: e.affine_select(out=IDENT[:], in_=IDENT[:], pattern=[[-1, 128]], compare_op=ALU.not_equal,
                                                fill=1.0, base=0, channel_multiplier=1), ["ident"], ["ident"])
        MEMSET("pool", IDENTF[:], 0.0, ["identf"])
        S.add("pool", lambda e: e.affine_select(out=IDENTF[:], in_=IDENTF[:], pattern=[[-1, 128]], compare_op=ALU.not_equal,
                                                fill=1.0, base=0, channel_multiplier=1), ["identf"], ["identf"])
        MEMSET("pool", ONESF[:], 1.0, ["const"])
        MEMSET("pool", ONESB[:], 1.0, ["const"])
        MEMSET("pool", NHALF[:], -0.5, ["const"])
        MEMSET("pool", EPSC[:], EPS, ["const"])
        DMA("pool", MASK[:], d_mask, [], ["const"], "mask")
        DMA("pool", PERM[:], d_perm, [], ["perm"], "perm")
        DMA("sp", CCF[:], d_cc, [], ["ccf"], "ccf")
        ACT(CCB[:].rearrange("p k r -> p (k r)"), CCF[:], AF.Silu, ["ccf"], ["ccb"])

        out_dmas = []

        def xsrc(l, i):
            if l == 0:
                if i < 16:
                    return d_x[i * 128:(i + 1) * 128, :], None
                return d_ctx[(i - 16) * 128:(i - 15) * 128, :], None
            return d_xs[i * 128:(i + 1) * 128, :], ("dxs", i)

        def mod_gen(l, bufs, bufres, bank, bw):
            VEC = VECS[l % 2]
            vecr = ("vec", l % 2)
            DMA("sp", VEC[:], d_vec[l], [], [vecr], ("vec", l % 2))
            DMA("sp", ESC[:], d_sink[l], [], ["esc"], "esc")
            ACT(ESC[:], ESC[:], AF.Exp, ["esc"], ["esc"])
            wv = d_wada[l].rearrange("(k p) n -> p k n", p=128)
            psM = psf[bank]
            mv = MODC[:].rearrange("p (j r) -> p j r", r=2)
            abv = AB[:].rearrange("p (w r k) -> p w r k", w=4, r=2)
            gcv = GCS[l % 2][:].rearrange("p (w r k) -> p w r k", w=2, r=2)
            nblk = 6 * D // bw
            cpb = bw // 128

            def load_blk(blk):
                b_ = blk % 2
                DMA("pool", bufs[b_], wv[:, :, blk * bw:(blk + 1) * bw], list(bufres[b_]), list(bufres[b_]), ("wada", b_))
            load_blk(0)
            for blk in range(nblk):
                buf = blk % 2
                if blk + 1 < nblk:
                    load_blk(blk + 1)
                for jj in range(cpb):
                    j = blk * cpb + jj
                    for k in range(8):
                        MM(psM[:, 2 * j:2 * j + 2], bufs[buf][:, k, jj * 128:(jj + 1) * 128], CCB[:, k, :],
                           k == 0, k == 7, list(bufres[buf]) + ["ccb"], [PSF[bank]])
                yield
                if blk == 16 // cpb - 1:
                    TT("dve", MODC[:, 0:32], psM[:, 0:32], VEC[:, 246:278], ALU.add, [PSF[bank], vecr], ["modc0"])
                    for r in range(2):
                        STT(abv[:, 0, r, :], mv[:, 8:16, r], 1.0, VEC[:, 0:8], ALU.add, ALU.mult, ["modc0", vecr], ["ab0"])
                        CP("dve", abv[:, 1, r, :], mv[:, 0:8, r], ["modc0"], ["ab0"])
                    yield
                if blk == 24 // cpb - 1:
                    TT("dve", MODC[:, 32:48], psM[:, 32:48], VEC[:, 278:294], ALU.add, [PSF[bank], vecr], ["modc1"])
                    for r in range(2):
                        TT("dve", gcv[:, 0, r, :], mv[:, 16:24, r], VEC[:, 8:16], ALU.mult, ["modc1", vecr], [("gc", l % 2, 0)])
                    yield
            TT("dve", MODC[:, 48:96], psM[:, 48:96], VEC[:, 294:342], ALU.add, [PSF[bank], vecr], ["modc"])
            for r in range(2):
                STT(abv[:, 2, r, :], mv[:, 32:40, r], 1.0, VEC[:, 16:24], ALU.add, ALU.mult, ["modc", vecr], ["ab1"])
                CP("dve", abv[:, 3, r, :], mv[:, 24:32, r], ["modc"], ["ab1"])
                TT("dve", gcv[:, 1, r, :], mv[:, 40:48, r], VEC[:, 24:32], ALU.mult, ["modc", vecr], [("gc", l % 2, 1)])
            yield

        def grows(w, l):
            gcv = GCS[l % 2][:].rearrange("p (w r k) -> p w r k", w=2, r=2)
            for r in range(2):
                for k in range(8):
                    dg = DIAGF[:] if k % 4 == 0 else TMP[k % 4 - 1][:, 0:128]
                    dgr = "diagf0" if k % 4 == 0 else ("tmp", k % 4 - 1)
                    TS(dg, IDENTF[:], gcv[:, w, r, k:k + 1], None, ALU.mult, None, ["identf", ("gc", l % 2, w)], [dgr])
                    bank = 1 + (k // 4)
                    MM(psf[bank][:, (k % 4) * 128:(k % 4 + 1) * 128], ONESF[:], dg, True, True,
                       ["const", dgr], [PSF[bank]])
                    if k % 4 == 3:
                        ACT(GB[:, r, (k // 4) * 512:(k // 4 + 1) * 512], psf[bank][:, :], AF.Copy,
                            [PSF[bank]], [("gb", r)])

        def norm_tile(t, xap, xres):
            sc = 4 * t
            ACT(JUNK[:], xap, AF.Square, [xres], ["junk", ("stat", sc)], accum=STAT[:, sc:sc + 1])
            TS(STAT[:, sc + 1:sc + 2], STAT[:, sc:sc + 1], 1.0 / D, EPS, ALU.mult, ALU.add, [("stat", sc)], [("stat", sc + 1)])
            POW(STAT[:, sc + 2:sc + 3], STAT[:, sc + 1:sc + 2], 1, [("stat", sc + 1)], [("stat", sc + 2)])
            TS(XN[:, t, :], xap, STAT[:, sc + 2:sc + 3], None, ALU.mult, None, [xres, ("stat", sc + 2)], [("xn", t)])

        def transpose_group(tiles, which, dst_res):
            r = 0 if tiles[0] < 16 else 1
            abv = AB[:].rearrange("p (w r k) -> p w r k", w=4, r=2)
            n = len(tiles)
            c0 = tiles[0] * 128
            for k in range(8):
                b = nxt("b", 2)
                for t in range(n):
                    TR(psb[b][:, t * 128:(t + 1) * 128], XN[:, t, k * 128:(k + 1) * 128], [("xn", t)], [PSB[b]])
                ACT(HT[:, k, c0:c0 + n * 128], psb[b][:, 0:n * 128], AF.Identity, [PSB[b], "ab%d" % which], [dst_res],
                    scale=abv[:, 2 * which, r, k:k + 1], bias=abv[:, 2 * which + 1, r, k:k + 1])

        GROUPS = [[0, 1, 2, 3], [4, 5, 6, 7], [8, 9, 10, 11], [12, 13, 14, 15], [16, 17]]

        def premix_gen(l, gsel):
            for g in gsel:
                tiles = GROUPS[g]
                for t, i in enumerate(tiles):
                    s = nxt("x", 3)
                    ap, res = xsrc(l, i)
                    DMA("sp", XT[s][:], ap, [res] if res else [], [("xt", s)], ("xt", s))
                    norm_tile(t, XT[s][:], ("xt", s))
                    yield
                transpose_group(tiles, 0, ("hT", g))
                yield

        def proj_gen(l):
            wv = d_win[l].rearrange("(k p) n -> p k n", p=128)
            coltiles = [(0, 512), (512, 512), (1024, 512), (1536, 512), (2048, 256)]
            blocks = [("qk", 0, 0, 2), ("qk", 2, 2, 2), ("qk", 4, 4, 1),
                      ("conf", 0, 5, 2), ("conf", 1, 7, 2), ("short", 0, 9, 3), ("short", 1, 12, 3), ("v", 0, 15, 1)]

            def load(bi):
                kind, idx, ch0, nch = blocks[bi]
                buf = bi % 2
                DMA("pool", WIN[buf][:, :, 0:nch * 128], wv[:, :, ch0 * 128:(ch0 + nch) * 128], [BIGR], [("win", buf)], ("win", buf))

            def rope_finish(item):
                fmain, tbq, dst, dres, c0, n = item
                f2 = (1 + nxt("f", 5))
                MM(psf[f2][:, 0:n], PERM[:], TMPB[tbq][:, 0:n], True, True, ["perm", ("tmpb", tbq)], [PSF[f2]])
                t1, t2 = nxt("t", 4), nxt("t", 4)
                TT("dve", TMP[t1][:, 0:n], psf[fmain][:, 0:n], ROPE[:, c0:c0 + n], ALU.mult,
                   [PSF[fmain], "rope"], [("tmp", t1)])
                TT("dve", TMP[t2][:, 0:n], psf[f2][:, 0:n], ROPE[:, SEQ + c0:SEQ + c0 + n], ALU.mult,
                   [PSF[f2], "rope"], [("tmp", t2)])
                TT("pool", dst, TMP[t1][:, 0:n], TMP[t2][:, 0:n], ALU.add, [("tmp", t1), ("tmp", t2)], [dres])
            load(0)
            for bi, (kind, idx, ch0, nch) in enumerate(blocks):
                if bi + 1 < len(blocks):
                    load(bi + 1)
                buf = bi % 2
                W = WIN[buf]
                wres = ("win", buf)
                if kind == "v":
                    for i in range(NT):
                        f = (1 + nxt("f", 5))
                        for k in range(8):
                            MM(psf[f][:, 0:128], HT[:, k, i * 128:(i + 1) * 128], W[:, k, 0:128], k == 0, k == 7,
                               [("hT", min(i // 4, 4)), wres], [PSF[f]])
                        CP("dve", VT[:, i, 0:64], psf[f][:, 0:64], [PSF[f]], [("V", i)])
                        ACT(VT[:, i, 192:256], psf[f][:, 64:128], AF.Copy, [PSF[f]], [("V", i)])
                        if i % 3 == 2:
                            yield
                    continue
                if kind == "qk":
                    pend = None
                    for ci in range(nch):
                        chunk = idx + ci
                        isk = chunk == 4
                        for g, (c0, n) in enumerate(coltiles):
                            isctx = g == 4
                            if isctx and l == DEPTH - 1 and not isk:
                                continue
                            f = (1 + nxt("f", 5))
                            for k in range(8):
                                MM(psf[f][:, 0:n], W[:, k, ci * 128:(ci + 1) * 128], HT[:, k, c0:c0 + n], k == 0, k == 7,
                                   [("hT", g), wres], [PSF[f]])
                            dst = KT[:, c0:c0 + n] if isk else QT[:, chunk, c0:c0 + n]
                            dres = ("kT", g) if isk else ("qT", chunk, g)
                            if isctx:
                                ACT(dst, psf[f][:, 0:n], AF.Copy, [PSF[f]], [dres])
                            else:
                                tbq = nxt("tb", 3)
                                ACT(TMPB[tbq][:, 0:n], psf[f][:, 0:n], AF.Copy, [PSF[f]], [("tmpb", tbq)])
                                if pend is not None:
                                    rope_finish(pend)
                                pend = (f, tbq, dst, dres, c0, n)
                            yield
                    if pend is not None:
                        rope_finish(pend)
                    continue
                for g, (c0, n) in enumerate(coltiles):
                    isctx = g == 4
                    if isctx and l == DEPTH - 1:
                        continue
                    banks = []
                    for ci in range(nch):
                        f = (1 + nxt("f", 5))
                        banks.append(f)
                        for k in range(8):
                            MM(psf[f][:, 0:n], W[:, k, ci * 128:(ci + 1) * 128], HT[:, k, c0:c0 + n], k == 0, k == 7,
                               [("hT", g), wres], [PSF[f]])
                    if kind == "conf":
                        t1 = nxt("t", 4)
                        ACT(TMP[t1][:, 0:n], psf[banks[1]][:, 0:n], AF.Sigmoid, [PSF[banks[1]]], [("tmp", t1)])
                        off = 15 + c0 if not isctx else 2093
                        TT("dve", UC[:, idx, off:off + n], psf[banks[0]][:, 0:n], TMP[t1][:, 0:n], ALU.mult,
                           [PSF[banks[0]], ("tmp", t1)], [("uc", g)])
                    elif kind == "short":
                        t1 = nxt("t", 4)
                        ACT(SBS[:, idx, c0:c0 + n], psf[banks[0]][:, 0:n], AF.Copy, [PSF[banks[0]]], [("sbs", g)])
                        ACT(TMP[t1][:, 0:n], psf[banks[2]][:, 0:n], AF.Copy, [PSF[banks[2]]], [("tmp", t1)])
                        off = 1 + c0 if not isctx else 2051
                        TT("dve", PP[:, idx, off:off + n], psf[banks[1]][:, 0:n], TMP[t1][:, 0:n], ALU.mult,
                           [PSF[banks[1]], ("tmp", t1)], [("pp", g)])
                    yield

        def stat_rstd(psS, n, N, rd, tmpi):
            ACT(TMP[tmpi][:, 0:N], psS[:, 0:N], AF.Sqrt, list(rd) + ["const"], [("tmp", tmpi)], scale=1.0 / n, bias=EPSC[:, 0:1])
            RECIP(TMP[tmpi][:, 0:N], TMP[tmpi][:, 0:N], [("tmp", tmpi)], [("tmp", tmpi)])

        def tgeom(T):
            isctx = T == 4
            N = 256 if isctx else 512
            q0 = 2048 if isctx else T * 512
            return isctx, N, q0

        def gnorm_gen(T, ch0, nch, bank, tmps):
            isctx, N, q0 = tgeom(T)
            zb = T % 2
            Z = ZT[zb]
            zr = lambda ch: ("zT", zb, ch)
            for ci in range(nch):
                tb = nxt("tb", 3)
                ACT(TMPB[tb][:, 0:N], Z[:, ch0 + ci, 0:N], AF.Square, [zr(ch0 + ci)], [("tmpb", tb)])
                MM(psf[bank][:, 0:N], ONESB[:, 0:128], TMPB[tb][:, 0:N], ci == 0, ci == nch - 1, ["const", ("tmpb", tb)], [PSF[bank]])
                yield
            tm_, tmr = tmps
            ACT(tm_[:, 0:N], psf[bank][:, 0:N], AF.Ln, [PSF[bank], "const"], [tmr], scale=1.0 / (nch * 128), bias=EPSC[:, 0:1])
            ACT(tm_[:, 0:N], tm_[:, 0:N], AF.Exp, [tmr], [tmr], scale=-0.5)
            yield
            for ci in range(nch):
                STT(Z[:, ch0 + ci, 0:N], Z[:, ch0 + ci, 0:N], L.VEC[:, 32 + ch0 + ci:33 + ch0 + ci], tm_[:, 0:N],
                    ALU.mult, ALU.mult, [zr(ch0 + ci), tmr, L.vecr], [zr(ch0 + ci)])
            yield

        def conf_short_gen(l, T):
            isctx, N, q0 = tgeom(T)
            zb = T % 2
            Z = ZT[zb]
            zr = lambda ch: ("zT", zb, ch)
            uoff = 2093 if isctx else 15 + q0
            cvt = [0, 1]
            tm = 2
            ucr = [("uc", T), ("uc", max(T - 1, 0)), ("uc", min(T + 1, 3) if not isctx else 4)]
            for cc in range(2):
                for k in range(31):
                    MM(psf[6 + cc][:, 0:N], DIAG[:, cc * 31 + k, :], UC[:, cc, uoff - 15 + k:uoff - 15 + k + N], k == 0, k == 30,
                       ucr + [("diag", cc * 31 + k)], [PSF[6 + cc]])
                    if k % 8 == 7:
                        yield
                ACT(TMP[cvt[cc]][:, 0:N], psf[6 + cc][:, 0:N], AF.Identity, [PSF[6 + cc], L.vecr], [("tmp", cvt[cc])], bias=L.VEC[:, 40 + cc:41 + cc])
                yield
            hb = [nxt("tb", 3), nxt("tb", 3)]
            for cc in range(2):
                CP("dve", TMPB[hb[cc]][:, 0:N], TMP[cvt[cc]][:, 0:N], [("tmp", cvt[cc])], [("tmpb", hb[cc])])
            yield
            for cc in range(2):
                MM(psf[6][:, 0:N], ONESB[:, 0:128], TMPB[hb[cc]][:, 0:N], cc == 0, cc == 1, ["const", ("tmpb", hb[cc])], [PSF[6]])
            yield
            TS(TMP[tm][:, 0:N], psf[6][:, 0:N], 1.0 / 256, None, ALU.mult, None, [PSF[6]], [("tmp", tm)])
            yield
            for cc in range(2):
                TT("dve", TMP[cvt[cc]][:, 0:N], TMP[cvt[cc]][:, 0:N], TMP[tm][:, 0:N], ALU.subtract,
                   [("tmp", cvt[cc]), ("tmp", tm)], [("tmp", cvt[cc])])
                ACT(TMPB[hb[cc]][:, 0:N], TMP[cvt[cc]][:, 0:N], AF.Square, [("tmp", cvt[cc])], [("tmpb", hb[cc])])
                yield
            for cc in range(2):
                MM(psf[7][:, 0:N], ONESB[:, 0:128], TMPB[hb[cc]][:, 0:N], cc == 0, cc == 1, ["const", ("tmpb", hb[cc])], [PSF[7]])
            yield
            ACT(TMP[tm][:, 0:N], psf[7][:, 0:N], AF.Ln, [PSF[7], "const"], [("tmp", tm)], scale=1.0 / 256, bias=EPSC[:, 0:1])
            ACT(TMP[tm][:, 0:N], TMP[tm][:, 0:N], AF.Exp, [("tmp", tm)], [("tmp", tm)], scale=-0.5)
            yield
            for cc in range(2):
                TT("dve", TMP[cvt[cc]][:, 0:N], TMP[cvt[cc]][:, 0:N], TMP[tm][:, 0:N], ALU.mult,
                   [("tmp", cvt[cc]), ("tmp", tm)], [("tmp", cvt[cc])])
            yield
            for cc in range(2):
                ACT(TMP[tm][:, 0:N], TMP[cvt[cc]][:, 0:N], AF.Exp, [("tmp", cvt[cc]), "nln"], [("tmp", tm)],
                    scale=NLN[:, cc:cc + 1], bias=NLN[:, 2 + cc:3 + cc])
                ACT(TMP[tm][:, 0:N], TMP[tm][:, 0:N], AF.Ln, [("tmp", tm), "const"], [("tmp", tm)], bias=ONESF[:, 0:1])
                ACT(TMP[tm][:, 0:N], TMP[tm][:, 0:N], AF.Exp, [("tmp", tm)], [("tmp", tm)], scale=-1.0)
                TS(TMP[cvt[cc]][:, 0:N], TMP[cvt[cc]][:, 0:N], L.VEC[:, 42 + cc:43 + cc], L.VEC[:, 44 + cc:45 + cc], ALU.mult, ALU.add,
                   [("tmp", cvt[cc]), L.vecr], [("tmp", cvt[cc])])
                TT("dve", Z[:, 4 + cc, 0:N], TMP[cvt[cc]][:, 0:N], TMP[tm][:, 0:N], ALU.mult,
                   [("tmp", cvt[cc]), ("tmp", tm)], [zr(4 + cc)])
                yield
            yield from gnorm_gen(T, 4, 2, 6, (TMP[tm], ("tmp", tm)))
            poff = 2051 if isctx else 1 + q0
            for cc in range(2):
                t1 = cc
                prd = [("pp", T), ("pp", max(T - 1, 0)), ("pp", min(T + 1, 3) if not isctx else 4), L.vecr]
                TS(TMP[t1][:, 0:N], PP[:, cc, poff - 1:poff - 1 + N], L.VEC[:, 46 + cc * 3:47 + cc * 3], None, ALU.mult, None,
                   prd, [("tmp", t1)])
                for k in (1, 2):
                    STT(TMP[t1][:, 0:N], PP[:, cc, poff - 1 + k:poff - 1 + k + N], L.VEC[:, 46 + cc * 3 + k:47 + cc * 3 + k],
                        TMP[t1][:, 0:N], ALU.mult, ALU.add, prd + [("tmp", t1)], [("tmp", t1)])
                yield
                TT("dve", Z[:, 6 + cc, 0:N], SBS[:, cc, (2048 if isctx else q0):(2048 if isctx else q0) + N], TMP[t1][:, 0:N], ALU.mult,
                   [("sbs", T), ("tmp", t1)], [zr(6 + cc)])
                yield
            yield from gnorm_gen(T, 6, 2, 7, (TMP[tm], ("tmp", tm)))

        def attn_gen(l, T):
            isctx, N, q0 = tgeom(T)
            zb = T % 2
            Z = ZT[zb]
            zr = lambda ch: ("zT", zb, ch)
            keys = [(16, 0, N, None), (17, 0, N, None)]
            if not isctx:
                for j in range(max(0, 4 * T - 1), min(15, 4 * T + 4) + 1):
                    lo = max(4 * T, j - 1)
                    hi = min(4 * T + 3, j + 1)
                    qoff = (lo - 4 * T) * 128
                    n = (hi - lo + 1) * 128
                    moff = (lo - (j - 1)) * 128
                    needm = (lo == j - 1) or (hi == j + 1)
                    keys.append((j, qoff, n, moff if needm else None))
            nk = len(keys)
            for c in range(4):
                for ki in range(nk + 1):
                    if ki < nk:
                        j, qoff, n, moff = keys[ki]
                        kg = min(j // 4, 4)
                        for half in range(2):
                            pr = slice(64 * half, 64 * half + 64)
                            f = half
                            MM(psf[f][:, 0:n], KT[pr, j * 128:(j + 1) * 128], QT[pr, c, q0 + qoff:q0 + qoff + n], True, moff is None,
                               [("kT", kg), ("qT", c, T)], [PSF[f]])
                        for half in range(2):
                            f = half
                            pt = 2 * half + (ki % 2)
                            if moff is not None:
                                msegs = []
                                if moff == 0:
                                    msegs.append((0, 0))
                                if moff + n == 384:
                                    msegs.append((n - 128, 256))
                                for si, (pc, mc) in enumerate(msegs):
                                    MM(psf[f][:, pc:pc + 128], IDENT[:], MASK[:, mc:mc + 128], False, si == len(msegs) - 1,
                                       ["ident", "const"], [PSF[f]])
                            ACT(PT[pt][:, 0:n], psf[f][:, 0:n], AF.Exp, [PSF[f]], [("pt", pt)], scale=0.125)
                    if ki >= 1:
                        pj, pqoff, pn, _ = keys[ki - 1]
                        for half in range(2):
                            pp_ = 2 * half + ((ki - 1) % 2)
                            MM(psf[4 + half][:, pqoff:pqoff + pn], VT[:, pj, half * 128:(half + 1) * 128], PT[pp_][:, 0:pn], ki == 1, ki == nk,
                               [("V", pj), "vones", ("pt", pp_)], [PSF[4 + half]])
                    yield
                for half in range(2):
                    h = c + 4 * half
                    pr = slice(64 * half, 64 * half + 64)
                    dn = slice(64 * (1 - half), 64 * (1 - half) + 64)
                    psO = psf[4 + half]
                    at = ATMP[half]
                    ACT(at[pr, 0:N], psO[dn, 0:N], AF.Ln, [PSF[4 + half], "esc"], [("atmp", half)], bias=ESC[dn, h:h + 1])
                    ACT(at[pr, 0:N], at[pr, 0:N], AF.Exp, [("atmp", half)], [("atmp", half)], scale=-1.0)
                    TT("dve", Z[pr, c, 0:N], psO[pr, 0:N], at[pr, 0:N], ALU.mult, [PSF[4 + half], ("atmp", half)], [zr(c)])
                yield

        def tail_gen(l, T):
            isctx, N, q0 = tgeom(T)
            zb = T % 2
            Z = ZT[zb]
            zr = lambda ch: ("zT", zb, ch)
            yield from gnorm_gen(T, 0, 4, 6, (TMP[3], ("tmp", 3)))
            ntl = N // 128
            tiles = [q0 // 128 + t for t in range(ntl)]
            for t, i in enumerate(tiles):
                for hf in range(2):
                    for k in range(8):
                        MM(psf[2 + hf][:, :], Z[:, k, t * 128:(t + 1) * 128], WOUT[:, k, hf * 512:(hf + 1) * 512], k == 0, k == 7,
                           [zr(k), "wout"], [PSF[2 + hf]])
                    yield
                s = nxt("x", 3)
                postnorm_tile(l, i, s, 0, yb=2)
                yield
                norm_tile(t, XT[s][:], ("xt", s))
                yield
            r = 0 if tiles[0] < 16 else 1
            abv = AB[:].rearrange("p (w r k) -> p w r k", w=4, r=2)
            n = len(tiles)
            c0 = tiles[0] * 128
            for k in range(8):
                b = nxt("b", 2)
                for t in range(n):
                    TR(psb[b][:, t * 128:(t + 1) * 128], XN[:, t, k * 128:(k + 1) * 128], [("xn", t)], [PSB[b]])
                ACT(HT[:, k, c0:c0 + n * 128], psb[b][:, 0:n * 128], AF.Identity, [PSB[b], "ab1"], [("hT", T)],
                    scale=abv[:, 2, r, k:k + 1], bias=abv[:, 3, r, k:k + 1])
                yield

        def chain(*gens):
            for g in gens:
                if g is not None:
                    yield from g

        def run_merged(gens):
            gens = [g for g in gens if g is not None]
            while gens:
                for g in list(gens):
                    try:
                        next(g)
                    except StopIteration:
                        gens.remove(g)

        def run_balanced(gens):
            st_ = [[g, 0, float(n)] for (g, n) in gens if g is not None]
            while st_:
                st_.sort(key=lambda x: x[1] / x[2])
                g = st_[0]
                try:
                    next(g[0])
                    g[1] += 1
                except StopIteration:
                    st_.remove(g)

        def mixer_all(l, nT, kstop=99):
            run_merged([conf_short_gen(l, 0)])
            run_merged([attn_gen(l, 0), conf_short_gen(l, 1)])
            for T in range(nT):
                if T == nT - 1:
                    return tail_gen(l, T)
                A = attn_gen(l, T + 1) if T + 1 < nT else None
                hasC = T + 2 < nT
                B = chain(tail_gen(l, T), conf_short_gen(l, T + 2) if hasC else None)
                run_balanced([(A, 40 if T + 1 < 4 else 12), (B, 60 if hasC else 30)])

        def postnorm_tile(l, i, s, w, final=False, yb=4):
            r = 0 if i < 16 else 1
            sc = 16 + 4 * (i % 4)
            if w == 0:
                ap, res = xsrc(l, i)
            else:
                ap, res = d_xs[i * 128:(i + 1) * 128, :], ("dxs", i)
            DMA("sp", XT[s][:], ap, [res] if res else [], [("xt", s)], ("xt", s))
            for hf in range(2):
                ACT(JUNK[:, 0:512], psf[yb + hf][:, :], AF.Square, [PSF[yb + hf]], ["junk", ("stat", sc + hf)],
                    accum=STAT[:, sc + hf:sc + hf + 1])
            TT("dve", STAT[:, sc + 2:sc + 3], STAT[:, sc:sc + 1], STAT[:, sc + 1:sc + 2], ALU.add,
               [("stat", sc), ("stat", sc + 1)], [("stat", sc + 2)])
            TS(STAT[:, sc + 2:sc + 3], STAT[:, sc + 2:sc + 3], 1.0 / D, EPS, ALU.mult, ALU.add, [("stat", sc + 2)], [("stat", sc + 2)])
            POW(STAT[:, sc + 3:sc + 4], STAT[:, sc + 2:sc + 3], 1, [("stat", sc + 2)], [("stat", sc + 3)])
            for hf in range(2):
                t1 = nxt("t", 4)
                STT(TMP[t1][:, :], psf[yb + hf][:, :], STAT[:, sc + 3:sc + 4], GB[:, r, hf * 512:(hf + 1) * 512],
                    ALU.mult, ALU.mult, [PSF[yb + hf], ("stat", sc + 3), ("gb", r)], [("tmp", t1)])
                TT("pool", XT[s][:, hf * 512:(hf + 1) * 512], XT[s][:, hf * 512:(hf + 1) * 512], TMP[t1][:, :], ALU.add,
                   [("xt", s), ("tmp", t1)], [("xt", s)])
            if final:
                if i < 16:
                    op = DMA("sp", d_out[i * 128:(i + 1) * 128, :], XT[s][:], [("xt", s)], [("dout", i)], ("xt", s))
                    out_dmas.append(op)
            else:
                DMA("sp", d_xs[i * 128:(i + 1) * 128, :], XT[s][:], [("xt", s)], [("dxs", i)], ("xt", s))

        def mixer_setup(l):
            TS(NLN[:, 0:4], L.VEC[:, 42:46], -1.0, None, ALU.mult, None, [L.vecr], ["nln"])
            DMA("pool", ROPE, d_rope, [BIGR], ["rope"], "rope")
            MEMSET("pool", UC[:, :, :], 0.0, [BIGR, ("uc", 0), ("uc", 1), ("uc", 2), ("uc", 3), ("uc", 4)])
            MEMSET("pool", PP[:, :, :], 0.0, [BIGR, ("pp", 0), ("pp", 1), ("pp", 2), ("pp", 3), ("pp", 4)])
            MEMSET("pool", VT[:, :, 64:192], 1.0, ["vones"])
            for cc in range(2):
                for k in range(31):
                    TS(DIAG[:, cc * 31 + k, :], IDENT[:], L.VEC[:, 52 + cc * 31 + k:53 + cc * 31 + k], None, ALU.mult, None,
                       ["ident", L.vecr], [("diag", cc * 31 + k)])

        def load_wout(l):
            wv = d_wout[l].rearrange("(k p) n -> p k n", p=128)
            DMA("pool", WOUT[:, :, :], wv, [("win", 0), ("win", 1)], ["wout", ("win", 0), ("win", 1)], "wout")

        def ffn_geom(l, hi_):
            last = l == DEPTH - 1
            ta, tb_ = [(0, 9), (9, 16 if last else 18)][hi_]
            tok_lo, tok_hi = ta * 128, min(tb_, 16) * 128
            segs = []
            ulo = max(tok_lo - 1, 0)
            uhi = min(tok_hi + 1, SEQ)
            for (c0, n) in split_cols(ulo, uhi):
                segs.append((c0, n, c0 - tok_lo + 1))
            nx = tok_hi - tok_lo
            ctxcol = None
            if tb_ > 16:
                ctxcol = nx + 3
                segs.append((2048, 256, ctxcol))
            rowlen = (ctxcol + 256 + 1) if ctxcol is not None else nx + 2
            return ta, tb_, segs, nx, ctxcol, rowlen

        FF_EARLY = [("diag", i) for i in range(62)] + ["vones", "rope", ("atmp", 0), ("atmp", 1)] + [("pt", i) for i in range(4)] + \
                   [("uc", g) for g in range(5)] + [("pp", g) for g in range(5)] + \
                   [("sbs", g) for g in range(5)] + [("kT", g) for g in range(5)] + [("V", i) for i in range(NT)] + \
                   [("qT", c, g) for c in range(4) for g in range(5)]
        FF_LATE = [BIGR, "wout", ("win", 0), ("win", 1)] + [("zT", i, ch) for i in range(2) for ch in range(8)]

        def ffn_fence_a():
            S.add("pool", lambda e: e.memset(STAT[:, 60:61], 0.0), FF_EARLY, ["ffnfence"] + FF_EARLY)

        def ffn_fence_b():
            S.add("pool", lambda e: e.memset(STAT[:, 62:63], 0.0), FF_LATE + ["ffnfence"], ["ffnfenceB"] + FF_LATE)

        def ffn_up_gen(l, hi_, banks=(0, 1, 2, 3), order=None):
            wuv = d_wup[l].rearrange("(k p) n -> p k n", p=128)
            ta, tb_, segs, nx, ctxcol, rowlen = ffn_geom(l, hi_)
            for i in range(4):
                S.add("pool", lambda e, a=UROW[i]: e.memset(a[:, :], 0.0), ["ffnfence"], [("urow", i)])

            order = list(range(NFF)) if order is None else list(order)

            def load_wup(p):
                b_ = p % 3
                jj_ = order[p]
                DMA("pool", WUP[b_][:, :, :], wuv[:, :, jj_ * 256:(jj_ + 1) * 256], ["ffnfence"], [("wup", b_)], ("wup", b_))
            load_wup(0)
            load_wup(1)
            for p, j in enumerate(order):
                buf = p % 3
                if p + 2 < NFF:
                    load_wup(p + 2)
                if hi_ == 0 and p in (1, 3):
                    wdv = d_wdn[l].rearrange("(k p) n -> p k n", p=128)
                    h0 = 0 if p == 1 else 11
                    DMA("pool", WDN[:, h0:h0 + 11, :], wdv[:, h0:h0 + 11, :], ["ffnfence"], ["wdn"], "wdn%d" % (p // 2))
                ub = (p % 2) * 2
                L_ = rowlen - 2
                for gv in range(2):
                    ur = ub + gv
                    for (c0, n, col) in segs:
                        f = banks[nxt("f", 4)]
                        gs = sorted(set([min(c0 // 512, 4), min((c0 + n - 1) // 512, 4)]))
                        for k in range(8):
                            MM(psf[f][:, 0:n], WUP[buf][:, k, gv * 128:(gv + 1) * 128], HT[:, k, c0:c0 + n], k == 0, k == 7,
                               [("hT", g) for g in gs] + [("wup", buf)], [PSF[f]])
                        ACT(UROW[ur][:, col:col + n], psf[f][:, 0:n], AF.Copy, [PSF[f]], [("urow", ur)])
                    wc = 114 + (gv * NFF + j) * 3
                    TS(CROW[ur][:, 1:1 + L_], UROW[ur][:, 0:L_], L.VEC[:, wc:wc + 1], None, ALU.mult, None,
                       [("urow", ur), L.vecr], [("crow", ur)])
                    for k in (1, 2):
                        STT(CROW[ur][:, 1:1 + L_], UROW[ur][:, k:k + L_], L.VEC[:, wc + k:wc + k + 1], CROW[ur][:, 1:1 + L_],
                            ALU.mult, ALU.add, [("urow", ur), ("crow", ur), L.vecr], [("crow", ur)])
                    yield
                ACT(CROW[ub][:, 1:1 + L_], CROW[ub][:, 1:1 + L_], AF.Silu, [("crow", ub)], [("crow", ub)])
                TT("pool", GT[:, j, 0:nx], CROW[ub][:, 1:1 + nx], CROW[ub + 1][:, 1:1 + nx], ALU.mult,
                   [("crow", ub), ("crow", ub + 1)] + (["ffnfenceB"] if j < 15 else []), [("gT", j)])
                if ctxcol is not None:
                    TT("pool", GT[:, j, nx:nx + 256], CROW[ub][:, ctxcol:ctxcol + 256], CROW[ub + 1][:, ctxcol:ctxcol + 256], ALU.mult,
                       [("crow", ub), ("crow", ub + 1)] + (["ffnfenceB"] if j < 15 else []), [("gT", j)])
                yield

        def ffn_down(l, hi_):
            last = l == DEPTH - 1
            ta, tb_, segs, nx, ctxcol, rowlen = ffn_geom(l, hi_)
            for t, i in enumerate(range(ta, tb_)):
                yb = 4 if t % 2 == 0 else 2
                for hf in range(2):
                    for k in range(NFF):
                        MM(psf[yb + hf][:, :], GT[:, k, t * 128:(t + 1) * 128], WDN[:, k, hf * 512:(hf + 1) * 512], k == 0, k == NFF - 1,
                           [("gT", k), "wdn"], [PSF[yb + hf]])
                s = nxt("x", 3)
                postnorm_tile(l, i, s, 1, final=last, yb=yb)
                g = i // 4
                if not last and g != 2:
                    norm_tile(i % 4, XT[s][:], ("xt", s))
                    if i == GROUPS[g][-1]:
                        transpose_group(GROUPS[g], 0, ("hT", g))

        def ffn_end(l):
            ffnres = ["wdn"] + [("wup", i) for i in range(3)] + [("urow", i) for i in range(4)] + \
                     [("crow", i) for i in range(4)] + [("gT", j) for j in range(NFF)]
            S.add("pool", lambda e: e.memset(STAT[:, 61:62], 0.0), ffnres + [BIGR], ffnres + [BIGR])

        kstop = int(os.environ.get("KSTOP", "99"))
        XNW = [XN[:, 0:2, :].rearrange("p a b -> p (a b)").rearrange("p (k n) -> p k n", n=256),
               XN[:, 2:4, :].rearrange("p a b -> p (a b)").rearrange("p (k n) -> p k n", n=256)]
        XNWR = [[("xn", 0), ("xn", 1)], [("xn", 2), ("xn", 3)]]
        for l in range(DEPTH):
            S.epoch = l
            L.VEC = VECS[l % 2]
            L.vecr = ("vec", l % 2)
            if l == 0:
                mg = mod_gen(0, [ZT[0], ZT[1]], [[("zT", 0, ch) for ch in range(8)], [("zT", 1, ch) for ch in range(8)]], 0, 512)
                import itertools
                for _ in range(5):
                    next(mg)
                run_merged([itertools.islice(mg, 3), premix_gen(0, [0, 1, 2, 3, 4])])

            else:
                run_merged([premix_gen(l, [2])])
            if kstop <= 0:
                break
            grows(0, l)
            if kstop <= 1:
                break
            mixer_setup(l)
            if l == 0:
                run_balanced([(proj_gen(l), 70), (mg, 7)])
            else:
                run_merged([proj_gen(l)])
            load_wout(l)
            if kstop <= 2:
                break
            last_tail = mixer_all(l, 5 if l < DEPTH - 1 else 4, kstop)
            if kstop <= 4:
                run_merged([last_tail])
                break
            ffn_fence_a()
            import itertools
            up0 = ffn_up_gen(l, 0, banks=(0, 1, 4, 5), order=list(range(15, NFF)) + list(range(15)))
            run_balanced([(last_tail, 30), (itertools.islice(up0, 21), 21)])
            ffn_fence_b()
            run_balanced([(mod_gen(l + 1, XNW, XNWR, 6, 256) if l + 1 < DEPTH else None, 26), (up0, 45)])
            grows(1, l)
            up1 = ffn_up_gen(l, 1)
            next(up1)
            ffn_down(l, 0)
            run_merged([up1])
            ffn_down(l, 1)
            ffn_end(l)
            if kstop <= 5:
                break
        S.emit(final_waits=out_dmas)
    return nc


def rope_tables():
    t = np.arange(SEQ)
    pos = np.stack([t // 64, t % 64], 0).astype(np.float32)
    inv = (10000.0 ** (-np.arange(16, dtype=np.float32) / 16)).astype(np.float32)
    cos = np.zeros((128, SEQ), np.float32)
    sin = np.zeros((128, SEQ), np.float32)
    for p in range(128):
        d = p % 64
        r, idx = d // 32, d % 32
        ang = pos[r] * inv[idx % 16]
        cos[p] = np.cos(ang)
        sin[p] = np.sin(ang) * (-1.0 if idx < 16 else 1.0)
    return np.concatenate([cos, sin], 1)


def col_layout(v):
    return np.ascontiguousarray(v.reshape(-1, 128).T)


def rot_partner(d):
    r, idx = d // 32, d % 32
    return r * 32 + (idx + 16 if idx < 16 else idx - 16)


def win_perm():
    out = []
    for c in range(4):
        out.append(np.concatenate([np.arange(c * 64, c * 64 + 64), np.arange((c + 4) * 64, (c + 4) * 64 + 64)]))
    out.append(512 + np.arange(128))
    cv0, cg0, sb0, scg0, su0 = 768, 1024, 1280, 1536, 1792
    for cc in range(2):
        out += [cv0 + cc * 128 + np.arange(128), cg0 + cc * 128 + np.arange(128)]
    for cc in range(2):
        out += [sb0 + cc * 128 + np.arange(128), scg0 + cc * 128 + np.arange(128), su0 + cc * 128 + np.arange(128)]
    out += [640 + np.arange(128)]
    return np.concatenate(out)


def perm_matrix():
    pm = np.zeros((128, 128), np.float32)
    for m in range(128):
        k = (m // 64) * 64 + rot_partner(m % 64)
        pm[k, m] = 1.0
    return pm


def attn_chan_perm():
    idx = []
    for c in range(4):
        idx += list(range(c * 64, c * 64 + 64)) + list(range((c + 4) * 64, (c + 4) * 64 + 64))
    idx += list(range(512, 1024))
    return np.array(idx)


_CACHE = {}


def prepare_inputs(inputs):
    f = lambda a: np.ascontiguousarray(np.asarray(a, dtype=np.float32))
    x, c, ctx, c_ctx = f(inputs["x"]), f(inputs["c"]), f(inputs["ctx"]), f(inputs["c_ctx"])
    rope = rope_tables()
    kk = np.arange(128)[:, None]
    qq = np.arange(128)[None, :]
    mask = np.zeros((128, 384), np.float32)
    mask[:, 0:128] = np.where(kk <= qq, 0.0, -30000.0)
    mask[:, 256:384] = np.where(qq <= kk, 0.0, -30000.0)
    perm = win_perm()
    zperm = attn_chan_perm()
    shared = {"rope": rope, "mask": mask, "perm": perm_matrix()}
    for l in range(DEPTH):
        shared["wada%d" % l] = f(inputs["w_ada"][l])
        vec = np.zeros((128, NV), np.float32)
        vec[:, 0:8] = col_layout(f(inputs["g_pre_mix"][l]))
        vec[:, 8:16] = col_layout(f(inputs["g_post_mix"][l]))
        vec[:, 16:24] = col_layout(f(inputs["g_pre_ffn"][l]))
        vec[:, 24:32] = col_layout(f(inputs["g_post_ffn"][l]))
        vec[:, 32:40] = col_layout(f(inputs["g_group"][l])[zperm])
        vec[:, 40:42] = col_layout(f(inputs["b_conf_dw"][l]))
        vec[:, 42:44] = col_layout(f(inputs["conf_ln_g"][l]))
        vec[:, 44:46] = col_layout(f(inputs["conf_ln_b"][l]))
        wsc = f(inputs["w_sc_dw"][l])
        wcf = f(inputs["w_conf_dw"][l])
        wff = f(inputs["w_ffn_dw"][l])
        for cc in range(2):
            vec[:, 46 + cc * 3:49 + cc * 3] = wsc[:, cc * 128:(cc + 1) * 128].T
            vec[:, 52 + cc * 31:83 + cc * 31] = wcf[:, cc * 128:(cc + 1) * 128].T
        for ch in range(44):
            vec[:, 114 + ch * 3:117 + ch * 3] = wff[:, ch * 128:(ch + 1) * 128].T
        ba = col_layout(f(inputs["b_ada"][l]))
        vec[:, 246:342] = np.repeat(ba, 2, axis=1)
        shared["vec%d" % l] = vec
        shared["win%d" % l] = np.ascontiguousarray(f(inputs["w_in"][l])[:, perm])
        shared["sink%d" % l] = np.ascontiguousarray(np.broadcast_to(f(inputs["sink"][l])[None, :], (128, 8)))
        shared["wout%d" % l] = np.ascontiguousarray(f(inputs["w_out"][l])[zperm, :])
        wu = f(inputs["w_up"][l])
        shared["wup%d" % l] = np.ascontiguousarray(
            np.stack([wu[:, :DFF].reshape(D, NFF, 128), wu[:, DFF:].reshape(D, NFF, 128)], axis=2).reshape(D, 2 * DFF))
        shared["wdn%d" % l] = f(inputs["w_down"][l])
    in_maps = []
    for b in range(8):
        m = dict(shared)
        m["x"] = x[b]
        m["ctx"] = ctx[b]
        cc = np.zeros((128, 16), np.float32)
        cc[:, 0::2] = col_layout(c[b])
        cc[:, 1::2] = col_layout(c_ctx)
        m["cc"] = cc
        in_maps.append(m)
    return in_maps


def kernel(**inputs):
    in_maps = prepare_inputs(inputs)
    if "nc" not in _CACHE:
        _CACHE["nc"] = build_program()
    nc = _CACHE["nc"]
    res = run_bass_kernel_spmd(nc, in_maps, core_ids=list(range(8)))
    out = np.stack([np.asarray(r["out"], dtype=np.float32) for r in res.results], 0)
    return out
```

```python
import contextlib
import os
import numpy as np
import ml_dtypes
import concourse.bass as bass
import concourse.mybir as mybir
from concourse.bass_utils import run_bass_kernel_spmd

F32 = mybir.dt.float32
BF16 = mybir.dt.bfloat16
AF = mybir.ActivationFunctionType
ALU = mybir.AluOpType

D = 1024
SEQ = 2048
CTX = 256
NT = 18
NTOK = NT * 128
DEPTH = 2
DFF = 2816
NFF = 22
EPS = 1e-6
NCW = 16
NV = 342


class Op:
    __slots__ = ("eng", "fn", "deps", "sem", "ticket", "signal", "ninc", "name")


class Sched:
    ENGS = ("pe", "act", "dve", "pool", "sp")

    def __init__(self, nc):
        self.nc = nc
        self.ops = {e: [] for e in self.ENGS}
        self.lastw = {}
        self.readers = {}
        self.dma_count = {}
        self.all_ops = []
        self.epoch = 0

    def add(self, eng, fn, reads=(), writes=(), dma_key=None, ndma=1, name=""):
        op = Op()
        op.eng, op.fn, op.name = eng, fn, name
        op.signal = False
        deps = set()
        excl = [r for r in reads if isinstance(r, str) and r.startswith("ps:")]
        reads = [r for r in reads if r not in excl]
        writes = list(writes) + excl
        for r in reads:
            lw = self.lastw.get(r)
            if lw is not None:
                deps.add(lw)
        for w in writes:
            lw = self.lastw.get(w)
            if lw is not None:
                deps.add(lw)
            lastrd = {}
            for rd in self.readers.get(w, ()):
                if rd.sem[0] == "dma":
                    deps.add(rd)
                else:
                    lastrd[rd.eng] = rd
            deps.update(lastrd.values())
        for r in reads:
            self.readers.setdefault(r, []).append(op)
        for w in writes:
            self.lastw[w] = op
            self.readers[w] = []
        deps.discard(op)
        if eng == "pe":
            deps = set(d for d in deps if d.eng != "pe")
        op.deps = deps
        if dma_key is not None:
            op.sem = ("dma", dma_key)
            self.dma_count[dma_key] = self.dma_count.get(dma_key, 0) + 16 * ndma
            op.ticket = self.dma_count[dma_key]
            op.ninc = ndma
            op.signal = True
        else:
            op.sem = ("eng", eng, self.epoch)
            op.ticket = None
            op.ninc = 0
        for d in deps:
            d.signal = True
        self.ops[eng].append(op)
        self.all_ops.append(op)
        return op

    def emit(self, final_waits=()):
        nc = self.nc
        cnts = {}
        for e in self.ENGS:
            for op in self.ops[e]:
                if op.sem[0] == "eng" and op.signal:
                    cnts[op.sem] = cnts.get(op.sem, 0) + 1
                    op.ticket = cnts[op.sem]
        with contextlib.ExitStack() as st:
            sems = {}
            for i, k in enumerate(cnts):
                sems[k] = st.enter_context(nc.semaphore("s%d" % i))
            for i, k in enumerate(self.dma_count):
                sems[("dma", k)] = st.enter_context(nc.semaphore("d%d" % i))
            block = st.enter_context(nc.Block())

            def run(engname, eng):
                waited = {}
                for op in self.ops[engname]:
                    need = {}
                    for d in op.deps:
                        if need.get(d.sem, 0) < d.ticket:
                            need[d.sem] = d.ticket
                    for s, v in need.items():
                        if waited.get(s, 0) < v:
                            eng.wait_ge(sems[s], v)
                            waited[s] = v
                    ins = op.fn(eng)
                    if op.sem[0] == "dma":
                        if not isinstance(ins, (list, tuple)):
                            ins = [ins]
                        assert len(ins) == op.ninc, (op.name, len(ins), op.ninc)
                        for i in ins:
                            i.then_inc(sems[op.sem], 16)
                    elif op.signal:
                        ins.then_inc(sems[op.sem], 1)
                if engname == "sp":
                    for op in final_waits:
                        eng.wait_ge(sems[op.sem], op.ticket)

            block.tensor(lambda eng: run("pe", eng))
            block.scalar(lambda eng: run("act", eng))
            block.vector(lambda eng: run("dve", eng))
            block.gpsimd(lambda eng: run("pool", eng))
            block.sync(lambda eng: run("sp", eng))


def split_cols(lo, hi, mx=512):
    out = []
    while lo < hi:
        n = min(mx, hi - lo)
        out.append((lo, n))
        lo += n
    return out


def build_program(stop_after=None):
    nc = bass.Bass("TRN2", target_bir_lowering=False)
    dram = lambda name, shape, dt=F32, kind="ExternalInput": nc.dram_tensor(name, shape, dt, kind=kind).ap()
    d_x = dram("x", [SEQ, D])
    d_ctx = dram("ctx", [CTX, D])
    d_cc = dram("cc", [128, 16])
    d_rope = dram("rope", [128, 2 * SEQ])
    d_mask = dram("mask", [128, 384])
    d_perm = dram("perm", [128, 128])
    d_wada = [dram("wada%d" % l, [D, 6 * D]) for l in range(DEPTH)]
    d_vec = [dram("vec%d" % l, [128, NV]) for l in range(DEPTH)]
    d_win = [dram("win%d" % l, [D, NCW * 128]) for l in range(DEPTH)]
    d_sink = [dram("sink%d" % l, [128, 8]) for l in range(DEPTH)]
    d_wout = [dram("wout%d" % l, [D, D]) for l in range(DEPTH)]
    d_wup = [dram("wup%d" % l, [D, 2 * DFF]) for l in range(DEPTH)]
    d_wdn = [dram("wdn%d" % l, [DFF, D]) for l in range(DEPTH)]
    d_out = dram("out", [SEQ, D], kind="ExternalOutput")
    d_xs = dram("xs_scratch", [NTOK, D], kind="Internal")

    st = contextlib.ExitStack()
    with st:
        sb = lambda name, shape, dt: st.enter_context(nc.sbuf_tensor(name, shape, dt))
        S = Sched(nc)

        BIG = sb("BIG", [128, 63296], BF16)
        HT = sb("HT", [128, 8, NTOK], BF16)
        GB = sb("GB", [128, 2, D], F32)
        XT = [sb("XT%d" % i, [128, D], F32) for i in range(3)]
        XN = sb("XN", [128, 4, D], BF16)
        JUNK = sb("JUNK", [128, D], BF16)
        TMP = [sb("TMP%d" % i, [128, 512], F32) for i in range(4)]
        TMPB = [sb("TMPB%d" % i, [128, 512], BF16) for i in range(3)]
        IDENT = sb("IDENT", [128, 128], BF16)
        IDENTF = sb("IDENTF", [128, 128], F32)
        ONESF = sb("ONESF", [128, 128], F32)
        ONESB = sb("ONESB", [128, 128], BF16)
        NHALF = sb("NHALF", [128, 8], F32)
        MASK = sb("MASK", [128, 384], BF16)
        PERM = sb("PERM", [128, 128], BF16)
        VECS = [sb("VEC%d" % i, [128, NV], F32) for i in range(2)]

        class L:
            VEC = None
            vecr = None
        CCF = sb("CCF", [128, 16], F32)
        CCB = sb("CCB", [128, 8, 2], BF16)
        MODC = sb("MODC", [128, 96], F32)
        AB = sb("AB", [128, 64], F32)
        GCS = [sb("GC%d" % i, [128, 32], F32) for i in range(2)]
        DIAGF = sb("DIAGF", [128, 128], F32)
        STAT = sb("STAT", [128, 64], F32)
        ESC = sb("ESC", [128, 8], F32)
        EPSC = sb("EPSC", [128, 1], F32)
        NLN = sb("NLN", [128, 4], F32)

        psf = [st.enter_context(nc.psum_tensor("psf%d" % i, [128, 512], F32)) for i in range(8)]
        psb = [psf[6 + i][:, :].bitcast(BF16) for i in range(2)]
        PSF = ["ps:f%d" % i for i in range(8)]
        PSB = [PSF[6], PSF[7]]

        def carve(off, shape):
            n = int(np.prod(shape))
            ap = BIG[:, off:off + n]
            if len(shape) == 2:
                ap = ap.rearrange("p (a b) -> p a b", b=shape[1])
            return ap, off + n
        o = 0
        QT, o = carve(o, [4, NTOK])
        KT, o = carve(o, [1, NTOK])
        KT = BIG[:, 9216:9216 + NTOK]
        VT, o = carve(o, [NT, 256])
        UCW = 2364
        UC, o = carve(o, [2, UCW])
        PW = 2308
        PP, o = carve(o, [2, PW])
        SBS, o = carve(o, [2, NTOK])
        DIAG, o = carve(o, [62, 128])
        WBUF_OFF = o
        WIN = []
        for i in range(2):
            w, o = carve(o, [8, 384]); WIN.append(w)
        WOUT, _ = carve(WBUF_OFF, [8, D])
        o = WBUF_OFF + 8 * D
        WADA = [WIN[0], WIN[1]]
        ZT = []
        for i in range(2):
            z, o = carve(o, [8, 512]); ZT.append(z)
        PT = []
        for i in range(4):
            p_, o = carve(o, [1, 512]); PT.append(BIG[:, o - 512:o])
        ROPE = BIG[:, o:o + 2 * SEQ]
        o += 2 * SEQ
        ATMP = []
        for i in range(2):
            ATMP.append(BIG[:, o:o + 1024].bitcast(F32))
            o += 1024
        assert o <= 63296, o
        o = 0
        WUP = []
        for i in range(3):
            w, o = carve(o, [8, 256]); WUP.append(w)
        UROW = []
        URW = 1160
        for i in range(4):
            u, o = carve(o, [1, URW]); UROW.append(BIG[:, o - URW:o])
        CROW = []
        for i in range(4):
            u, o = carve(o, [1, URW]); CROW.append(BIG[:, o - URW:o])
        WDN, o = carve(o, [NFF, D])
        assert o <= 38016, o
        GT, o = carve(o, [NFF, 1152])
        assert o <= 63296, o
        BIGR = "BIG"

        def MM(out, lhsT, rhs, start, stop, rd, wr):
            return S.add("pe", lambda e, a=(out, lhsT, rhs, start, stop): e.matmul(
                a[0], a[1], a[2], start=a[3], stop=a[4], skip_group_check=True), rd, wr)

        def TR(out, in_, rd, wr):
            return S.add("pe", lambda e, a=(out, in_): e.transpose(a[0], a[1], IDENT[:]), list(rd) + ["ident"], wr)

        def ACT(out, in_, func, rd, wr, scale=None, bias=None, accum=None):
            kw = {}
            if scale is not None:
                kw["scale"] = scale
            if bias is not None:
                kw["bias"] = bias
            if accum is not None:
                kw["accum_out"] = accum
            return S.add("act", lambda e, a=(out, in_, func, kw): e.activation(out=a[0], in_=a[1], func=a[2], **a[3]), rd, wr)

        def TT(eng, out, in0, in1, op, rd, wr):
            return S.add(eng, lambda e, a=(out, in0, in1, op): e.tensor_tensor(out=a[0], in0=a[1], in1=a[2], op=a[3]), rd, wr)

        def TS(out, in0, s1, s2, op0, op1, rd, wr):
            if op1 is None:
                return S.add("dve", lambda e, a=(out, in0, s1, op0): e.tensor_scalar(
                    out=a[0], in0=a[1], scalar1=a[2], scalar2=None, op0=a[3]), rd, wr)
            return S.add("dve", lambda e, a=(out, in0, s1, s2, op0, op1): e.tensor_scalar(
                out=a[0], in0=a[1], scalar1=a[2], scalar2=a[3], op0=a[4], op1=a[5]), rd, wr)

        def STT(out, in0, scalar, in1, op0, op1, rd, wr):
            return S.add("dve", lambda e, a=(out, in0, scalar, in1, op0, op1): e.scalar_tensor_tensor(
                out=a[0], in0=a[1], scalar=a[2], in1=a[3], op0=a[4], op1=a[5]), rd, wr)

        def CP(eng, out, in_, rd, wr):
            return S.add(eng, lambda e, a=(out, in_): e.tensor_copy(a[0], a[1]), rd, wr)

        def RECIP(out, in_, rd, wr):
            return S.add("dve", lambda e, a=(out, in_): e.reciprocal(a[0], a[1]), rd, wr)

        def MEMSET(eng, ap, val, wr):
            return S.add(eng, lambda e, a=(ap, val): e.memset(a[0], a[1]), [], wr)

        def DMA(q, out, in_, rd, wr, key):
            return S.add(q, lambda e, a=(out, in_): e.dma_start(out=a[0], in_=a[1]), rd, wr, dma_key=key)

        def POW(out, in_, n, rd, wr):
            return TT("pool", out, in_, NHALF[:, 0:n], ALU.pow, list(rd) + ["const"], wr)

        rr = {"f": 0, "b": 0, "x": 0, "t": 0, "tb": 0, "pt": 0}

        def nxt(kind, n):
            v = rr[kind] % n
            rr[kind] = (v + 1) % n
            return v

        MEMSET("pool", IDENT[:], 0.0, ["ident"])
        S.add("pool", lambda e: e.affine_select(out=IDENT[:], in_=IDENT[:], pattern=[[-1, 128]], compare_op=ALU.not_equal,
                                                fill=1.0, base=0, channel_multiplier=1), ["ident"], ["ident"])
        MEMSET("pool", IDENTF[:], 0.0, ["identf"])
        S.add("pool", lambda e: e.affine_select(out=IDENTF[:], in_=IDENTF[:], pattern=[[-1, 128]], compare_op=ALU.not_equal,
                                                fill=1.0, base=0, channel_multiplier=1), ["identf"], ["identf"])
        MEMSET("pool", ONESF[:], 1.0, ["const"])
        MEMSET("pool", ONESB[:], 1.0, ["const"])
        MEMSET("pool", NHALF[:], -0.5, ["const"])
        MEMSET("pool", EPSC[:], EPS, ["const"])
        DMA("pool", MASK[:], d_mask, [], ["const"], "mask")
        DMA("pool", PERM[:], d_perm, [], ["perm"], "perm")
        DMA("sp", CCF[:], d_cc, [], ["ccf"], "ccf")
        ACT(CCB[:].rearrange("p k r -> p (k r)"), CCF[:], AF.Silu, ["ccf"], ["ccb"])

        out_dmas = []

        def xsrc(l, i):
            if l == 0:
                if i < 16:
                    return d_x[i * 128:(i + 1) * 128, :], None
                return d_ctx[(i - 16) * 128:(i - 15) * 128, :], None
            return d_xs[i * 128:(i + 1) * 128, :], ("dxs", i)

        def mod_gen(l, bufs, bufres, bank, bw):
            VEC = VECS[l % 2]
            vecr = ("vec", l % 2)
            DMA("sp", VEC[:], d_vec[l], [], [vecr], ("vec", l % 2))
            DMA("sp", ESC[:], d_sink[l], [], ["esc"], "esc")
            ACT(ESC[:], ESC[:], AF.Exp, ["esc"], ["esc"])
            wv = d_wada[l].rearrange("(k p) n -> p k n", p=128)
            psM = psf[bank]
            mv = MODC[:].rearrange("p (j r) -> p j r", r=2)
            abv = AB[:].rearrange("p (w r k) -> p w r k", w=4, r=2)
            gcv = GCS[l % 2][:].rearrange("p (w r k) -> p w r k", w=2, r=2)
            nblk = 6 * D // bw
            cpb = bw // 128

            def load_blk(blk):
                b_ = blk % 2
                DMA("pool", bufs[b_], wv[:, :, blk * bw:(blk + 1) * bw], list(bufres[b_]), list(bufres[b_]), ("wada", b_))
            load_blk(0)
            for blk in range(nblk):
                buf = blk % 2
                if blk + 1 < nblk:
                    load_blk(blk + 1)
                for jj in range(cpb):
                    j = blk * cpb + jj
                    for k in range(8):
                        MM(psM[:, 2 * j:2 * j + 2], bufs[buf][:, k, jj * 128:(jj + 1) * 128], CCB[:, k, :],
                           k == 0, k == 7, list(bufres[buf]) + ["ccb"], [PSF[bank]])
                yield
                if blk == 16 // cpb - 1:
                    TT("dve", MODC[:, 0:32], psM[:, 0:32], VEC[:, 246:278], ALU.add, [PSF[bank], vecr], ["modc0"])
                    for r in range(2):
                        STT(abv[:, 0, r, :], mv[:, 8:16, r], 1.0, VEC[:, 0:8], ALU.add, ALU.mult, ["modc0", vecr], ["ab0"])
                        CP("dve", abv[:, 1, r, :], mv[:, 0:8, r], ["modc0"], ["ab0"])
                    yield
                if blk == 24 // cpb - 1:
                    TT("dve", MODC[:, 32:48], psM[:, 32:48], VEC[:, 278:294], ALU.add, [PSF[bank], vecr], ["modc1"])
                    for r in range(2):
                        TT("dve", gcv[:, 0, r, :], mv[:, 16:24, r], VEC[:, 8:16], ALU.mult, ["modc1", vecr], [("gc", l % 2, 0)])
                    yield
            TT("dve", MODC[:, 48:96], psM[:, 48:96], VEC[:, 294:342], ALU.add, [PSF[bank], vecr], ["modc"])
            for r in range(2):
                STT(abv[:, 2, r, :], mv[:, 32:40, r], 1.0, VEC[:, 16:24], ALU.add, ALU.mult, ["modc", vecr], ["ab1"])
                CP("dve", abv[:, 3, r, :], mv[:, 24:32, r], ["modc"], ["ab1"])
                TT("dve", gcv[:, 1, r, :], mv[:, 40:48, r], VEC[:, 24:32], ALU.mult, ["modc", vecr], [("gc", l % 2, 1)])
            yield

        def grows(w, l):
            gcv = GCS[l % 2][:].rearrange("p (w r k) -> p w r k", w=2, r=2)
            for r in range(2):
                for k in range(8):
                    dg = DIAGF[:] if k % 4 == 0 else TMP[k % 4 - 1][:, 0:128]
                    dgr = "diagf0" if k % 4 == 0 else ("tmp", k % 4 - 1)
                    TS(dg, IDENTF[:], gcv[:, w, r, k:k + 1], None, ALU.mult, None, ["identf", ("gc", l % 2, w)], [dgr])
                    bank = 1 + (k // 4)
                    MM(psf[bank][:, (k % 4) * 128:(k % 4 + 1) * 128], ONESF[:], dg, True, True,
                       ["const", dgr], [PSF[bank]])
                    if k % 4 == 3:
                        ACT(GB[:, r, (k // 4) * 512:(k // 4 + 1) * 512], psf[bank][:, :], AF.Copy,
                            [PSF[bank]], [("gb", r)])

        def norm_tile(t, xap, xres):
            sc = 4 * t
            ACT(JUNK[:], xap, AF.Square, [xres], [("stat", sc)], accum=STAT[:, sc:sc + 1])
            TS(STAT[:, sc + 1:sc + 2], STAT[:, sc:sc + 1], 1.0 / D, EPS, ALU.mult, ALU.add, [("stat", sc)], [("stat", sc + 1)])
            POW(STAT[:, sc + 2:sc + 3], STAT[:, sc + 1:sc + 2], 1, [("stat", sc + 1)], [("stat", sc + 2)])
            TS(XN[:, t, :], xap, STAT[:, sc + 2:sc + 3], None, ALU.mult, None, [xres, ("stat", sc + 2)], [("xn", t)])

        def transpose_group(tiles, which, dst_res):
            r = 0 if tiles[0] < 16 else 1
            abv = AB[:].rearrange("p (w r k) -> p w r k", w=4, r=2)
            n = len(tiles)
            c0 = tiles[0] * 128
            for k in range(8):
                b = nxt("b", 2)
                for t in range(n):
                    TR(psb[b][:, t * 128:(t + 1) * 128], XN[:, t, k * 128:(k + 1) * 128], [("xn", t)], [PSB[b]])
                ACT(HT[:, k, c0:c0 + n * 128], psb[b][:, 0:n * 128], AF.Identity, [PSB[b], "ab%d" % which], [dst_res],
                    scale=abv[:, 2 * which, r, k:k + 1], bias=abv[:, 2 * which + 1, r, k:k + 1])

        GROUPS = [[0, 1, 2, 3], [4, 5, 6, 7], [8, 9, 10, 11], [12, 13, 14, 15], [16, 17]]

        def premix_gen(l, gsel):
            for g in gsel:
                tiles = GROUPS[g]
                for t, i in enumerate(tiles):
                    s = nxt("x", 3)
                    ap, res = xsrc(l, i)
                    DMA("sp", XT[s][:], ap, [res] if res else [], [("xt", s)], ("xt", s))
                    norm_tile(t, XT[s][:], ("xt", s))
                    yield
                transpose_group(tiles, 0, ("hT", g))
                yield

        def proj_gen(l):
            wv = d_win[l].rearrange("(k p) n -> p k n", p=128)
            coltiles = [(0, 512), (512, 512), (1024, 512), (1536, 512), (2048, 256)]
            blocks = [("qk", 0, 0, 2), ("qk", 2, 2, 2), ("qk", 4, 4, 1),
                      ("conf", 0, 5, 2), ("conf", 1, 7, 2), ("short", 0, 9, 3), ("short", 1, 12, 3), ("v", 0, 15, 1)]

            def load(bi):
                kind, idx, ch0, nch = blocks[bi]
                buf = bi % 2
                DMA("pool", WIN[buf][:, :, 0:nch * 128], wv[:, :, ch0 * 128:(ch0 + nch) * 128], [BIGR], [("win", buf)], ("win", buf))

            def rope_finish(item):
                fmain, tbq, dst, dres, c0, n = item
                f2 = (1 + nxt("f", 5))
                MM(psf[f2][:, 0:n], PERM[:], TMPB[tbq][:, 0:n], True, True, ["perm", ("tmpb", tbq)], [PSF[f2]])
                t1, t2 = nxt("t", 4), nxt("t", 4)
                TT("dve", TMP[t1][:, 0:n], psf[fmain][:, 0:n], ROPE[:, c0:c0 + n], ALU.mult,
                   [PSF[fmain], "rope"], [("tmp", t1)])
                TT("dve", TMP[t2][:, 0:n], psf[f2][:, 0:n], ROPE[:, SEQ + c0:SEQ + c0 + n], ALU.mult,
                   [PSF[f2], "rope"], [("tmp", t2)])
                TT("pool", dst, TMP[t1][:, 0:n], TMP[t2][:, 0:n], ALU.add, [("tmp", t1), ("tmp", t2)], [dres])
            load(0)
            for bi, (kind, idx, ch0, nch) in enumerate(blocks):
                if bi + 1 < len(blocks):
                    load(bi + 1)
                buf = bi % 2
                W = WIN[buf]
                wres = ("win", buf)
                if kind == "v":
                    for i in range(NT):
                        f = (1 + nxt("f", 5))
                        for k in range(8):
                            MM(psf[f][:, 0:128], HT[:, k, i * 128:(i + 1) * 128], W[:, k, 0:128], k == 0, k == 7,
                               [("hT", min(i // 4, 4)), wres], [PSF[f]])
                        CP("dve", VT[:, i, 0:64], psf[f][:, 0:64], [PSF[f]], [("V", i)])
                        ACT(VT[:, i, 192:256], psf[f][:, 64:128], AF.Copy, [PSF[f]], [("V", i)])
                        if i % 3 == 2:
                            yield
                    continue
                if kind == "qk":
                    pend = None
                    for ci in range(nch):
                        chunk = idx + ci
                        isk = chunk == 4
                        for g, (c0, n) in enumerate(coltiles):
                            isctx = g == 4
                            if isctx and l == DEPTH - 1 and not isk:
                                continue
                            f = (1 + nxt("f", 5))
                            for k in range(8):
                                MM(psf[f][:, 0:n], W[:, k, ci * 128:(ci + 1) * 128], HT[:, k, c0:c0 + n], k == 0, k == 7,
                                   [("hT", g), wres], [PSF[f]])
                            dst = KT[:, c0:c0 + n] if isk else QT[:, chunk, c0:c0 + n]
                            dres = ("kT", g) if isk else ("qT", chunk, g)
                            if isctx:
                                ACT(dst, psf[f][:, 0:n], AF.Copy, [PSF[f]], [dres])
                            else:
                                tbq = nxt("tb", 3)
                                ACT(TMPB[tbq][:, 0:n], psf[f][:, 0:n], AF.Copy, [PSF[f]], [("tmpb", tbq)])
                                if pend is not None:
                                    rope_finish(pend)
                                pend = (f, tbq, dst, dres, c0, n)
                            yield
                    if pend is not None:
                        rope_finish(pend)
                    continue
                for g, (c0, n) in enumerate(coltiles):
                    isctx = g == 4
                    if isctx and l == DEPTH - 1:
                        continue
                    banks = []
                    for ci in range(nch):
                        f = (1 + nxt("f", 5))
                        banks.append(f)
                        for k in range(8):
                            MM(psf[f][:, 0:n], W[:, k, ci * 128:(ci + 1) * 128], HT[:, k, c0:c0 + n], k == 0, k == 7,
                               [("hT", g), wres], [PSF[f]])
                    if kind == "conf":
                        t1 = nxt("t", 4)
                        ACT(TMP[t1][:, 0:n], psf[banks[1]][:, 0:n], AF.Sigmoid, [PSF[banks[1]]], [("tmp", t1)])
                        off = 15 + c0 if not isctx else 2093
                        TT("dve", UC[:, idx, off:off + n], psf[banks[0]][:, 0:n], TMP[t1][:, 0:n], ALU.mult,
                           [PSF[banks[0]], ("tmp", t1)], [("uc", g)])
                    elif kind == "short":
                        t1 = nxt("t", 4)
                        ACT(SBS[:, idx, c0:c0 + n], psf[banks[0]][:, 0:n], AF.Copy, [PSF[banks[0]]], [("sbs", g)])
                        ACT(TMP[t1][:, 0:n], psf[banks[2]][:, 0:n], AF.Copy, [PSF[banks[2]]], [("tmp", t1)])
                        off = 1 + c0 if not isctx else 2051
                        TT("dve", PP[:, idx, off:off + n], psf[banks[1]][:, 0:n], TMP[t1][:, 0:n], ALU.mult,
                           [PSF[banks[1]], ("tmp", t1)], [("pp", g)])
                    yield

        def stat_rstd(psS, n, N, rd, tmpi):
            ACT(TMP[tmpi][:, 0:N], psS[:, 0:N], AF.Sqrt, list(rd) + ["const"], [("tmp", tmpi)], scale=1.0 / n, bias=EPSC[:, 0:1])
            RECIP(TMP[tmpi][:, 0:N], TMP[tmpi][:, 0:N], [("tmp", tmpi)], [("tmp", tmpi)])

        def tgeom(T):
            isctx = T == 4
            N = 256 if isctx else 512
            q0 = 2048 if isctx else T * 512
            return isctx, N, q0

        def gnorm_gen(T, ch0, nch, bank, tmps):
            isctx, N, q0 = tgeom(T)
            zb = T % 2
            Z = ZT[zb]
            zr = lambda ch: ("zT", zb, ch)
            for ci in range(nch):
                tb = nxt("tb", 3)
                ACT(TMPB[tb][:, 0:N], Z[:, ch0 + ci, 0:N], AF.Square, [zr(ch0 + ci)], [("tmpb", tb)])
                MM(psf[bank][:, 0:N], ONESB[:, 0:128], TMPB[tb][:, 0:N], ci == 0, ci == nch - 1, ["const", ("tmpb", tb)], [PSF[bank]])
                yield
            tm_, tmr = tmps
            ACT(tm_[:, 0:N], psf[bank][:, 0:N], AF.Ln, [PSF[bank], "const"], [tmr], scale=1.0 / (nch * 128), bias=EPSC[:, 0:1])
            ACT(tm_[:, 0:N], tm_[:, 0:N], AF.Exp, [tmr], [tmr], scale=-0.5)
            yield
            for ci in range(nch):
                STT(Z[:, ch0 + ci, 0:N], Z[:, ch0 + ci, 0:N], L.VEC[:, 32 + ch0 + ci:33 + ch0 + ci], tm_[:, 0:N],
                    ALU.mult, ALU.mult, [zr(ch0 + ci), tmr, L.vecr], [zr(ch0 + ci)])
            yield

        def conf_short_gen(l, T):
            isctx, N, q0 = tgeom(T)
            zb = T % 2
            Z = ZT[zb]
            zr = lambda ch: ("zT", zb, ch)
            uoff = 2093 if isctx else 15 + q0
            cvt = [0, 1]
            tm = 2
            ucr = [("uc", T), ("uc", max(T - 1, 0)), ("uc", min(T + 1, 3) if not isctx else 4)]
            for cc in range(2):
                for k in range(31):
                    MM(psf[6 + cc][:, 0:N], DIAG[:, cc * 31 + k, :], UC[:, cc, uoff - 15 + k:uoff - 15 + k + N], k == 0, k == 30,
                       ucr + [("diag", cc * 31 + k)], [PSF[6 + cc]])
                    if k % 8 == 7:
                        yield
                ACT(TMP[cvt[cc]][:, 0:N], psf[6 + cc][:, 0:N], AF.Identity, [PSF[6 + cc], L.vecr], [("tmp", cvt[cc])], bias=L.VEC[:, 40 + cc:41 + cc])
                yield
            hb = [nxt("tb", 3), nxt("tb", 3)]
            for cc in range(2):
                CP("dve", TMPB[hb[cc]][:, 0:N], TMP[cvt[cc]][:, 0:N], [("tmp", cvt[cc])], [("tmpb", hb[cc])])
            yield
            for cc in range(2):
                MM(psf[6][:, 0:N], ONESB[:, 0:128], TMPB[hb[cc]][:, 0:N], cc == 0, cc == 1, ["const", ("tmpb", hb[cc])], [PSF[6]])
            yield
            TS(TMP[tm][:, 0:N], psf[6][:, 0:N], 1.0 / 256, None, ALU.mult, None, [PSF[6]], [("tmp", tm)])
            yield
            for cc in range(2):
                TT("dve", TMP[cvt[cc]][:, 0:N], TMP[cvt[cc]][:, 0:N], TMP[tm][:, 0:N], ALU.subtract,
                   [("tmp", cvt[cc]), ("tmp", tm)], [("tmp", cvt[cc])])
                ACT(TMPB[hb[cc]][:, 0:N], TMP[cvt[cc]][:, 0:N], AF.Square, [("tmp", cvt[cc])], [("tmpb", hb[cc])])
                yield
            for cc in range(2):
                MM(psf[7][:, 0:N], ONESB[:, 0:128], TMPB[hb[cc]][:, 0:N], cc == 0, cc == 1, ["const", ("tmpb", hb[cc])], [PSF[7]])
            yield
            ACT(TMP[tm][:, 0:N], psf[7][:, 0:N], AF.Ln, [PSF[7], "const"], [("tmp", tm)], scale=1.0 / 256, bias=EPSC[:, 0:1])
            ACT(TMP[tm][:, 0:N], TMP[tm][:, 0:N], AF.Exp, [("tmp", tm)], [("tmp", tm)], scale=-0.5)
            yield
            for cc in range(2):
                TT("dve", TMP[cvt[cc]][:, 0:N], TMP[cvt[cc]][:, 0:N], TMP[tm][:, 0:N], ALU.mult,
                   [("tmp", cvt[cc]), ("tmp", tm)], [("tmp", cvt[cc])])
            yield
            for cc in range(2):
                ACT(TMP[tm][:, 0:N], TMP[cvt[cc]][:, 0:N], AF.Exp, [("tmp", cvt[cc]), "nln"], [("tmp", tm)],
                    scale=NLN[:, cc:cc + 1], bias=NLN[:, 2 + cc:3 + cc])
                ACT(TMP[tm][:, 0:N], TMP[tm][:, 0:N], AF.Ln, [("tmp", tm), "const"], [("tmp", tm)], bias=ONESF[:, 0:1])
                ACT(TMP[tm][:, 0:N], TMP[tm][:, 0:N], AF.Exp, [("tmp", tm)], [("tmp", tm)], scale=-1.0)
                TS(TMP[cvt[cc]][:, 0:N], TMP[cvt[cc]][:, 0:N], L.VEC[:, 42 + cc:43 + cc], L.VEC[:, 44 + cc:45 + cc], ALU.mult, ALU.add,
                   [("tmp", cvt[cc]), L.vecr], [("tmp", cvt[cc])])
                TT("dve", Z[:, 4 + cc, 0:N], TMP[cvt[cc]][:, 0:N], TMP[tm][:, 0:N], ALU.mult,
                   [("tmp", cvt[cc]), ("tmp", tm)], [zr(4 + cc)])
                yield
            yield from gnorm_gen(T, 4, 2, 6, (TMP[tm], ("tmp", tm)))
            poff = 2051 if isctx else 1 + q0
            for cc in range(2):
                t1 = cc
                prd = [("pp", T), ("pp", max(T - 1, 0)), ("pp", min(T + 1, 3) if not isctx else 4), L.vecr]
                TS(TMP[t1][:, 0:N], PP[:, cc, poff - 1:poff - 1 + N], L.VEC[:, 46 + cc * 3:47 + cc * 3], None, ALU.mult, None,
                   prd, [("tmp", t1)])
                for k in (1, 2):
                    STT(TMP[t1][:, 0:N], PP[:, cc, poff - 1 + k:poff - 1 + k + N], L.VEC[:, 46 + cc * 3 + k:47 + cc * 3 + k],
                        TMP[t1][:, 0:N], ALU.mult, ALU.add, prd + [("tmp", t1)], [("tmp", t1)])
                yield
                TT("dve", Z[:, 6 + cc, 0:N], SBS[:, cc, (2048 if isctx else q0):(2048 if isctx else q0) + N], TMP[t1][:, 0:N], ALU.mult,
                   [("sbs", T), ("tmp", t1)], [zr(6 + cc)])
                yield
            yield from gnorm_gen(T, 6, 2, 7, (TMP[tm], ("tmp", tm)))

        def attn_gen(l, T):
            isctx, N, q0 = tgeom(T)
            zb = T % 2
            Z = ZT[zb]
            zr = lambda ch: ("zT", zb, ch)
            keys = [(16, 0, N, None), (17, 0, N, None)]
            if not isctx:
                for j in range(max(0, 4 * T - 1), min(15, 4 * T + 4) + 1):
                    lo = max(4 * T, j - 1)
                    hi = min(4 * T + 3, j + 1)
                    qoff = (lo - 4 * T) * 128
                    n = (hi - lo + 1) * 128
                    moff = (lo - (j - 1)) * 128
                    needm = (lo == j - 1) or (hi == j + 1)
                    keys.append((j, qoff, n, moff if needm else None))
            nk = len(keys)
            for c in range(4):
                for ki in range(nk + 1):
                    if ki < nk:
                        j, qoff, n, moff = keys[ki]
                        kg = min(j // 4, 4)
                        for half in range(2):
                            pr = slice(64 * half, 64 * half + 64)
                            f = half
                            MM(psf[f][:, 0:n], KT[pr, j * 128:(j + 1) * 128], QT[pr, c, q0 + qoff:q0 + qoff + n], True, moff is None,
                               [("kT", kg), ("qT", c, T)], [PSF[f]])
                        for half in range(2):
                            f = half
                            pt = 2 * half + (ki % 2)
                            if moff is not None:
                                msegs = []
                                if moff == 0:
                                    msegs.append((0, 0))
                                if moff + n == 384:
                                    msegs.append((n - 128, 256))
                                for si, (pc, mc) in enumerate(msegs):
                                    MM(psf[f][:, pc:pc + 128], IDENT[:], MASK[:, mc:mc + 128], False, si == len(msegs) - 1,
                                       ["ident", "const"], [PSF[f]])
                            ACT(PT[pt][:, 0:n], psf[f][:, 0:n], AF.Exp, [PSF[f]], [("pt", pt)], scale=0.125)
                    if ki >= 1:
                        pj, pqoff, pn, _ = keys[ki - 1]
                        for half in range(2):
                            pp_ = 2 * half + ((ki - 1) % 2)
                            MM(psf[4 + half][:, pqoff:pqoff + pn], VT[:, pj, half * 128:(half + 1) * 128], PT[pp_][:, 0:pn], ki == 1, ki == nk,
                               [("V", pj), "vones", ("pt", pp_)], [PSF[4 + half]])
                    yield
                for half in range(2):
                    h = c + 4 * half
                    pr = slice(64 * half, 64 * half + 64)
                    dn = slice(64 * (1 - half), 64 * (1 - half) + 64)
                    psO = psf[4 + half]
                    at = ATMP[half]
                    ACT(at[pr, 0:N], psO[dn, 0:N], AF.Ln, [PSF[4 + half], "esc"], [("atmp", half)], bias=ESC[dn, h:h + 1])
                    ACT(at[pr, 0:N], at[pr, 0:N], AF.Exp, [("atmp", half)], [("atmp", half)], scale=-1.0)
                    TT("dve", Z[pr, c, 0:N], psO[pr, 0:N], at[pr, 0:N], ALU.mult, [PSF[4 + half], ("atmp", half)], [zr(c)])
                yield

        def tail_gen(l, T):
            isctx, N, q0 = tgeom(T)
            zb = T % 2
            Z = ZT[zb]
            zr = lambda ch: ("zT", zb, ch)
            yield from gnorm_gen(T, 0, 4, 6, (TMP[3], ("tmp", 3)))
            ntl = N // 128
            tiles = [q0 // 128 + t for t in range(ntl)]
            for t, i in enumerate(tiles):
                for hf in range(2):
                    for k in range(8):
                        MM(psf[2 + hf][:, :], Z[:, k, t * 128:(t + 1) * 128], WOUT[:, k, hf * 512:(hf + 1) * 512], k == 0, k == 7,
                           [zr(k), "wout"], [PSF[2 + hf]])
                    yield
                s = nxt("x", 3)
                postnorm_tile(l, i, s, 0, yb=2)
                yield
                norm_tile(t, XT[s][:], ("xt", s))
                yield
            r = 0 if tiles[0] < 16 else 1
            abv = AB[:].rearrange("p (w r k) -> p w r k", w=4, r=2)
            n = len(tiles)
            c0 = tiles[0] * 128
            for k in range(8):
                b = nxt("b", 2)
                for t in range(n):
                    TR(psb[b][:, t * 128:(t + 1) * 128], XN[:, t, k * 128:(k + 1) * 128], [("xn", t)], [PSB[b]])
                ACT(HT[:, k, c0:c0 + n * 128], psb[b][:, 0:n * 128], AF.Identity, [PSB[b], "ab1"], [("hT", T)],
                    scale=abv[:, 2, r, k:k + 1], bias=abv[:, 3, r, k:k + 1])
                yield

        def chain(*gens):
            for g in gens:
                if g is not None:
                    yield from g

        def run_merged(gens):
            gens = [g for g in gens if g is not None]
            while gens:
                for g in list(gens):
                    try:
                        next(g)
                    except StopIteration:
                        gens.remove(g)

        def run_balanced(gens):
            st_ = [[g, 0, float(n)] for (g, n) in gens if g is not None]
            while st_:
                st_.sort(key=lambda x: x[1] / x[2])
                g = st_[0]
                try:
                    next(g[0])
                    g[1] += 1
                except StopIteration:
                    st_.remove(g)

        def mixer_all(l, nT, kstop=99):
            run_merged([conf_short_gen(l, 0)])
            run_merged([attn_gen(l, 0), conf_short_gen(l, 1)])
            for T in range(nT):
                if T == nT - 1:
                    return tail_gen(l, T)
                A = attn_gen(l, T + 1) if T + 1 < nT else None
                hasC = T + 2 < nT
                B = chain(tail_gen(l, T), conf_short_gen(l, T + 2) if hasC else None)
                run_balanced([(A, 40 if T + 1 < 4 else 12), (B, 60 if hasC else 30)])

        def postnorm_tile(l, i, s, w, final=False, yb=4):
            r = 0 if i < 16 else 1
            sc = 16 + 4 * (i % 4)
            if w == 0:
                ap, res = xsrc(l, i)
            else:
                ap, res = d_xs[i * 128:(i + 1) * 128, :], ("dxs", i)
            DMA("sp", XT[s][:], ap, [res] if res else [], [("xt", s)], ("xt", s))
            for hf in range(2):
                ACT(JUNK[:, 0:512], psf[yb + hf][:, :], AF.Square, [PSF[yb + hf]], [("stat", sc + hf)],
                    accum=STAT[:, sc + hf:sc + hf + 1])
            TT("dve", STAT[:, sc + 2:sc + 3], STAT[:, sc:sc + 1], STAT[:, sc + 1:sc + 2], ALU.add,
               [("stat", sc), ("stat", sc + 1)], [("stat", sc + 2)])
            TS(STAT[:, sc + 2:sc + 3], STAT[:, sc + 2:sc + 3], 1.0 / D, EPS, ALU.mult, ALU.add, [("stat", sc + 2)], [("stat", sc + 2)])
            POW(STAT[:, sc + 3:sc + 4], STAT[:, sc + 2:sc + 3], 1, [("stat", sc + 2)], [("stat", sc + 3)])
            for hf in range(2):
                t1 = nxt("t", 4)
                STT(TMP[t1][:, :], psf[yb + hf][:, :], STAT[:, sc + 3:sc + 4], GB[:, r, hf * 512:(hf + 1) * 512],
                    ALU.mult, ALU.mult, [PSF[yb + hf], ("stat", sc + 3), ("gb", r)], [("tmp", t1)])
                TT("pool", XT[s][:, hf * 512:(hf + 1) * 512], XT[s][:, hf * 512:(hf + 1) * 512], TMP[t1][:, :], ALU.add,
                   [("xt", s), ("tmp", t1)], [("xt", s)])
            if final:
                if i < 16:
                    op = DMA("sp", d_out[i * 128:(i + 1) * 128, :], XT[s][:], [("xt", s)], [("dout", i)], ("xt", s))
                    out_dmas.append(op)
            else:
                DMA("sp", d_xs[i * 128:(i + 1) * 128, :], XT[s][:], [("xt", s)], [("dxs", i)], ("xt", s))

        def mixer_setup(l):
            TS(NLN[:, 0:4], L.VEC[:, 42:46], -1.0, None, ALU.mult, None, [L.vecr], ["nln"])
            DMA("pool", ROPE, d_rope, [BIGR], ["rope"], "rope")
            MEMSET("pool", UC[:, :, :], 0.0, [BIGR, ("uc", 0), ("uc", 1), ("uc", 2), ("uc", 3), ("uc", 4)])
            MEMSET("pool", PP[:, :, :], 0.0, [BIGR, ("pp", 0), ("pp", 1), ("pp", 2), ("pp", 3), ("pp", 4)])
            MEMSET("pool", VT[:, :, 64:192], 1.0, ["vones"])
            for cc in range(2):
                for k in range(31):
                    TS(DIAG[:, cc * 31 + k, :], IDENT[:], L.VEC[:, 52 + cc * 31 + k:53 + cc * 31 + k], None, ALU.mult, None,
                       ["ident", L.vecr], [("diag", cc * 31 + k)])

        def load_wout(l):
            wv = d_wout[l].rearrange("(k p) n -> p k n", p=128)
            DMA("pool", WOUT[:, :, :], wv, [("win", 0), ("win", 1)], ["wout", ("win", 0), ("win", 1)], "wout")

        def ffn_geom(l, hi_):
            last = l == DEPTH - 1
            ta, tb_ = [(0, 9), (9, 16 if last else 18)][hi_]
            tok_lo, tok_hi = ta * 128, min(tb_, 16) * 128
            segs = []
            ulo = max(tok_lo - 1, 0)
            uhi = min(tok_hi + 1, SEQ)
            for (c0, n) in split_cols(ulo, uhi):
                segs.append((c0, n, c0 - tok_lo + 1))
            nx = tok_hi - tok_lo
            ctxcol = None
            if tb_ > 16:
                ctxcol = nx + 3
                segs.append((2048, 256, ctxcol))
            rowlen = (ctxcol + 256 + 1) if ctxcol is not None else nx + 2
            return ta, tb_, segs, nx, ctxcol, rowlen

        FF_EARLY = [("diag", i) for i in range(62)] + ["vones", "rope", ("atmp", 0), ("atmp", 1)] + [("pt", i) for i in range(4)] + \
                   [("uc", g) for g in range(5)] + [("pp", g) for g in range(5)] + \
                   [("sbs", g) for g in range(5)] + [("kT", g) for g in range(5)] + [("V", i) for i in range(NT)] + \
                   [("qT", c, g) for c in range(4) for g in range(5)]
        FF_LATE = [BIGR, "wout", ("win", 0), ("win", 1)] + [("zT", i, ch) for i in range(2) for ch in range(8)]

        def ffn_fence_a():
            S.add("pool", lambda e: e.memset(STAT[:, 60:61], 0.0), FF_EARLY, ["ffnfence"] + FF_EARLY)

        def ffn_fence_b():
            S.add("pool", lambda e: e.memset(STAT[:, 62:63], 0.0), FF_LATE + ["ffnfence"], ["ffnfenceB"] + FF_LATE)

        def ffn_up_gen(l, hi_, banks=(0, 1, 2, 3), order=None):
            wuv = d_wup[l].rearrange("(k p) n -> p k n", p=128)
            ta, tb_, segs, nx, ctxcol, rowlen = ffn_geom(l, hi_)
            for i in range(4):
                S.add("pool", lambda e, a=UROW[i]: e.memset(a[:, :], 0.0), ["ffnfence"], [("urow", i)])

            order = list(range(NFF)) if order is None else list(order)

            def load_wup(p):
                b_ = p % 3
                jj_ = order[p]
                DMA("pool", WUP[b_][:, :, :], wuv[:, :, jj_ * 256:(jj_ + 1) * 256], ["ffnfence"], [("wup", b_)], ("wup", b_))
            load_wup(0)
            load_wup(1)
            for p, j in enumerate(order):
                buf = p % 3
                if p + 2 < NFF:
                    load_wup(p + 2)
                if hi_ == 0 and p in (1, 3):
                    wdv = d_wdn[l].rearrange("(k p) n -> p k n", p=128)
                    h0 = 0 if p == 1 else 11
                    DMA("pool", WDN[:, h0:h0 + 11, :], wdv[:, h0:h0 + 11, :], ["ffnfence"], ["wdn"], "wdn%d" % (p // 2))
                ub = (p % 2) * 2
                L_ = rowlen - 2
                for gv in range(2):
                    ur = ub + gv
                    for (c0, n, col) in segs:
                        f = banks[nxt("f", 4)]
                        gs = sorted(set([min(c0 // 512, 4), min((c0 + n - 1) // 512, 4)]))
                        for k in range(8):
                            MM(psf[f][:, 0:n], WUP[buf][:, k, gv * 128:(gv + 1) * 128], HT[:, k, c0:c0 + n], k == 0, k == 7,
                               [("hT", g) for g in gs] + [("wup", buf)], [PSF[f]])
                        ACT(UROW[ur][:, col:col + n], psf[f][:, 0:n], AF.Copy, [PSF[f]], [("urow", ur)])
                    wc = 114 + (gv * NFF + j) * 3
                    TS(CROW[ur][:, 1:1 + L_], UROW[ur][:, 0:L_], L.VEC[:, wc:wc + 1], None, ALU.mult, None,
                       [("urow", ur), L.vecr], [("crow", ur)])
                    for k in (1, 2):
                        STT(CROW[ur][:, 1:1 + L_], UROW[ur][:, k:k + L_], L.VEC[:, wc + k:wc + k + 1], CROW[ur][:, 1:1 + L_],
                            ALU.mult, ALU.add, [("urow", ur), ("crow", ur), L.vecr], [("crow", ur)])
                    yield
                ACT(CROW[ub][:, 1:1 + L_], CROW[ub][:, 1:1 + L_], AF.Silu, [("crow", ub)], [("crow", ub)])
                TT("pool", GT[:, j, 0:nx], CROW[ub][:, 1:1 + nx], CROW[ub + 1][:, 1:1 + nx], ALU.mult,
                   [("crow", ub), ("crow", ub + 1)] + (["ffnfenceB"] if j < 15 else []), [("gT", j)])
                if ctxcol is not None:
                    TT("pool", GT[:, j, nx:nx + 256], CROW[ub][:, ctxcol:ctxcol + 256], CROW[ub + 1][:, ctxcol:ctxcol + 256], ALU.mult,
                       [("crow", ub), ("crow", ub + 1)] + (["ffnfenceB"] if j < 15 else []), [("gT", j)])
                yield

        def ffn_down(l, hi_):
            last = l == DEPTH - 1
            ta, tb_, segs, nx, ctxcol, rowlen = ffn_geom(l, hi_)
            for t, i in enumerate(range(ta, tb_)):
                yb = 4 if t % 2 == 0 else 2
                for hf in range(2):
                    for k in range(NFF):
                        MM(psf[yb + hf][:, :], GT[:, k, t * 128:(t + 1) * 128], WDN[:, k, hf * 512:(hf + 1) * 512], k == 0, k == NFF - 1,
                           [("gT", k), "wdn"], [PSF[yb + hf]])
                s = nxt("x", 3)
                postnorm_tile(l, i, s, 1, final=last, yb=yb)
                g = i // 4
                if not last and g != 2:
                    norm_tile(i % 4, XT[s][:], ("xt", s))
                    if i == GROUPS[g][-1]:
                        transpose_group(GROUPS[g], 0, ("hT", g))

        def ffn_end(l):
            ffnres = ["wdn"] + [("wup", i) for i in range(3)] + [("urow", i) for i in range(4)] + \
                     [("crow", i) for i in range(4)] + [("gT", j) for j in range(NFF)]
            S.add("pool", lambda e: e.memset(STAT[:, 61:62], 0.0), ffnres + [BIGR], ffnres + [BIGR])

        kstop = int(os.environ.get("KSTOP", "99"))
        XNW = [XN[:, 0:2, :].rearrange("p a b -> p (a b)").rearrange("p (k n) -> p k n", n=256),
               XN[:, 2:4, :].rearrange("p a b -> p (a b)").rearrange("p (k n) -> p k n", n=256)]
        XNWR = [[("xn", 0), ("xn", 1)], [("xn", 2), ("xn", 3)]]
        for l in range(DEPTH):
            S.epoch = l
            L.VEC = VECS[l % 2]
            L.vecr = ("vec", l % 2)
            if l == 0:
                mg = mod_gen(0, [ZT[0], ZT[1]], [[("zT", 0, ch) for ch in range(8)], [("zT", 1, ch) for ch in range(8)]], 0, 512)
                import itertools
                for _ in range(5):
                    next(mg)
                run_merged([itertools.islice(mg, 3), premix_gen(0, [0, 1, 2, 3, 4])])

            else:
                run_merged([premix_gen(l, [2])])
            if kstop <= 0:
                break
            grows(0, l)
            if kstop <= 1:
                break
            mixer_setup(l)
            if l == 0:
                run_balanced([(proj_gen(l), 70), (mg, 7)])
            else:
                run_merged([proj_gen(l)])
            load_wout(l)
            if kstop <= 2:
                break
            last_tail = mixer_all(l, 5 if l < DEPTH - 1 else 4, kstop)
            if kstop <= 4:
                run_merged([last_tail])
                break
            ffn_fence_a()
            import itertools
            up0 = ffn_up_gen(l, 0, banks=(0, 1, 4, 5), order=list(range(15, NFF)) + list(range(15)))
            run_balanced([(last_tail, 30), (itertools.islice(up0, 21), 21)])
            ffn_fence_b()
            run_balanced([(mod_gen(l + 1, XNW, XNWR, 6, 256) if l + 1 < DEPTH else None, 26), (up0, 45)])
            grows(1, l)
            up1 = ffn_up_gen(l, 1)
            next(up1)
            ffn_down(l, 0)
            run_merged([up1])
            ffn_down(l, 1)
            ffn_end(l)
            if kstop <= 5:
                break
        S.emit(final_waits=out_dmas)
    return nc


def rope_tables():
    t = np.arange(SEQ)
    pos = np.stack([t // 64, t % 64], 0).astype(np.float32)
    inv = (10000.0 ** (-np.arange(16, dtype=np.float32) / 16)).astype(np.float32)
    cos = np.zeros((128, SEQ), np.float32)
    sin = np.zeros((128, SEQ), np.float32)
    for p in range(128):
        d = p % 64
        r, idx = d // 32, d % 32
        ang = pos[r] * inv[idx % 16]
        cos[p] = np.cos(ang)
        sin[p] = np.sin(ang) * (-1.0 if idx < 16 else 1.0)
    return np.concatenate([cos, sin], 1)


def col_layout(v):
    return np.ascontiguousarray(v.reshape(-1, 128).T)


def rot_partner(d):
    r, idx = d // 32, d % 32
    return r * 32 + (idx + 16 if idx < 16 else idx - 16)


def win_perm():
    out = []
    for c in range(4):
        out.append(np.concatenate([np.arange(c * 64, c * 64 + 64), np.arange((c + 4) * 64, (c + 4) * 64 + 64)]))
    out.append(512 + np.arange(128))
    cv0, cg0, sb0, scg0, su0 = 768, 1024, 1280, 1536, 1792
    for cc in range(2):
        out += [cv0 + cc * 128 + np.arange(128), cg0 + cc * 128 + np.arange(128)]
    for cc in range(2):
        out += [sb0 + cc * 128 + np.arange(128), scg0 + cc * 128 + np.arange(128), su0 + cc * 128 + np.arange(128)]
    out += [640 + np.arange(128)]
    return np.concatenate(out)


def perm_matrix():
    pm = np.zeros((128, 128), np.float32)
    for m in range(128):
        k = (m // 64) * 64 + rot_partner(m % 64)
        pm[k, m] = 1.0
    return pm


def attn_chan_perm():
    idx = []
    for c in range(4):
        idx += list(range(c * 64, c * 64 + 64)) + list(range((c + 4) * 64, (c + 4) * 64 + 64))
    idx += list(range(512, 1024))
    return np.array(idx)


_CACHE = {}


def prepare_inputs(inputs):
    f = lambda a: np.ascontiguousarray(np.asarray(a, dtype=np.float32))
    x, c, ctx, c_ctx = f(inputs["x"]), f(inputs["c"]), f(inputs["ctx"]), f(inputs["c_ctx"])
    rope = rope_tables()
    kk = np.arange(128)[:, None]
    qq = np.arange(128)[None, :]
    mask = np.zeros((128, 384), np.float32)
    mask[:, 0:128] = np.where(kk <= qq, 0.0, -30000.0)
    mask[:, 256:384] = np.where(qq <= kk, 0.0, -30000.0)
    perm = win_perm()
    zperm = attn_chan_perm()
    shared = {"rope": rope, "mask": mask, "perm": perm_matrix()}
    for l in range(DEPTH):
        shared["wada%d" % l] = f(inputs["w_ada"][l])
        vec = np.zeros((128, NV), np.float32)
        vec[:, 0:8] = col_layout(f(inputs["g_pre_mix"][l]))
        vec[:, 8:16] = col_layout(f(inputs["g_post_mix"][l]))
        vec[:, 16:24] = col_layout(f(inputs["g_pre_ffn"][l]))
        vec[:, 24:32] = col_layout(f(inputs["g_post_ffn"][l]))
        vec[:, 32:40] = col_layout(f(inputs["g_group"][l])[zperm])
        vec[:, 40:42] = col_layout(f(inputs["b_conf_dw"][l]))
        vec[:, 42:44] = col_layout(f(inputs["conf_ln_g"][l]))
        vec[:, 44:46] = col_layout(f(inputs["conf_ln_b"][l]))
        wsc = f(inputs["w_sc_dw"][l])
        wcf = f(inputs["w_conf_dw"][l])
        wff = f(inputs["w_ffn_dw"][l])
        for cc in range(2):
            vec[:, 46 + cc * 3:49 + cc * 3] = wsc[:, cc * 128:(cc + 1) * 128].T
            vec[:, 52 + cc * 31:83 + cc * 31] = wcf[:, cc * 128:(cc + 1) * 128].T
        for ch in range(44):
            vec[:, 114 + ch * 3:117 + ch * 3] = wff[:, ch * 128:(ch + 1) * 128].T
        ba = col_layout(f(inputs["b_ada"][l]))
        vec[:, 246:342] = np.repeat(ba, 2, axis=1)
        shared["vec%d" % l] = vec
        shared["win%d" % l] = np.ascontiguousarray(f(inputs["w_in"][l])[:, perm])
        shared["sink%d" % l] = np.ascontiguousarray(np.broadcast_to(f(inputs["sink"][l])[None, :], (128, 8)))
        shared["wout%d" % l] = np.ascontiguousarray(f(inputs["w_out"][l])[zperm, :])
        wu = f(inputs["w_up"][l])
        shared["wup%d" % l] = np.ascontiguousarray(
            np.stack([wu[:, :DFF].reshape(D, NFF, 128), wu[:, DFF:].reshape(D, NFF, 128)], axis=2).reshape(D, 2 * DFF))
        shared["wdn%d" % l] = f(inputs["w_down"][l])
    in_maps = []
    for b in range(8):
        m = dict(shared)
        m["x"] = x[b]
        m["ctx"] = ctx[b]
        cc = np.zeros((128, 16), np.float32)
        cc[:, 0::2] = col_layout(c[b])
        cc[:, 1::2] = col_layout(c_ctx)
        m["cc"] = cc
        in_maps.append(m)
    return in_maps


def kernel(**inputs):
    in_maps = prepare_inputs(inputs)
    if "nc" not in _CACHE:
        _CACHE["nc"] = build_program()
    nc = _CACHE["nc"]
    res = run_bass_kernel_spmd(nc, in_maps, core_ids=list(range(8)))
    out = np.stack([np.asarray(r["out"], dtype=np.float32) for r in res.results], 0)
    return out
```
